# Optimizing a Trainium2 kernel written in Bass

```python
import jax, jax.numpy as jnp
from jax import lax
import numpy as np

D_MODEL = 1024
BATCH = 32
SEQ = 256
DEPTH = 2
DEC_BATCH = 8
DEC_SEQ = 1024
PAST_LEN = 256

GRID_W = 64
CHUNK = 128
A_GROUPS = 4
A_WIDTH = D_MODEL // 2
A_GROUP_DIM = A_WIDTH // A_GROUPS
B_GROUPS = 4
B_WIDTH = D_MODEL // 2
B_GROUP_DIM = B_WIDTH // B_GROUPS
HEAD_DIM = 128
N_HEADS = D_MODEL // HEAD_DIM
N_KV_HEADS = 2
KV_GROUP = N_HEADS // N_KV_HEADS
WINDOW = 128
Q_BLOCK = 128
BAND = Q_BLOCK + 2 * WINDOW
AXIS_DIM = HEAD_DIM // 2
ROPE_BASE = 10000.0
D_FF = ((8 * D_MODEL + 3 * 256 - 1) // (3 * 256)) * 256
N_EVEN = (DEPTH + 1) // 2
N_ODD = DEPTH // 2
EPS = 1e-6
NEG = -1e30

kernel_name = "hybrid_dit_gmlp_fnet_swa_step"


def rms_norm(x, g):
    xf = x.astype(jnp.float32)
    y = xf * lax.rsqrt(jnp.mean(xf * xf, axis=-1, keepdims=True) + EPS)
    return (y * g.astype(jnp.float32)).astype(x.dtype)


def layer_norm(x, g):
    xf = x.astype(jnp.float32)
    xc = xf - jnp.mean(xf, axis=-1, keepdims=True)
    y = xc * lax.rsqrt(jnp.mean(xc * xc, axis=-1, keepdims=True) + EPS)
    return (y * g.astype(jnp.float32)).astype(x.dtype)


def modulation(cond, w, b):
    m = jax.nn.silu(cond) @ w + b
    return [p[:, None, :] for p in jnp.split(m, 6, axis=-1)]


def modulate(h, shift, scale):
    return h * (1 + scale) + shift


def chunk_gmlp(u, v, sgu_w, sgu_b, sgu_g):
    bn, t, _ = v.shape
    v = layer_norm(v, sgu_g)
    vc = v.reshape(bn, t // CHUNK, CHUNK, A_GROUPS, A_GROUP_DIM)
    mixed = jnp.einsum('hpq,bnqhc->bnphc', sgu_w, vc) + sgu_b.T[None, None, :, :, None]
    return u * mixed.reshape(bn, t, A_WIDTH)


def fourier_mix(z):
    bn, t, _ = z.shape
    zg = z.reshape(bn, t, B_GROUPS, B_GROUP_DIM).astype(jnp.float32)
    f = jnp.fft.fft2(zg, axes=(1, 3), norm='ortho')
    return jnp.real(f).reshape(bn, t, B_WIDTH).astype(z.dtype)


def mixer_ab(h, w_in, sgu_w, sgu_b, sgu_g, w_out):
    z = h @ w_in
    u, v, zb = jnp.split(z, [A_WIDTH, 2 * A_WIDTH], axis=-1)
    a = chunk_gmlp(jax.nn.gelu(u), jax.nn.gelu(v), sgu_w, sgu_b, sgu_g)
    b = fourier_mix(zb)
    return jnp.concatenate([a, b], axis=-1) @ w_out


def qkv_proj(h, w_qkv):
    bn, t, _ = h.shape
    z = h @ w_qkv
    q, k, v = jnp.split(z, [N_HEADS * HEAD_DIM, (N_HEADS + N_KV_HEADS) * HEAD_DIM], axis=-1)
    return (q.reshape(bn, t, N_KV_HEADS, KV_GROUP, HEAD_DIM),
            k.reshape(bn, t, N_KV_HEADS, HEAD_DIM),
            v.reshape(bn, t, N_KV_HEADS, HEAD_DIM))


def axial_rope_tables(t):
    rows_n = t // GRID_W
    rows = jnp.repeat(jnp.arange(rows_n), GRID_W).astype(jnp.float32)
    cols = jnp.tile(jnp.arange(GRID_W), rows_n).astype(jnp.float32)
    inv = ROPE_BASE ** (-jnp.arange(0, AXIS_DIM, 2, dtype=jnp.float32) / AXIS_DIM)
    ar = rows[:, None] * inv
    ac = cols[:, None] * inv
    ang = jnp.concatenate([ar, ar, ac, ac], axis=-1)
    return jnp.cos(ang), jnp.sin(ang)


def apply_rope(x, cos, sin):
    xf = x.astype(jnp.float32)
    xs = xf.reshape(xf.shape[:-1] + (2, 2, AXIS_DIM // 2))
    rot = jnp.stack([-xs[..., 1, :], xs[..., 0, :]], axis=-2).reshape(xf.shape)
    bshape = (1, x.shape[1]) + (1,) * (x.ndim - 3) + (HEAD_DIM,)
    return (xf * cos.reshape(bshape) + rot * sin.reshape(bshape)).astype(x.dtype)


def sink_attend(q, ks, vs, masks, sink):
    scale = HEAD_DIM ** -0.5
    scores = []
    for k, m in zip(ks, masks):
        s = jnp.einsum('bqkgd,bskd->bkgqs', q, k).astype(jnp.float32) * scale
        if m is not None:
            s = jnp.where(m, s, NEG)
        scores.append(s)
    bn, nq = q.shape[0], q.shape[1]
    sink_col = jnp.broadcast_to(sink.astype(jnp.float32).reshape(1, N_KV_HEADS, KV_GROUP, 1, 1),
                                (bn, N_KV_HEADS, KV_GROUP, nq, 1))
    p = jax.nn.softmax(jnp.concatenate(scores + [sink_col], axis=-1), axis=-1)
    out = None
    off = 0
    for v in vs:
        n = v.shape[1]
        o = jnp.einsum('bkgqs,bskd->bqkgd', p[..., off:off + n].astype(v.dtype), v)
        out = o if out is None else out + o
        off += n
    return out


def split_query_blocks(q):
    bn, t = q.shape[:2]
    nb = t // Q_BLOCK
    return q.reshape(bn, nb, Q_BLOCK, N_KV_HEADS, KV_GROUP, HEAD_DIM).transpose(1, 0, 2, 3, 4, 5)


def merge_query_blocks(o, bn, t):
    return o.transpose(1, 0, 2, 3, 4, 5).reshape(bn, t, N_HEADS * HEAD_DIM)


def context_attention(q, k, v, sink):
    bn, s = q.shape[:2]
    out = lax.map(lambda qi: sink_attend(qi, [k], [v], [None], sink), split_query_blocks(q))
    return merge_query_blocks(out, bn, s)


def latent_attention(q, k, v, ck, cv, sink):
    bn, t = q.shape[:2]
    nb = t // Q_BLOCK
    pad = ((0, 0), (WINDOW, WINDOW), (0, 0), (0, 0))
    kp = jnp.pad(k, pad)
    vp = jnp.pad(v, pad)

    def block(args):
        qi, j = args
        start = j * Q_BLOCK
        kb = lax.dynamic_slice_in_dim(kp, start, BAND, axis=1)
        vb = lax.dynamic_slice_in_dim(vp, start, BAND, axis=1)
        qpos = start + jnp.arange(Q_BLOCK)
        kpos = start - WINDOW + jnp.arange(BAND)
        mask = ((kpos >= 0) & (kpos < t))[None, :] & (jnp.abs(qpos[:, None] - kpos[None, :]) <= WINDOW)
        return sink_attend(qi, [kb, ck], [vb, cv], [mask, None], sink)

    out = lax.map(block, (split_query_blocks(q), jnp.arange(nb)))
    return merge_query_blocks(out, bn, t)


def swiglu(h, w_gate, w_up, w_down):
    return (jax.nn.silu(h @ w_gate) * (h @ w_up)) @ w_down


def _normal(k, shape, scale):
    return jax.random.normal(k, shape, jnp.float32) * scale


def setup_inputs(seed: int = 0) -> dict:
    key = jax.random.key(seed)
    ks = jax.random.split(key, 24)
    d = D_MODEL
    qkv_w = (N_HEADS + 2 * N_KV_HEADS) * HEAD_DIM
    return {
        'x_prompt': _normal(ks[0], (BATCH, SEQ, d), 1.0),
        'x_sample': _normal(ks[1], (DEC_BATCH, DEC_SEQ, d), 1.0),
        'cache_k': _normal(ks[2], (DEC_BATCH, N_ODD, PAST_LEN, N_KV_HEADS, HEAD_DIM), 1.0),
        'cache_v': _normal(ks[3], (DEC_BATCH, N_ODD, PAST_LEN, N_KV_HEADS, HEAD_DIM), 1.0),
        'c': _normal(ks[4], (DEC_BATCH, d), 1.0),
        'c_ctx': _normal(ks[5], (d,), 1.0),
        'mod_w': _normal(ks[6], (DEPTH, d, 6 * d), 0.5 * d ** -0.5),
        'mod_b': _normal(ks[7], (DEPTH, 6 * d), 0.02),
        'norm_pre_mix': 1.0 + _normal(ks[8], (DEPTH, d), 0.02),
        'norm_post_mix': 1.0 + _normal(ks[9], (DEPTH, d), 0.02),
        'norm_pre_ffn': 1.0 + _normal(ks[10], (DEPTH, d), 0.02),
        'norm_post_ffn': 1.0 + _normal(ks[11], (DEPTH, d), 0.02),
        'ab_w_in': _normal(ks[12], (N_EVEN, d, 2 * A_WIDTH + B_WIDTH), d ** -0.5),
        'sgu_w': _normal(ks[13], (N_EVEN, A_GROUPS, CHUNK, CHUNK), CHUNK ** -0.5),
        'sgu_b': 1.0 + _normal(ks[14], (N_EVEN, A_GROUPS, CHUNK), 0.02),
        'sgu_g': 1.0 + _normal(ks[15], (N_EVEN, A_WIDTH), 0.02),
        'ab_w_out': _normal(ks[16], (N_EVEN, A_WIDTH + B_WIDTH, d), (A_WIDTH + B_WIDTH) ** -0.5),
        'attn_w_qkv': _normal(ks[17], (N_ODD, d, qkv_w), d ** -0.5),
        'attn_sink': _normal(ks[18], (N_ODD, N_HEADS), 0.5),
        'attn_w_o': _normal(ks[19], (N_ODD, N_HEADS * HEAD_DIM, d), (N_HEADS * HEAD_DIM) ** -0.5),
        'ffn_w_gate': _normal(ks[20], (DEPTH, d, D_FF), d ** -0.5),
        'ffn_w_up': _normal(ks[21], (DEPTH, d, D_FF), d ** -0.5),
        'ffn_w_down': _normal(ks[22], (DEPTH, D_FF, d), D_FF ** -0.5),
    }


def reference(x_prompt, x_sample, cache_k, cache_v, c, c_ctx, mod_w, mod_b,
              norm_pre_mix, norm_post_mix, norm_pre_ffn, norm_post_ffn,
              ab_w_in, sgu_w, sgu_b, sgu_g, ab_w_out,
              attn_w_qkv, attn_sink, attn_w_o,
              ffn_w_gate, ffn_w_up, ffn_w_down):
    t = x_sample.shape[1]
    cos, sin = axial_rope_tables(t)
    xp, xs = x_prompt, x_sample
    new_k, new_v = [], []
    for layer in range(DEPTH):
        mp = modulation(c_ctx[None, :], mod_w[layer], mod_b[layer])
        msm = modulation(c, mod_w[layer], mod_b[layer])
        hp = modulate(rms_norm(xp, norm_pre_mix[layer]), mp[0], mp[1])
        hs = modulate(rms_norm(xs, norm_pre_mix[layer]), msm[0], msm[1])
        if layer % 2 == 0:
            e = layer // 2
            op = mixer_ab(hp, ab_w_in[e], sgu_w[e], sgu_b[e], sgu_g[e], ab_w_out[e])
            osm = mixer_ab(hs, ab_w_in[e], sgu_w[e], sgu_b[e], sgu_g[e], ab_w_out[e])
        else:
            o = layer // 2
            qp, kp, vp = qkv_proj(hp, attn_w_qkv[o])
            new_k.append(kp)
            new_v.append(vp)
            op = context_attention(qp, kp, vp, attn_sink[o]) @ attn_w_o[o]
            qs, kl, vl = qkv_proj(hs, attn_w_qkv[o])
            qs = apply_rope(qs, cos, sin)
            kl = apply_rope(kl, cos, sin)
            osm = latent_attention(qs, kl, vl, cache_k[:, o], cache_v[:, o], attn_sink[o]) @ attn_w_o[o]
        xp = xp + mp[2] * rms_norm(op, norm_post_mix[layer])
        xs = xs + msm[2] * rms_norm(osm, norm_post_mix[layer])
        hp = modulate(rms_norm(xp, norm_pre_ffn[layer]), mp[3], mp[4])
        hs = modulate(rms_norm(xs, norm_pre_ffn[layer]), msm[3], msm[4])
        fp = swiglu(hp, ffn_w_gate[layer], ffn_w_up[layer], ffn_w_down[layer])
        fs = swiglu(hs, ffn_w_gate[layer], ffn_w_up[layer], ffn_w_down[layer])
        xp = xp + mp[5] * rms_norm(fp, norm_post_ffn[layer])
        xs = xs + msm[5] * rms_norm(fs, norm_post_ffn[layer])
    state_k = jnp.stack(new_k, axis=1)
    state_v = jnp.stack(new_v, axis=1)
    return (xp, xs, state_k, state_v)
```

```python
import contextlib
import numpy as np
import ml_dtypes
import concourse.bass as bass
import concourse.mybir as mybir
from concourse.bass_utils import run_bass_kernel_spmd

F32 = mybir.dt.float32
BF16 = mybir.dt.bfloat16
U8 = mybir.dt.uint8
AF = mybir.ActivationFunctionType
ALU = mybir.AluOpType

D = 1024
DFF = 2816
NFF = DFF // 128
EPS = 1e-6
NCORES = 8


class Op:
    __slots__ = ("eng", "fn", "reads", "writes", "dma", "idx", "waits", "clock",
                 "signal", "sem", "semval", "gidx")

    def __init__(self, eng, fn, reads, writes, dma):
        self.eng = eng
        self.fn = fn
        self.reads = reads
        self.writes = writes
        self.dma = dma
        self.waits = None
        self.clock = None
        self.signal = False
        self.sem = None
        self.semval = None


class RingKey(tuple):
    gen = None


class Prog:
    ENGS = ("pe", "act", "dve", "pool", "sp")
    MAX_DMA_SEMS = 72

    def __init__(self, nc):
        self.nc = nc
        self.ops = []
        self.byeng = {e: [] for e in self.ENGS}
        self.ring_gen = {}

    def add(self, eng, fn, reads=(), writes=(), dma=False):
        for r in reads:
            assert getattr(r, "gen", None) is None or self.ring_gen[r[1]] == r.gen, ("stale ring tile", r)
        op = Op(eng, fn, tuple(reads), tuple(writes), dma)
        op.gidx = len(self.ops)
        op.idx = len(self.byeng[eng])
        self.ops.append(op)
        self.byeng[eng].append(op)
        return op

    def analyze(self):
        last_w = {}
        readers = {}
        clock = {e: {} for e in self.ENGS}
        dsems = []
        for op in self.ops:
            E = op.eng
            deps = {}
            for r in op.reads:
                d = last_w.get(r)
                if d is not None:
                    deps[d.gidx] = (d, True)
                if isinstance(r, tuple) and r[0] == "ps":
                    for d in readers.get(r, ()):
                        if d.eng != E and d.gidx not in deps:
                            deps[d.gidx] = (d, False)
            for w in op.writes:
                d = last_w.get(w)
                if d is not None and d.gidx not in deps:
                    deps[d.gidx] = (d, False)
                for d in readers.get(w, ()):
                    if d.gidx not in deps:
                        deps[d.gidx] = (d, False)
            ck = clock[E]
            need = {}
            for d, raw in deps.values():
                if d is op:
                    continue
                if d.dma:
                    key, val = ("dma", d.sem), d.semval
                else:
                    if d.eng == E and not op.dma and E == "pe":
                        continue
                    key, val = d.eng, d.idx + 1
                if ck.get(key, 0) >= val:
                    continue
                if key not in need or need[key][0] < val:
                    need[key] = (val, d)
            if op.dma:
                chosen = None
                cls = "sw" if E == "pool" else "hw"
                for si, (tot, lop, scls) in enumerate(dsems):
                    if scls != cls:
                        continue
                    k = ("dma", si)
                    have = max(ck.get(k, 0), need[k][0] if k in need else 0)
                    if have >= tot:
                        chosen = si
                        break
                if chosen is None:
                    if len(dsems) < self.MAX_DMA_SEMS:
                        dsems.append([0, None, cls])
                        chosen = len(dsems) - 1
                    else:
                        chosen = min((i for i in range(len(dsems)) if dsems[i][2] == cls),
                                     key=lambda i: dsems[i][1].gidx)
                        tot, lop, _ = dsems[chosen]
                        need[("dma", chosen)] = (tot, lop)
                op.sem = chosen
                dsems[chosen][0] += 16
                dsems[chosen][1] = op
                op.semval = dsems[chosen][0]
                op.signal = True
            if need:
                ck = dict(ck)
                for key, (val, d) in need.items():
                    d.signal = True
                    for k2, v2 in d.clock.items():
                        if ck.get(k2, 0) < v2:
                            ck[k2] = v2
                    if ck.get(key, 0) < val:
                        ck[key] = val
                clock[E] = ck
            op.waits = [(key, d) for key, (val, d) in need.items()]
            op.clock = ck
            for r in op.reads:
                readers.setdefault(r, []).append(op)
            for w in op.writes:
                last_w[w] = op
                readers[w] = []
        self.ndsems = len(dsems)
        for e in self.ENGS:
            n = 0
            for op in self.byeng[e]:
                if op.dma:
                    continue
                if op.signal:
                    n += 1
                    op.semval = n

    def emit(self, block, esems, dsems):
        engobj = {"pe": "tensor", "act": "scalar", "dve": "vector", "pool": "gpsimd", "sp": "sync"}

        def make(ename):
            ops = self.byeng[ename]

            def body(e):
                for op in ops:
                    for key, d in op.waits:
                        if isinstance(key, tuple):
                            e.wait_ge(dsems[key[1]], d.semval)
                        else:
                            e.wait_ge(esems[key], d.semval)
                    ins = op.fn(e)
                    assert ins is not None, (ename, op.reads, op.writes)
                    if op.signal:
                        if op.dma:
                            ins.then_inc(dsems[op.sem], 16)
                        else:
                            ins.then_inc(esems[ename], 1)
            return body

        for ename in self.ENGS:
            getattr(block, engobj[ename])(make(ename))


def _dft(n, scale, kind):
    idx = (np.arange(n)[:, None].astype(np.int64) * np.arange(n)[None, :].astype(np.int64)) % n
    ang = 2.0 * np.pi * idx.astype(np.float64) / n
    m = np.cos(ang) if kind == "c" else np.sin(ang)
    return (m * scale).astype(np.float32)


def _consts():
    bf = ml_dtypes.bfloat16
    c = {}
    c["ident"] = np.eye(128, dtype=np.float32).astype(bf)
    c["ones"] = np.ones((128, 128), dtype=np.float32).astype(bf)
    c["identf"] = np.eye(16, dtype=np.float32)
    sl = np.arange(128)[:, None]
    ql = np.arange(128)[None, :]
    c["masks"] = np.concatenate([(sl <= ql), (ql <= sl)], axis=1).astype(np.float32).astype(bf)
    c["c1024"] = _dft(1024, 1.0 / 32.0, "c").astype(bf)
    c["s1024"] = _dft(1024, 1.0 / 32.0, "s").astype(bf)
    c["c256"] = _dft(256, 1.0 / 16.0, "c").astype(bf)
    c["s256"] = _dft(256, 1.0 / 16.0, "s").astype(bf)
    c["cc"] = _dft(128, 1.0 / np.sqrt(128.0), "c").astype(bf)
    c["nsc"] = (-_dft(128, 1.0 / np.sqrt(128.0), "s")).astype(bf)
    t = 1024
    rows = np.repeat(np.arange(t // 64), 64).astype(np.float32)
    cols = np.tile(np.arange(64), t // 64).astype(np.float32)
    inv = (np.float32(10000.0) ** (-np.arange(0, 64, 2, dtype=np.float32) / np.float32(64))).astype(np.float32)
    ar = rows[:, None] * inv
    ac = cols[:, None] * inv
    ang = np.concatenate([ar, ar, ac, ac], axis=-1).astype(np.float32)
    sign = np.where((np.arange(128) % 64) < 32, -1.0, 1.0).astype(np.float32)
    c["cosT"] = np.ascontiguousarray(np.cos(ang).T.astype(np.float32))
    c["ssinT"] = np.ascontiguousarray((np.sin(ang) * sign[None, :]).T.astype(np.float32))
    return c


_IN_SHAPES = {
    "xs": ([1024, D], F32), "xp": ([1024, D], F32), "ck": ([256, 256], F32), "cv": ([256, 256], F32),
    "cond": ([2, D], F32),
    "mod_w": ([2, D, 6 * D], F32), "mod_b": ([2, 6 * D], F32),
    "norm_pre_mix": ([2, D], F32), "norm_post_mix": ([2, D], F32),
    "norm_pre_ffn": ([2, D], F32), "norm_post_ffn": ([2, D], F32),
    "ab_w_in": ([1, D, 1536], F32), "sgu_w": ([1, 4, 128, 128], F32), "sgu_b": ([1, 4, 128], F32),
    "sgu_g": ([1, 512], F32), "ab_w_out": ([1, D, D], F32),
    "attn_w_qkv": ([1, D, 1536], F32), "attn_sink": ([1, 8], F32), "attn_w_o": ([1, D, D], F32),
    "ffn_w_gate": ([2, D, DFF], F32), "ffn_w_up": ([2, D, DFF], F32), "ffn_w_down": ([2, DFF, D], F32),
    "ident": ([128, 128], BF16), "identf": ([16, 16], F32), "ones": ([128, 128], BF16), "masks": ([128, 256], BF16),
    "c1024": ([1024, 1024], BF16), "s1024": ([1024, 1024], BF16),
    "c256": ([256, 256], BF16), "s256": ([256, 256], BF16),
    "cc": ([128, 128], BF16), "nsc": ([128, 128], BF16),
    "cosT": ([128, 1024], F32), "ssinT": ([128, 1024], F32),
}
_OUT_SHAPES = {"ys": [1024, D], "yp": [1024, D], "sk": [1024, 256], "sv": [1024, 256]}


def build(phases=None):
    nc = bass.Bass("TRN2", target_bir_lowering=False)
    I = {n: nc.dram_tensor(n, s, dt, kind="ExternalInput").ap() for n, (s, dt) in _IN_SHAPES.items()}
    O = {n: nc.dram_tensor(n, s, F32, kind="ExternalOutput").ap() for n, s in _OUT_SHAPES.items()}
    MSCR = nc.dram_tensor("mscr", [2, 2, 6 * D], F32).ap()

    rem = nc.sbuf_bytes_remaining
    arena = nc.alloc_sbuf_tensor("arena", [128, rem - 2048], U8)
    base = nc.lookup_mloc(arena).addr
    limit = base + rem - 2048
    cur = [base]

    def alloc(name, shape, dt, at=None):
        nbytes = int(np.prod(shape[1:])) * (4 if dt == F32 else 2)
        if at is None:
            at = cur[0]
            cur[0] += (nbytes + 31) // 32 * 32
            assert cur[0] <= limit, (name, cur[0], limit)
        return nc.alloc_sbuf_tensor_at(name, list(shape), dt, offset=at)

    X = alloc("X", [128, 16, D], F32)
    HT = alloc("HT", [128, 8, 1024], BF16)
    OT_off = cur[0]
    OT = alloc("OT", [128, 8, 1024], BF16)
    HID_off = cur[0]
    HIDB = 45056
    HID = alloc("HID", [128, NFF, 1024], BF16)
    cur[0] = HID_off + HIDB
    cur_ring0 = cur[0]
    RING = [alloc("ring%d" % i, [128, 4096], BF16) for i in range(4)]
    MOD = [alloc("mod%d" % i, [128, D], F32) for i in range(3)]
    IDN = alloc("idn", [128, 128], BF16)
    ONES = alloc("ones_sb", [128, 128], BF16)
    MASKS = alloc("masks_sb", [128, 256], BF16)
    JUNK = alloc("junk", [128, D], BF16)
    SS = alloc("ss", [128, 8], F32)
    RS = alloc("rs", [128, 8], F32)
    SS2 = alloc("ss2", [128, 2], F32)
    RS2 = alloc("rs2", [128, 2], F32)
    SCT0 = alloc("sct0", [128, 8, 2], F32)
    C16 = alloc("c16", [16, 128], F32)
    IDF = alloc("idf", [16, 16], F32)
    SCT = alloc("sct", [128, 8, 2], BF16)
    SINKE = alloc("sinke", [128, 8], F32)
    SGG = alloc("sgg", [128, 4], F32)
    BIASB = alloc("biasb", [128, 4, 128], F32)
    SGWT = alloc("sgwt", [128, 4, 128], BF16)
    CCS = alloc("ccs", [128, 128], BF16)
    NSCS = alloc("nscs", [128, 128], BF16)
    LNS = alloc("lns", [128, 8, 4], F32)
    MRW = [alloc("mrw%d" % j, [2, 3, 256], F32) for j in range(2)]
    MW = [alloc("mw%d" % j, [128, 8, 256], BF16) for j in range(2)]
    NSC = {}
    for kind, off0 in (("OT", OT_off), ("HID", HID_off)):
        NSC[kind] = dict(
            tmp=[alloc("tmp%s%d" % (kind, j), [128, D], F32, at=off0 + 4096 * j) for j in range(2)],
            hb=[alloc("hb%s%d" % (kind, j), [128, D], BF16, at=off0 + 8192 + 2048 * j) for j in range(4)],
            tk=[[(kind, 2 * j), (kind, 2 * j + 1)] for j in range(2)],
            hk=[[(kind, 4 + j)] for j in range(4)])
    SG = [alloc("sg%d" % j, [128, 512], F32, at=OT_off + 12288 + 2048 * j) for j in range(2)]
    VLN = alloc("vln", [128, 8, 512], BF16, at=HID_off)
    ZB = alloc("zb", [128, 8, 512], BF16, at=HID_off + 8192)
    GV = alloc("gv", [128, 8, 512], F32, at=HID_off + 16384)
    UG = [alloc("ug%d" % j, [128, 512], F32, at=HID_off + 32768 + 2048 * j) for j in range(2)]
    CZ = [alloc("cz%d" % j, [128, 2, 512], BF16, at=HID_off + 36864 + 2048 * j) for j in range(2)]
    SGT = [alloc("sgt%d" % j, [128, 512], F32, at=HID_off + 40960 + 2048 * j) for j in range(2)]
    QT = alloc("qt", [128, 4, 1024], BF16, at=HID_off)
    KT = alloc("kt", [128, 2, 1280], BF16, at=HID_off + 8192)
    VT = alloc("vt", [128, 10, 256], BF16, at=HID_off + 13312)
    ROPE = alloc("rope", [128, 2, 1024], F32, at=HID_off + 18432)
    EXPB = [alloc("expb%d" % j, [128, 512], BF16, at=HID_off + 26624 + 1024 * j) for j in range(3)]
    RCP = [alloc("rcp%d" % j, [128, 512], F32, at=HID_off + 29696 + 2048 * j) for j in range(2)]
    KVST = [alloc("kvst%d" % j, [128, 512], F32, at=HID_off + 33792 + 2048 * j) for j in range(2)]
    RT = [alloc("rt%d" % j, [128, 512], F32, at=HID_off + 37888 + 2048 * j) for j in range(2)]
    CKB = alloc("ckb", [128, 2, 256], BF16, at=HID_off + 41984)
    print('SBUF free bytes/partition:', limit - cur[0])
    PSUM = nc.alloc_psum_tensor("psum", [128, 4096], F32)

    def bank(b, n=1):
        return PSUM[:, b * 512:(b + n) * 512]

    def bankbf(b):
        return PSUM[:, b * 512:(b + 1) * 512].bitcast(BF16)

    EPSB = alloc("epsb", [128, 1], F32)
    P = Prog(nc)
    st = {"ring": 0, "b1": 0, "b2": 0, "k": 0, "mk": 0}
    FENCE = alloc("fence", [128, 1], F32)
    hidk = [("HID", c) for c in range(NFF)] + ["hidx"]
    REGK = list(hidk)
    REGK += [(n, i) for n in ("vln", "zb", "gv") for i in range(8)]
    REGK += [(n, j) for n in ("ug", "sgt", "rcp", "kvst") for j in range(2)]
    REGK += [("cz", j, c) for j in range(2) for c in range(2)]
    REGK += [("qt", h, t) for h in range(4) for t in range(2)] + [("kt", g, t) for g in range(2) for t in range(2)]
    REGK += [("ktc", g) for g in range(2)] + [("vt", i) for i in range(10)] + ["rope", "ckb"]
    REGK += [("expb", j) for j in range(3)] + [("rt", j) for j in range(2)]

    def fence():
        P.add("dve", lambda e: e.memset(FENCE[:], 0.0), writes=REGK)

    def ps1():
        b = st["b1"] % 8
        st["b1"] += 1
        return b

    def ps2():
        b = (st["b2"] % 4) * 2
        st["b2"] += 1
        return b

    def pk(b, n=1):
        return [("ps", b + i) for i in range(n)]

    def ring_load(src, shape, eng="pool"):
        s = st["ring"] % 4
        st["ring"] += 1
        n = int(np.prod(shape[1:]))
        assert n <= 4096
        if len(shape) == 3:
            dst = RING[s][:, 0:n].rearrange("p (a b) -> p a b", a=shape[1])
        else:
            dst = RING[s][:, 0:n]
        key = RingKey(("ring", s))
        key.gen = P.ring_gen[s] = P.ring_gen.get(s, 0) + 1
        P.add(eng, lambda e, dst=dst, src=src: e.dma_start(out=dst, in_=src), writes=[key], dma=True)
        return dst, key

    def wtile(w2d, c0, c1, r0=0, nk=8):
        return w2d[r0:r0 + nk * 128, :].rearrange("(kc p) n -> p kc n", p=128)[:, :, c0:c1]

    P.add("act", lambda e: e.dma_start(out=C16[:], in_=I["cond"].rearrange("c (kc p) -> (c kc) p", p=128)),
          writes=["c16"], dma=True)
    P.add("act", lambda e: e.dma_start(out=IDF[:], in_=I["identf"]), writes=["idf"], dma=True)
    P.add("sp", lambda e: e.dma_start(out=IDN[:], in_=I["ident"]), writes=["idn"], dma=True)
    P.add("sp", lambda e: e.dma_start(out=ONES[:], in_=I["ones"]), writes=["ones"], dma=True)
    P.add("sp", lambda e: e.dma_start(out=MASKS[:], in_=I["masks"]), writes=["masks"], dma=True)
    P.add("sp", lambda e: e.dma_start(out=CCS[:], in_=I["cc"]), writes=["ccs"], dma=True)
    P.add("sp", lambda e: e.dma_start(out=NSCS[:], in_=I["nsc"]), writes=["nscs"], dma=True)
    P.add("sp", lambda e: e.dma_start(out=SINKE[:], in_=I["attn_sink"][0:1, :].broadcast_to([128, 8])),
          writes=["sinke"], dma=True)

    def load_small_consts():
        P.add("sp", lambda e: e.dma_start(out=SGG[:], in_=I["sgu_g"][0].rearrange("(g p) -> p g", p=128),
                                          allow_slow_non_contiguous=True), writes=["sgg"], dma=True)
        P.add("sp", lambda e: e.dma_start(
            out=BIASB[:], in_=I["sgu_b"][0:1].broadcast_to([128, 4, 128])), writes=["biasb"], dma=True)
    P.add("dve", lambda e: e.memset(EPSB[:], EPS), writes=["epsb"])

    def store_x(half):
        for i in range(half * 8, half * 8 + 8):
            dst = O["ys"] if i < 8 else O["yp"]
            r = (i % 8) * 128
            P.add("sp", lambda e, i=i, dst=dst, r=r: e.dma_start(out=dst[r:r + 128, :], in_=X[:, i, :]),
                  reads=[("X", i)], writes=[("o_y", i)], dma=True)

    def load_x(tiles):
        for i in tiles:
            src = I["xs"] if i < 8 else I["xp"]
            r = (i % 8) * 128
            P.add("sp", lambda e, i=i, src=src, r=r: e.dma_start(out=X[:, i, :], in_=src[r:r + 128, :]),
                  writes=[("X", i)], dma=True)

    P.add("act", lambda e: e.activation(out=SINKE[:], in_=SINKE[:], func=AF.Exp), reads=["sinke"], writes=["sinke"])
    P.add("pe", lambda e: e.transpose(out=PSUM[:, 0:16], in_=C16[:], identity=IDF[:]), reads=["c16", "idf"],
          writes=[("ps", 0)])
    P.add("act", lambda e: e.activation(out=SCT[:].rearrange("p kc c -> p c kc"),
                                        in_=PSUM[:, 0:16].rearrange("p (c kc) -> p c kc", c=2), func=AF.Silu),
          reads=[("ps", 0)], writes=["sct"])

    norm_of_chunk = {1: "norm_pre_mix", 2: "norm_post_mix", 4: "norm_pre_ffn", 5: "norm_post_ffn"}
    allk = [("OT", c) for c in range(8)]
    MBW = 256
    NMB = 6 * D // MBW
    mod_pending = [(l, cb) for l in range(2) for cb in range(NMB)]

    def mod_block(l, cb):
        j = st["mk"] % 2
        st["mk"] += 1
        mk = [("mrow", j)]
        wk = ("mw", j)
        wt = MW[j]
        c0 = cb * MBW
        P.add("pool", lambda e: e.dma_start(out=wt[:], in_=wtile(I["mod_w"][l], c0, c0 + MBW)), writes=[wk], dma=True)
        chunk, hc = c0 // D, (c0 % D)
        P.add("sp", lambda e: e.dma_start(out=MRW[j][:, 0, :], in_=I["mod_b"][l:l + 1, c0:c0 + MBW]
                                          .broadcast_to([2, MBW])), writes=mk, dma=True)
        if chunk in norm_of_chunk:
            P.add("sp", lambda e: e.dma_start(
                out=MRW[j][:, 1, :], in_=I[norm_of_chunk[chunk]][l:l + 1, hc:hc + MBW].broadcast_to([2, MBW])),
                writes=mk, dma=True)
        b = ps1()

        def mm(e):
            ins = None
            for kc in range(8):
                ins = e.matmul(bank(b)[0:2, 0:MBW], lhsT=SCT[:, kc, :], rhs=wt[:, kc, :],
                               start=(kc == 0), stop=(kc == 7))
            return ins
        P.add("pe", mm, reads=[wk, "sct"], writes=pk(b))
        P.add("dve", lambda e: e.tensor_tensor(out=MRW[j][:, 2, :], in0=bank(b)[0:2, 0:MBW], in1=MRW[j][:, 0, :],
                                               op=ALU.add), reads=pk(b) + mk, writes=mk)
        if chunk in (1, 4):
            P.add("dve", lambda e: e.scalar_tensor_tensor(
                out=MRW[j][:, 2, :], in0=MRW[j][:, 2, :], scalar=1.0, in1=MRW[j][:, 1, :],
                op0=ALU.add, op1=ALU.mult), reads=mk, writes=mk)
        elif chunk in (2, 5):
            P.add("dve", lambda e: e.tensor_tensor(out=MRW[j][:, 2, :], in0=MRW[j][:, 2, :], in1=MRW[j][:, 1, :],
                                                   op=ALU.mult), reads=mk, writes=mk)
        P.add("sp", lambda e: e.dma_start(out=MSCR[l, :, c0:c0 + MBW], in_=MRW[j][:, 2, :]),
              reads=mk, writes=[("mscr", l, cb)], dma=True)

    ROWS = alloc("rows", [2, 4096], F32, at=cur_ring0)
    GROW = alloc("grow", [2, 2048], F32, at=cur_ring0 + 16384)
    MWX = [alloc("mwx%d" % j, [128, 8, 256], BF16, at=cur_ring0 + 24576 + 4096 * j) for j in range(2)]

    def mod_group_fast(l, sub, after_first=None):
        rk = [RingKey(("ring", q)) for q in range(4)]
        for q in range(4):
            rk[q].gen = P.ring_gen[q] = P.ring_gen.get(q, 0) + 1
        st["ring"] = 4
        nblk = [0]
        r0 = sub * 3 * D
        P.add("sp", lambda e: e.dma_start(out=ROWS[:, 0:3 * D], in_=I["mod_b"][l:l + 1, r0:r0 + 3 * D]
                                          .broadcast_to([2, 3 * D])), writes=rk[0:2], dma=True)
        for gi, nm in enumerate(("norm_pre_mix", "norm_post_mix") if sub == 0 else ("norm_pre_ffn", "norm_post_ffn")):
            P.add("sp", lambda e, gi=gi, nm=nm: e.dma_start(
                out=GROW[:, gi * D:(gi + 1) * D], in_=I[nm][l:l + 1, :].broadcast_to([2, D])),
                writes=[rk[2]], dma=True)
        blocks = [(pl, pcb) for (pl, pcb) in mod_pending if pl == l and pcb // (NMB // 2) == sub]
        for blk in blocks:
            mod_pending.remove(blk)
            cb = blk[1]
            j = nblk[0] % 4
            nblk[0] += 1
            if j < 2:
                wk, wt = ("mw", j), MW[j]
            else:
                wk, wt = ("mwx", j), MWX[j - 2]
            c0 = cb * MBW
            lc = c0 - r0
            P.add("pool", lambda e, wt=wt, c0=c0: e.dma_start(out=wt[:], in_=wtile(I["mod_w"][l], c0, c0 + MBW)),
                  writes=[wk], dma=True)
            b = ps1()

            def mm(e, wt=wt, b=b):
                ins = None
                for kc in range(8):
                    ins = e.matmul(bank(b)[0:2, 0:MBW], lhsT=SCT[:, kc, :], rhs=wt[:, kc, :],
                                   start=(kc == 0), stop=(kc == 7))
                return ins
            P.add("pe", mm, reads=[wk, "sct"], writes=pk(b))
            P.add("dve", lambda e, b=b, lc=lc: e.tensor_tensor(
                out=ROWS[:, lc:lc + MBW], in0=bank(b)[0:2, 0:MBW], in1=ROWS[:, lc:lc + MBW], op=ALU.add),
                reads=pk(b) + rk[0:2], writes=rk[0:2])
            chunk = lc // D
            hc = lc % D
            if chunk == 1:
                P.add("dve", lambda e, lc=lc, hc=hc: e.scalar_tensor_tensor(
                    out=ROWS[:, lc:lc + MBW], in0=ROWS[:, lc:lc + MBW], scalar=1.0, in1=GROW[:, hc:hc + MBW],
                    op0=ALU.add, op1=ALU.mult), reads=rk, writes=rk[0:2])
            elif chunk == 2:
                P.add("dve", lambda e, lc=lc, hc=hc: e.tensor_tensor(
                    out=ROWS[:, lc:lc + MBW], in0=ROWS[:, lc:lc + MBW], in1=GROW[:, D + hc:D + hc + MBW],
                    op=ALU.mult), reads=rk, writes=rk[0:2])
            if len(blocks) == 12 and blk is blocks[7]:
                P.add("sp", lambda e: e.dma_start(out=MSCR[l, :, r0:r0 + 2 * D], in_=ROWS[:, 0:2 * D]),
                      reads=rk[0:2], writes=[("mscr", l, bb[1]) for bb in blocks[0:8]], dma=True)
                if after_first is not None:
                    after_first()
        if len(blocks) == 12:
            P.add("sp", lambda e: e.dma_start(out=MSCR[l, :, r0 + 2 * D:r0 + 3 * D], in_=ROWS[:, 2 * D:3 * D]),
                  reads=rk[0:2], writes=[("mscr", l, bb[1]) for bb in blocks[8:12]], dma=True)
        else:
            P.add("sp", lambda e: e.dma_start(out=MSCR[l, :, r0:r0 + 3 * D], in_=ROWS[:, 0:3 * D]),
                  reads=rk[0:2], writes=[("mscr", l, bb[1]) for bb in blocks], dma=True)
        P.add("dve", lambda e: e.memset(FENCE[:], 0.0), writes=[("mwx", 2), ("mwx", 3), rk[3]])

    def drain_mod(n):
        for _ in range(n):
            if mod_pending:
                mod_block(*mod_pending.pop(0))

    def ensure_mod(l, sub):
        while any(pl == l and (pcb // (NMB // 2)) == sub for (pl, pcb) in mod_pending):
            mod_block(*mod_pending.pop(0))

    wt, wk = ring_load(I["sgu_w"][0].rearrange("g p q -> p g q"), [128, 4, 128])
    b = ps1()

    def sgw_t(e, wt=wt, b=b):
        ins = None
        for g in range(4):
            ins = e.transpose(out=bankbf(b)[:, g * 128:(g + 1) * 128], in_=wt[:, g, :], identity=IDN[:])
        return ins
    P.add("pe", sgw_t, reads=[wk, "idn"], writes=pk(b))
    P.add("dve", lambda e, b=b: e.tensor_copy(out=SGWT[:].rearrange("p g q -> p (g q)"), in_=bankbf(b)[:, 0:512]),
          reads=pk(b), writes=["sgwt"])

    fence()

    def load_mod_slots01(l, half, sub):
        load_mod(l, half, sub, (0, 1), ensure=False)

    def load_mod(l, half, sub, slots=(0, 1, 2), ensure=True):
        if ensure:
            ensure_mod(l, sub)
        cond = 0 if half == 0 else 1
        base_v = 0 if sub == 0 else 3
        for slot, v in ((0, base_v + 1), (1, base_v + 0), (2, base_v + 2)):
            if slot not in slots:
                continue
            P.add("sp", lambda e, slot=slot, v=v, l=l, cond=cond: e.dma_start(
                out=MOD[slot][:], in_=MSCR[l, cond:cond + 1, v * D:(v + 1) * D].broadcast_to([128, D])),
                reads=[("mscr", l, 4 * v + q) for q in range(4)], writes=[("mod", slot)], dma=True)

    def rstd_cols(ss, rs, n, keys_r, keys_w):
        P.add("dve", lambda e: e.tensor_scalar(out=rs[:, 0:n], in0=ss[:, 0:n], scalar1=1.0 / D, scalar2=EPS,
                                               op0=ALU.mult, op1=ALU.add), reads=keys_r, writes=keys_w)
        P.add("act", lambda e: e.activation(out=rs[:, 0:n], in_=rs[:, 0:n], func=AF.Sqrt), reads=keys_w, writes=keys_w)
        P.add("dve", lambda e: e.reciprocal(out=rs[:, 0:n], in_=rs[:, 0:n]), reads=keys_w, writes=keys_w)

    def pre_norm_tile(xh, hi, kind, bank_t):
        sc = NSC[kind]
        xi = xh * 8 + hi
        q = hi % 4
        j = hi % 2
        sk, rk = ("ss", hi), ("rs", hi)
        P.add("act", lambda e: e.activation(out=JUNK[:], in_=X[:, xi, :], func=AF.Square,
                                            accum_out=SS[:, hi:hi + 1]), reads=[("X", xi)], writes=["junk", sk])
        P.add("act", lambda e: e.activation(out=RS[:, hi:hi + 1], in_=SS[:, hi:hi + 1], func=AF.Sqrt,
                                            bias=EPSB[:, 0:1], scale=1.0 / D), reads=[sk, "epsb"], writes=[rk])
        P.add("dve", lambda e: e.tensor_tensor(out=sc["tmp"][j][:], in0=X[:, xi, :], in1=MOD[0][:], op=ALU.mult),
              reads=[("X", xi), ("mod", 0)], writes=sc["tk"][j])
        P.add("dve", lambda e: e.reciprocal(out=RS[:, hi:hi + 1], in_=RS[:, hi:hi + 1]), reads=[rk], writes=[rk])
        P.add("dve", lambda e: e.scalar_tensor_tensor(
            out=sc["hb"][q][:], in0=sc["tmp"][j][:], scalar=RS[:, hi:hi + 1], in1=MOD[1][:],
            op0=ALU.mult, op1=ALU.add), reads=sc["tk"][j] + [rk, ("mod", 1)], writes=sc["hk"][q])

        def tail():
            b = bank_t

            def tr(e):
                ins = None
                for kc in range(8):
                    ins = e.transpose(out=bankbf(b)[:, kc * 128:(kc + 1) * 128],
                                      in_=sc["hb"][q][:, kc * 128:(kc + 1) * 128], identity=IDN[:])
                return ins
            P.add("pe", tr, reads=sc["hk"][q] + ["idn"], writes=pk(b))
            P.add("act", lambda e: e.activation(
                out=HT[:, :, hi * 128:(hi + 1) * 128], in_=bankbf(b).rearrange("p (kc t) -> p kc t", kc=8),
                func=AF.Copy), reads=pk(b), writes=[("HT", hi)])
        return tail

    def post_norm(half, i, b):
        t0 = half * 8
        j = i % 2
        P.add("act", lambda e: e.activation(out=JUNK[:], in_=bank(b, 2), func=AF.Square,
                                            accum_out=SS2[:, j:j + 1]), reads=pk(b, 2), writes=["junk", ("ss2", j)])
        P.add("act", lambda e: e.activation(out=RS2[:, j:j + 1], in_=SS2[:, j:j + 1], func=AF.Sqrt,
                                            bias=EPSB[:, 0:1], scale=1.0 / D),
              reads=[("ss2", j), "epsb"], writes=[("rs2", j)])
        P.add("dve", lambda e: e.tensor_tensor(out=bank(b, 2), in0=bank(b, 2), in1=MOD[2][:], op=ALU.mult),
              reads=pk(b, 2) + [("mod", 2)], writes=pk(b, 2))
        P.add("dve", lambda e: e.reciprocal(out=RS2[:, j:j + 1], in_=RS2[:, j:j + 1]),
              reads=[("rs2", j)], writes=[("rs2", j)])
        P.add("dve", lambda e: e.scalar_tensor_tensor(
            out=X[:, t0 + i, :], in0=bank(b, 2), scalar=RS2[:, j:j + 1], in1=X[:, t0 + i, :],
            op0=ALU.mult, op1=ALU.add), reads=pk(b, 2) + [("rs2", j), ("X", t0 + i)], writes=[("X", t0 + i)])

    def out_proj(half, w2d, nxt):
        w0, k0 = ring_load(wtile(w2d, 0, 512), [128, 8, 512])
        w1, k1 = ring_load(wtile(w2d, 512, 1024), [128, 8, 512])
        if nxt is not None:
            load_mod(nxt[0], nxt[1], nxt[2], (0, 1))
            fence()
        pend = []
        for i in range(8):
            b = 2 * (i % 3)

            def mm(e, i=i, b=b):
                ins = None
                for hcol, w in ((0, w0), (1, w1)):
                    for c in range(8):
                        ins = e.matmul(bank(b + hcol), lhsT=OT[:, c, i * 128:(i + 1) * 128], rhs=w[:, c, :],
                                       start=(c == 0), stop=(c == 7))
                return ins
            P.add("pe", mm, reads=[k0, k1] + allk, writes=pk(b, 2))
            if len(pend) > 1:
                pend.pop(0)()
            post_norm(half, i, b)
            if nxt is not None and i >= 4:
                pend.append(pre_norm_tile(nxt[1], i - 4, "HID", 6 + (i % 2)))
        if nxt is not None:
            for i in range(4, 8):
                pend.append(pre_norm_tile(nxt[1], i, "HID", 6 + (i % 2)))
                pend.pop(0)()
        while pend:
            pend.pop(0)()

    def ffn(l, half, nxt):
        wg2, wu2, wd2 = I["ffn_w_gate"][l], I["ffn_w_up"][l], I["ffn_w_down"][l]
        groups = [(g * 4, min(4, NFF - g * 4)) for g in range((NFF + 3) // 4)]
        dgroups = [(0, 2)] + [(2 + g * 4, 4) for g in range(5)]
        for (f0, nf) in groups:
            wg, kg = ring_load(wtile(wg2, f0 * 128, (f0 + nf) * 128), [128, 8, nf * 128])
            wu, ku = ring_load(wtile(wu2, f0 * 128, (f0 + nf) * 128), [128, 8, nf * 128])
            for tt in range(2):
                drain_mod(2)
                for f in range(nf):
                    bg = ps1()
                    bu = ps1()

                    def mm(e, w=wg, f=f, tt=tt, b=bg):
                        ins = None
                        for kc in range(8):
                            ins = e.matmul(bank(b), lhsT=w[:, kc, f * 128:(f + 1) * 128],
                                           rhs=HT[:, kc, tt * 512:(tt + 1) * 512], start=(kc == 0), stop=(kc == 7))
                        return ins
                    htk = [("HT", tt * 4 + q) for q in range(4)]
                    P.add("pe", mm, reads=[kg] + htk, writes=pk(bg))
                    P.add("pe", lambda e, mm=mm, w=wu, f=f, tt=tt, b=bu: mm(e, w, f, tt, b), reads=[ku] + htk, writes=pk(bu))
                    j = st["k"] % 2
                    st["k"] += 1
                    sgk = [("OT", 6 + j)]
                    P.add("act", lambda e, j=j, b=bg: e.activation(out=SG[j][:], in_=bank(b), func=AF.Silu),
                          reads=pk(bg), writes=sgk)
                    P.add("dve", lambda e, j=j, b=bu, ff=f0 + f, tt=tt: e.tensor_tensor(
                        out=HID[:, ff, tt * 512:(tt + 1) * 512], in0=SG[j][:], in1=bank(b), op=ALU.mult),
                        reads=pk(bu) + sgk, writes=[("HID", f0 + f)])
        if nxt is not None:
            load_mod(nxt[0], nxt[1], nxt[2], (0, 1))
        tails = []
        for ps in range(2):
            for gi, (f0, nf) in enumerate(dgroups):
                wd, kd = ring_load(wd2[f0 * 128:(f0 + nf) * 128, :].rearrange("(c p) n -> p c n", p=128),
                                   [128, nf, 1024])
                for ii in range(4):
                    i = ps * 4 + ii

                    def mm(e, wd=wd, f0=f0, nf=nf, i=i, ii=ii, first=(gi == 0), last=(gi == len(dgroups) - 1)):
                        ins = None
                        for c in range(nf):
                            for hcol in range(2):
                                ins = e.matmul(bank(2 * ii + hcol), lhsT=HID[:, f0 + c, i * 128:(i + 1) * 128],
                                               rhs=wd[:, c, hcol * 512:(hcol + 1) * 512],
                                               start=(first and c == 0), stop=(last and c == nf - 1))
                        return ins
                    P.add("pe", mm, reads=[kd] + [("HID", f0 + c) for c in range(nf)], writes=pk(2 * ii, 2))
            for ii in range(4):
                post_norm(half, ps * 4 + ii, 2 * ii)
            while tails:
                tails.pop(0)()
            st["b1"] = 0
            st["b2"] = 0
            if nxt is not None:
                for q in range(4):
                    tails.append(pre_norm_tile(nxt[1], ps * 4 + q, "OT", 2 * q))
        while tails:
            tails.pop(0)()

    def mixer_ab(half, nxt):
        w_in = I["ab_w_in"][0]
        wv, kv = ring_load(wtile(w_in, 512, 1024), [128, 8, 512])
        for i in range(8):
            b = ps1()

            def mm(e, i=i, b=b, w=wv):
                ins = None
                for kc in range(8):
                    ins = e.matmul(bank(b), lhsT=HT[:, kc, i * 128:(i + 1) * 128], rhs=w[:, kc, :],
                                   start=(kc == 0), stop=(kc == 7))
                return ins
            P.add("pe", mm, reads=[kv, ("HT", i)], writes=pk(b))
            if i % 2:
                drain_mod(1)
            P.add("act", lambda e, i=i, b=b: e.activation(out=GV[:, i, :], in_=bank(b), func=AF.Gelu_apprx_tanh,
                                                          accum_out=LNS[:, i, 0:1]),
                  reads=pk(b), writes=[("gv", i), ("lns", i)])
            P.add("dve", lambda e, i=i: e.tensor_scalar(out=LNS[:, i, 2:3], in0=LNS[:, i, 0:1], scalar1=-1.0 / 512,
                                                        scalar2=None, op0=ALU.mult),
                  reads=[("lns", i)], writes=[("lns", i)])
            P.add("act", lambda e, i=i: e.activation(out=JUNK[:, 0:512], in_=GV[:, i, :], func=AF.Square,
                                                     bias=LNS[:, i, 2:3], accum_out=LNS[:, i, 1:2]),
                  reads=[("gv", i), ("lns", i)], writes=["junk", ("lns", i)])
        lk = [("lns", i) for i in range(8)]
        P.add("dve", lambda e: e.tensor_scalar(out=LNS[:, :, 3], in0=LNS[:, :, 1], scalar1=1.0 / 512, scalar2=EPS,
                                               op0=ALU.mult, op1=ALU.add), reads=lk, writes=lk)
        P.add("act", lambda e: e.activation(out=LNS[:, :, 3], in_=LNS[:, :, 3], func=AF.Sqrt), reads=lk, writes=lk)
        P.add("dve", lambda e: e.reciprocal(out=LNS[:, :, 3], in_=LNS[:, :, 3]), reads=lk, writes=lk)
        for i in range(8):
            P.add("dve", lambda e, i=i: e.tensor_scalar(out=VLN[:, i, :], in0=GV[:, i, :], scalar1=LNS[:, i, 2:3],
                                                        scalar2=LNS[:, i, 3:4], op0=ALU.add, op1=ALU.mult),
                  reads=[("gv", i), ("lns", i)], writes=[("vln", i)])
        wb, kb = ring_load(wtile(w_in, 1024, 1536), [128, 8, 512])
        for i in range(8):
            b = ps1()
            P.add("pe", lambda e, mm=mm, i=i, b=b, w=wb: mm(e, i, b, w), reads=[kb, ("HT", i)], writes=pk(b))
            if i % 2:
                drain_mod(1)
            P.add("act", lambda e, i=i, b=b: e.activation(out=ZB[:, i, :], in_=bank(b), func=AF.Copy),
                  reads=pk(b), writes=[("zb", i)])
        wu, ku = ring_load(wtile(w_in, 0, 512), [128, 8, 512])
        for g in range(4):
            for tt in range(2):
                b = ps1()
                j = st["k"] % 2
                st["k"] += 1

                def mmu(e, g=g, tt=tt, b=b):
                    ins = None
                    for kc in range(8):
                        ins = e.matmul(bank(b), lhsT=wu[:, kc, g * 128:(g + 1) * 128],
                                       rhs=HT[:, kc, tt * 512:(tt + 1) * 512], start=(kc == 0), stop=(kc == 7))
                    return ins
                htk = [("HT", tt * 4 + q) for q in range(4)]
                P.add("pe", mmu, reads=[ku] + htk, writes=pk(b))
                if tt == 0:
                    drain_mod(1)
                P.add("act", lambda e, j=j, b=b: e.activation(out=UG[j][:], in_=bank(b), func=AF.Gelu_apprx_tanh),
                      reads=pk(b), writes=[("ug", j)])
                b2 = ps1()

                def mms(e, g=g, tt=tt, b2=b2):
                    ins = None
                    for n in range(4):
                        ins = e.matmul(bank(b2)[:, n * 128:(n + 1) * 128],
                                       lhsT=VLN[:, tt * 4 + n, g * 128:(g + 1) * 128], rhs=SGWT[:, g, :],
                                       start=True, stop=True)
                    return ins
                P.add("pe", mms, reads=["sgwt"] + [("vln", tt * 4 + n) for n in range(4)], writes=pk(b2))
                P.add("dve", lambda e, g=g, j=j, b2=b2: e.scalar_tensor_tensor(
                    out=SGT[j][:].rearrange("p (n q) -> p n q", n=4),
                    in0=bank(b2).rearrange("p (n q) -> p n q", n=4), scalar=SGG[:, g:g + 1],
                    in1=BIASB[:, g:g + 1, :].broadcast_to([128, 4, 128]), op0=ALU.mult, op1=ALU.add),
                    reads=pk(b2) + ["sgg", "biasb"], writes=[("sgt", j)])
                P.add("dve", lambda e, g=g, tt=tt, j=j: e.tensor_tensor(
                    out=OT[:, g, tt * 512:(tt + 1) * 512], in0=SGT[j][:], in1=UG[j][:], op=ALU.mult),
                    reads=[("sgt", j), ("ug", j)], writes=[("OT", g)])
        fpend = None
        if half == 0:
            for pt in range(2):
                ct, kc_ = ring_load(I["c1024"].rearrange("(qc p) n -> p qc n", p=128)[:, :, pt * 512:(pt + 1) * 512],
                                    [128, 8, 512])
                stt, ks_ = ring_load(I["s1024"].rearrange("(qc p) n -> p qc n", p=128)[:, :, pt * 512:(pt + 1) * 512],
                                     [128, 8, 512])
                for g in range(4):
                    s2 = fourier_group(g, pt * 512, [(0, 512, ct, stt, list(range(8)))], [kc_, ks_])
                    if fpend is not None:
                        fpend()
                    fpend = s2
        else:
            ct, kc_ = ring_load(I["c256"].rearrange("(qc p) n -> p qc n", p=128), [128, 2, 256])
            stt, ks_ = ring_load(I["s256"].rearrange("(qc p) n -> p qc n", p=128), [128, 2, 256])
            for pr in range(2):
                for g in range(4):
                    parts = [(jb * 256, 256, ct, stt, [2 * (2 * pr + jb), 2 * (2 * pr + jb) + 1]) for jb in range(2)]
                    s2 = fourier_group(g, pr * 512, parts, [kc_, ks_])
                    if fpend is not None:
                        fpend()
                    fpend = s2
        fpend()
        out_proj(half, I["ab_w_out"][0], nxt)

    def fourier_group(g, col0, parts, wkeys):
        bc = ps1()
        bs = ps1()
        zk = []

        def mmf(e, which, b):
            ins = None
            first = True
            for (co, w, ct, stt, tiles) in parts:
                m = ct if which == 0 else stt
                for qi, ti in enumerate(tiles):
                    ins = e.matmul(bank(b)[:, co:co + w], lhsT=ZB[:, ti, g * 128:(g + 1) * 128], rhs=m[:, qi, 0:w],
                                   start=(qi == 0 and first), stop=(qi == len(tiles) - 1),
                                   skip_group_check=True)
                first = False
            return ins
        for (_, _, _, _, tiles) in parts:
            zk += [("zb", ti) for ti in tiles]
        P.add("pe", lambda e: mmf(e, 0, bc), reads=wkeys + zk, writes=pk(bc))
        P.add("pe", lambda e: mmf(e, 1, bs), reads=wkeys + zk, writes=pk(bs))
        j = st["k"] % 2
        st["k"] += 1
        P.add("act", lambda e: e.activation(out=CZ[j][:, 0, :], in_=bank(bc), func=AF.Copy),
              reads=pk(bc), writes=[("cz", j, 0)])
        P.add("dve", lambda e: e.tensor_copy(out=CZ[j][:, 1, :], in_=bank(bs)), reads=pk(bs), writes=[("cz", j, 1)])
        def stage2():
            bo = ps1()

            def mmc(e):
                e.matmul(bank(bo), lhsT=CCS[:], rhs=CZ[j][:, 0, :], start=True, stop=False)
                return e.matmul(bank(bo), lhsT=NSCS[:], rhs=CZ[j][:, 1, :], start=False, stop=True)
            P.add("pe", mmc, reads=["ccs", "nscs", ("cz", j, 0), ("cz", j, 1)], writes=pk(bo))
            P.add("act", lambda e: e.activation(out=OT[:, 4 + g, col0:col0 + 512], in_=bank(bo), func=AF.Copy),
                  reads=pk(bo), writes=[("OT", 4 + g)])
        return stage2

    def mixer_attn(half, nxt):
        wqkv = I["attn_w_qkv"][0]
        sample = (half == 0)
        scale = 1.0 / np.sqrt(128.0)
        wkv, kkv = ring_load(wtile(wqkv, 1024, 1536), [128, 8, 512])
        if sample:
            P.add("sp", lambda e: e.dma_start(out=ROPE[:, 0, :], in_=I["cosT"]), writes=["rope"], dma=True)
            P.add("sp", lambda e: e.dma_start(out=ROPE[:, 1, :], in_=I["ssinT"]), writes=["rope"], dma=True)
            P.add("pool", lambda e: e.dma_start(out=CKB[:], in_=I["ck"].rearrange("(t p) c -> p t c", p=128)),
                  writes=["ckb"], dma=True)
            P.add("pool", lambda e: e.dma_start(out=VT[:, 8:10, :], in_=I["cv"].rearrange("(t p) c -> p t c", p=128)),
                  writes=[("vt", 8), ("vt", 9)], dma=True)
            b = ps1()

            def trk(e, b=b):
                ins = None
                for gg in range(2):
                    for t in range(2):
                        ins = e.transpose(out=bankbf(b)[:, (gg * 2 + t) * 128:(gg * 2 + t + 1) * 128],
                                          in_=CKB[:, t, gg * 128:(gg + 1) * 128], identity=IDN[:])
                return ins
            P.add("pe", trk, reads=["ckb", "idn"], writes=pk(b))
            P.add("act", lambda e, b=b: e.activation(
                out=KT[:, :, 1024:1280], in_=bankbf(b)[:, 0:512].rearrange("p (g t) -> p g t", g=2), func=AF.Copy),
                reads=pk(b), writes=[("ktc", 0), ("ktc", 1)])
        for i in range(8):
            b = ps1()

            def mm(e, i=i, b=b):
                ins = None
                for kc in range(8):
                    ins = e.matmul(bank(b), lhsT=HT[:, kc, i * 128:(i + 1) * 128], rhs=wkv[:, kc, :],
                                   start=(kc == 0), stop=(kc == 7))
                return ins
            P.add("pe", mm, reads=[kkv, ("HT", i)], writes=pk(b))
            P.add("act", lambda e, i=i, b=b: e.activation(out=VT[:, i, :], in_=bank(b)[:, 256:512], func=AF.Copy),
                  reads=pk(b), writes=[("vt", i)])
            drain_mod(1)
            if not sample:
                j = i % 2
                P.add("dve", lambda e, j=j, b=b: e.tensor_copy(out=KVST[j][:], in_=bank(b)),
                      reads=pk(b), writes=[("kvst", j)])
                P.add("sp", lambda e, j=j, i=i: e.dma_start(out=O["sk"][i * 128:(i + 1) * 128, :], in_=KVST[j][:, 0:256]),
                      reads=[("kvst", j)], writes=[("o_sk", i)], dma=True)
                P.add("sp", lambda e, j=j, i=i: e.dma_start(out=O["sv"][i * 128:(i + 1) * 128, :], in_=KVST[j][:, 256:512]),
                      reads=[("kvst", j)], writes=[("o_sv", i)], dma=True)
        wkvp = kkvp = None
        if sample:
            s = st["ring"] % 4
            st["ring"] += 1
            kkvp = RingKey(("ring", s))
            kkvp.gen = P.ring_gen[s] = P.ring_gen.get(s, 0) + 1
            wkvp = RING[s][:, 0:2048].rearrange("p (a b) -> p a b", a=8)
            for jj in range(2):
                P.add("dve", lambda e, s=s, jj=jj: e.tensor_copy(
                    out=RING[s][:, 0:2048].rearrange("p (k m j c) -> p k m j c", k=8, j=2, c=32)[:, :, :, jj, :],
                    in_=wkv[:, :, 0:256].rearrange("p k (m j c) -> p k m j c", j=2, c=32)[:, :, :, 1 - jj, :]),
                    reads=[kkv], writes=[kkvp])

        def proj_fm(dst, dkeys, w, wkey, wp, wpkey, c0, tt):
            htk = [("HT", tt * 4 + q) for q in range(4)]
            b = ps1()

            def mmq(e, w=w, b=b):
                ins = None
                for kc in range(8):
                    ins = e.matmul(bank(b), lhsT=w[:, kc, c0:c0 + 128], rhs=HT[:, kc, tt * 512:(tt + 1) * 512],
                                   start=(kc == 0), stop=(kc == 7))
                return ins
            P.add("pe", mmq, reads=[wkey] + htk, writes=pk(b))
            if wp is None:
                P.add("act", lambda e: e.activation(out=dst, in_=bank(b), func=AF.Copy), reads=pk(b), writes=dkeys)
                return
            bp = ps1()
            P.add("pe", lambda e: mmq(e, wp, bp), reads=[wpkey] + htk, writes=pk(bp))
            j = 0
            P.add("dve", lambda e: e.tensor_tensor(out=RT[2 * j][:], in0=bank(b), in1=ROPE[:, 0, tt * 512:(tt + 1) * 512],
                                                   op=ALU.mult), reads=pk(b) + ["rope"], writes=[("rt", 2 * j)])
            P.add("dve", lambda e: e.tensor_tensor(out=RT[2 * j + 1][:], in0=bank(bp),
                                                   in1=ROPE[:, 1, tt * 512:(tt + 1) * 512], op=ALU.mult),
                  reads=pk(bp) + ["rope"], writes=[("rt", 2 * j + 1)])
            P.add("dve", lambda e: e.tensor_tensor(out=dst, in0=RT[2 * j][:], in1=RT[2 * j + 1][:], op=ALU.add),
                  reads=[("rt", 2 * j), ("rt", 2 * j + 1)], writes=dkeys)

        for gg in range(2):
            for tt in range(2):
                proj_fm(KT[:, gg, tt * 512:(tt + 1) * 512], [("kt", gg, tt)], wkv, kkv, wkvp, kkvp, gg * 128, tt)
                drain_mod(1)
        for gg in range(2):
            wq, kq = ring_load(wtile(wqkv, gg * 512, (gg + 1) * 512), [128, 8, 512])
            wqp = kqp = None
            if sample:
                s = st["ring"] % 4
                st["ring"] += 1
                kqp = RingKey(("ring", s))
                kqp.gen = P.ring_gen[s] = P.ring_gen.get(s, 0) + 1
                wqp = RING[s][:, 0:4096].rearrange("p (a b) -> p a b", a=8)
                for jj in range(2):
                    P.add("dve", lambda e, s=s, jj=jj, wq=wq: e.tensor_copy(
                        out=RING[s][:, 0:4096].rearrange("p (k m j c) -> p k m j c", k=8, j=2, c=32)[:, :, :, jj, :],
                        in_=wq.rearrange("p k (m j c) -> p k m j c", j=2, c=32)[:, :, :, 1 - jj, :]),
                        reads=[kq], writes=[kqp])
            for hh in range(4):
                for tt in range(2):
                    proj_fm(QT[:, hh, tt * 512:(tt + 1) * 512], [("qt", hh, tt)], wq, kq, wqp, kqp, hh * 128, tt)
            fin = None
            for hh in range(4):
                h = gg * 4 + hh
                for qt in range(2):
                    fin = attend(sample, gg, hh, h, qt, scale, fin)
            fin()
        out_proj(half, I["attn_w_o"][0], nxt)

    def attend(sample, gg, hh, h, qt, scale, prev_fin):
        a = st.get("att", 0)
        st["att"] = a + 1
        bo = 2 * (a % 2)
        bd = bo + 1

        def ps1():
            b_ = 4 + st.get("bs", 0) % 4
            st["bs"] = st.get("bs", 0) + 1
            return b_
        items = []
        if sample:
            for jc in range(2):
                items.append((KT[:, gg, 1024 + jc * 128:1024 + (jc + 1) * 128], VT[:, 8 + jc, gg * 128:(gg + 1) * 128],
                              0, 512, [("ktc", gg), ("vt", 8 + jc)], []))
            for j in range(max(0, 4 * qt - 1), min(8, 4 * qt + 5)):
                ilo = max(j - 1, 4 * qt)
                ihi = min(j + 1, 4 * qt + 3)
                lo = (ilo - 4 * qt) * 128
                hi = (ihi - 4 * qt + 1) * 128
                ms = []
                if ilo == j - 1:
                    ms.append((0, 0))
                if ihi == j + 1:
                    ms.append((hi - lo - 128, 1))
                items.append((KT[:, gg, j * 128:(j + 1) * 128], VT[:, j, gg * 128:(gg + 1) * 128], lo, hi,
                              [("kt", gg, j // 4), ("vt", j)], ms))
        else:
            for jb in range(2):
                bt = 2 * qt + jb
                for kt_ in range(2):
                    ti = 2 * bt + kt_
                    items.append((KT[:, gg, ti * 128:(ti + 1) * 128], VT[:, ti, gg * 128:(gg + 1) * 128],
                                  jb * 256, jb * 256 + 256, [("kt", gg, ti // 4), ("vt", ti)], []))
        first = True
        pend = []
        for (kap, vap, lo, hi, keys, ms) in items:
            n = hi - lo
            bs = ps1()
            P.add("pe", lambda e, kap=kap, lo=lo, hi=hi, bs=bs: e.matmul(
                bank(bs)[:, 0:hi - lo], lhsT=kap, rhs=QT[:, hh, qt * 512 + lo:qt * 512 + hi], start=True, stop=True),
                reads=[keys[0], ("qt", hh, qt)], writes=pk(bs))
            j = st["k"] % 3
            st["k"] += 1
            ek = [("expb", j)]
            P.add("act", lambda e, j=j, bs=bs, n=n: e.activation(out=EXPB[j][:, 0:n], in_=bank(bs)[:, 0:n],
                                                                 func=AF.Exp, scale=float(scale)),
                  reads=pk(bs), writes=ek)
            for (co, which) in ms:
                P.add("dve", lambda e, j=j, co=co, which=which: e.tensor_tensor(
                    out=EXPB[j][:, co:co + 128], in0=EXPB[j][:, co:co + 128],
                    in1=MASKS[:, which * 128:(which + 1) * 128], op=ALU.mult), reads=ek + ["masks"], writes=ek)

            def pv(e, vap=vap, j=j, lo=lo, hi=hi, first=first):
                e.matmul(bank(bo)[:, lo:hi], lhsT=vap, rhs=EXPB[j][:, 0:hi - lo], start=first, stop=False,
                         skip_group_check=True)
                return e.matmul(bank(bd)[:, lo:hi], lhsT=ONES[:], rhs=EXPB[j][:, 0:hi - lo], start=first, stop=False,
                                skip_group_check=True)
            pend.append((pv, ek + [keys[1], "ones"]))
            if len(pend) > 2:
                pv_, rd_ = pend.pop(0)
                P.add("pe", pv_, reads=rd_, writes=pk(bo) + pk(bd))
                if prev_fin is not None:
                    prev_fin()
                    prev_fin = None
            first = False
        while pend:
            pv_, rd_ = pend.pop(0)
            P.add("pe", pv_, reads=rd_, writes=pk(bo) + pk(bd))
        if prev_fin is not None:
            prev_fin()
        j = st["k"] % 2
        st["k"] += 1
        rk = [("rcp", j)]

        def fin():
            P.add("act", lambda e: e.activation(out=RCP[j][:], in_=bank(bd), func=AF.Ln, bias=SINKE[:, h:h + 1]),
                  reads=pk(bd) + ["sinke"], writes=rk)
            P.add("act", lambda e: e.activation(out=RCP[j][:], in_=RCP[j][:], func=AF.Exp, scale=-1.0),
                  reads=rk, writes=rk)
            P.add("dve", lambda e: e.tensor_tensor(out=OT[:, h, qt * 512:(qt + 1) * 512], in0=bank(bo), in1=RCP[j][:],
                                                   op=ALU.mult), reads=pk(bo) + rk, writes=[("OT", h)])
        return fin

    if phases is None:
        phases = [(l, half, sub) for half in range(2) for l in range(2) for sub in range(2)]
    for idx, (l, half, sub) in enumerate(phases):
        nxt = phases[idx + 1] if idx + 1 < len(phases) else None
        if idx == 0:
            load_x(range(half * 8, half * 8 + 8))
            mod_group_fast(l, sub, after_first=lambda: load_mod_slots01(l, half, sub))
            load_small_consts()
            load_x(range((1 - half) * 8, (1 - half) * 8 + 8))
            pend = []
            for i in range(8):
                pend.append(pre_norm_tile(half, i, "OT", i % 8))
                if len(pend) > 2:
                    pend.pop(0)()
            while pend:
                pend.pop(0)()
        load_mod(l, half, sub, (2,))
        fence()
        if sub == 0:
            if l == 0:
                mixer_ab(half, nxt)
            else:
                mixer_attn(half, nxt)
        else:
            ffn(l, half, nxt)
        st["b1"] = 0
        st["b2"] = 0
        if not any(ph[1] == half for ph in phases[idx + 1:]):
            store_x(half)
    for hf in range(2):
        if not any(ph[1] == hf for ph in phases):
            store_x(hf)
    outk = [("o_y", i) for i in range(16)]
    outk += [("o_sk", i) for i in range(8)] + [("o_sv", i) for i in range(8)]
    P.add("sp", lambda e: e.nop(), reads=outk)

    P.analyze()
    with contextlib.ExitStack() as stack:
        esems = {k: stack.enter_context(nc.semaphore("es_" + k)) for k in Prog.ENGS}
        dsems = [stack.enter_context(nc.semaphore("ds_%d" % i)) for i in range(P.ndsems)]
        block = stack.enter_context(nc.Block())
        P.emit(block, esems, dsems)
    return nc


_CONSTS = None


def make_in_maps(inputs):
    global _CONSTS
    if _CONSTS is None:
        _CONSTS = _consts()
    f = lambda a: np.ascontiguousarray(np.asarray(a, dtype=np.float32))
    shared = {k: f(inputs[k]) for k in ("mod_w", "mod_b", "norm_pre_mix", "norm_post_mix", "norm_pre_ffn",
                                        "norm_post_ffn", "ab_w_in", "sgu_w", "sgu_b", "sgu_g", "ab_w_out",
                                        "attn_w_qkv", "attn_sink", "attn_w_o", "ffn_w_gate", "ffn_w_up",
                                        "ffn_w_down")}
    shared.update(_CONSTS)
    xs, xp = f(inputs["x_sample"]), f(inputs["x_prompt"])
    ck, cv = f(inputs["cache_k"]), f(inputs["cache_v"])
    c, cctx = f(inputs["c"]), f(inputs["c_ctx"])
    maps = []
    for b in range(NCORES):
        m = dict(shared)
        m["xs"] = xs[b]
        m["xp"] = np.ascontiguousarray(xp[4 * b:4 * b + 4].reshape(1024, D))
        m["ck"] = np.ascontiguousarray(ck[b, 0].reshape(256, 256))
        m["cv"] = np.ascontiguousarray(cv[b, 0].reshape(256, 256))
        m["cond"] = np.ascontiguousarray(np.stack([c[b], cctx], axis=0))
        maps.append(m)
    return maps


def gather(results):
    ys = np.stack([r["ys"] for r in results], axis=0).astype(np.float32)
    yp = np.concatenate([r["yp"].reshape(4, 256, D) for r in results], axis=0).astype(np.float32)
    sk = np.concatenate([r["sk"].reshape(4, 1, 256, 2, 128) for r in results], axis=0).astype(np.float32)
    sv = np.concatenate([r["sv"].reshape(4, 1, 256, 2, 128) for r in results], axis=0).astype(np.float32)
    return (yp, ys, sk, sv)


def kernel(**inputs):
    nc = build()
    res = run_bass_kernel_spmd(nc, make_in_maps(inputs), core_ids=list(range(NCORES)))
    return gather(res.results)
```

```python
import contextlib
import numpy as np
import ml_dtypes
import concourse.bass as bass
import concourse.mybir as mybir
from concourse.bass_utils import run_bass_kernel_spmd

F32 = mybir.dt.float32
BF16 = mybir.dt.bfloat16
U8 = mybir.dt.uint8
AF = mybir.ActivationFunctionType
ALU = mybir.AluOpType

D = 1024
DFF = 2816
NFF = DFF // 128
EPS = 1e-6
NCORES = 8


class Op:
    __slots__ = ("eng", "fn", "reads", "writes", "dma", "idx", "waits", "clock",
                 "signal", "sem", "semval", "gidx")

    def __init__(self, eng, fn, reads, writes, dma):
        self.eng = eng
        self.fn = fn
        self.reads = reads
        self.writes = writes
        self.dma = dma
        self.waits = None
        self.clock = None
        self.signal = False
        self.sem = None
        self.semval = None


class RingKey(tuple):
    gen = None


class Prog:
    ENGS = ("pe", "act", "dve", "pool", "sp")
    MAX_DMA_SEMS = 72

    def __init__(self, nc):
        self.nc = nc
        self.ops = []
        self.byeng = {e: [] for e in self.ENGS}
        self.ring_gen = {}

    def add(self, eng, fn, reads=(), writes=(), dma=False):
        for r in reads:
            assert getattr(r, "gen", None) is None or self.ring_gen[r[1]] == r.gen, ("stale ring tile", r)
        op = Op(eng, fn, tuple(reads), tuple(writes), dma)
        op.gidx = len(self.ops)
        op.idx = len(self.byeng[eng])
        self.ops.append(op)
        self.byeng[eng].append(op)
        return op

    def analyze(self):
        last_w = {}
        readers = {}
        clock = {e: {} for e in self.ENGS}
        dsems = []
        for op in self.ops:
            E = op.eng
            deps = {}
            for r in op.reads:
                d = last_w.get(r)
                if d is not None:
                    deps[d.gidx] = (d, True)
                if isinstance(r, tuple) and r[0] == "ps":
                    for d in readers.get(r, ()):
                        if d.eng != E and d.gidx not in deps:
                            deps[d.gidx] = (d, False)
            for w in op.writes:
                d = last_w.get(w)
                if d is not None and d.gidx not in deps:
                    deps[d.gidx] = (d, False)
                for d in readers.get(w, ()):
                    if d.gidx not in deps:
                        deps[d.gidx] = (d, False)
            ck = clock[E]
            need = {}
            for d, raw in deps.values():
                if d is op:
                    continue
                if d.dma:
                    key, val = ("dma", d.sem), d.semval
                else:
                    if d.eng == E and not op.dma and E == "pe":
                        continue
                    key, val = d.eng, d.idx + 1
                if ck.get(key, 0) >= val:
                    continue
                if key not in need or need[key][0] < val:
                    need[key] = (val, d)
            if op.dma:
                chosen = None
                cls = "sw" if E == "pool" else "hw"
                for si, (tot, lop, scls) in enumerate(dsems):
                    if scls != cls:
                        continue
                    k = ("dma", si)
                    have = max(ck.get(k, 0), need[k][0] if k in need else 0)
                    if have >= tot:
                        chosen = si
                        break
                if chosen is None:
                    if len(dsems) < self.MAX_DMA_SEMS:
                        dsems.append([0, None, cls])
                        chosen = len(dsems) - 1
                    else:
                        chosen = min((i for i in range(len(dsems)) if dsems[i][2] == cls),
                                     key=lambda i: dsems[i][1].gidx)
                        tot, lop, _ = dsems[chosen]
                        need[("dma", chosen)] = (tot, lop)
                op.sem = chosen
                dsems[chosen][0] += 16
                dsems[chosen][1] = op
                op.semval = dsems[chosen][0]
                op.signal = True
            if need:
                ck = dict(ck)
                for key, (val, d) in need.items():
                    d.signal = True
                    for k2, v2 in d.clock.items():
                        if ck.get(k2, 0) < v2:
                            ck[k2] = v2
                    if ck.get(key, 0) < val:
                        ck[key] = val
                clock[E] = ck
            op.waits = [(key, d) for key, (val, d) in need.items()]
            op.clock = ck
            for r in op.reads:
                readers.setdefault(r, []).append(op)
            for w in op.writes:
                last_w[w] = op
                readers[w] = []
        self.ndsems = len(dsems)
        for e in self.ENGS:
            n = 0
            for op in self.byeng[e]:
                if op.dma:
                    continue
                if op.signal:
                    n += 1
                    op.semval = n

    def emit(self, block, esems, dsems):
        engobj = {"pe": "tensor", "act": "scalar", "dve": "vector", "pool": "gpsimd", "sp": "sync"}

        def make(ename):
            ops = self.byeng[ename]

            def body(e):
                for op in ops:
                    for key, d in op.waits:
                        if isinstance(key, tuple):
                            e.wait_ge(dsems[key[1]], d.semval)
                        else:
                            e.wait_ge(esems[key], d.semval)
                    ins = op.fn(e)
                    assert ins is not None, (ename, op.reads, op.writes)
                    if op.signal:
                        if op.dma:
                            ins.then_inc(dsems[op.sem], 16)
                        else:
                            ins.then_inc(esems[ename], 1)
            return body

        for ename in self.ENGS:
            getattr(block, engobj[ename])(make(ename))


def _dft(n, scale, kind):
    idx = (np.arange(n)[:, None].astype(np.int64) * np.arange(n)[None, :].astype(np.int64)) % n
    ang = 2.0 * np.pi * idx.astype(np.float64) / n
    m = np.cos(ang) if kind == "c" else np.sin(ang)
    return (m * scale).astype(np.float32)


def _consts():
    bf = ml_dtypes.bfloat16
    c = {}
    c["ident"] = np.eye(128, dtype=np.float32).astype(bf)
    c["ones"] = np.ones((128, 128), dtype=np.float32).astype(bf)
    c["identf"] = np.eye(16, dtype=np.float32)
    sl = np.arange(128)[:, None]
    ql = np.arange(128)[None, :]
    c["masks"] = np.concatenate([(sl <= ql), (ql <= sl)], axis=1).astype(np.float32).astype(bf)
    c["c1024"] = _dft(1024, 1.0 / 32.0, "c").astype(bf)
    c["s1024"] = _dft(1024, 1.0 / 32.0, "s").astype(bf)
    c["c256"] = _dft(256, 1.0 / 16.0, "c").astype(bf)
    c["s256"] = _dft(256, 1.0 / 16.0, "s").astype(bf)
    c["cc"] = _dft(128, 1.0 / np.sqrt(128.0), "c").astype(bf)
    c["nsc"] = (-_dft(128, 1.0 / np.sqrt(128.0), "s")).astype(bf)
    t = 1024
    rows = np.repeat(np.arange(t // 64), 64).astype(np.float32)
    cols = np.tile(np.arange(64), t // 64).astype(np.float32)
    inv = (np.float32(10000.0) ** (-np.arange(0, 64, 2, dtype=np.float32) / np.float32(64))).astype(np.float32)
    ar = rows[:, None] * inv
    ac = cols[:, None] * inv
    ang = np.concatenate([ar, ar, ac, ac], axis=-1).astype(np.float32)
    sign = np.where((np.arange(128) % 64) < 32, -1.0, 1.0).astype(np.float32)
    c["cosT"] = np.ascontiguousarray(np.cos(ang).T.astype(np.float32))
    c["ssinT"] = np.ascontiguousarray((np.sin(ang) * sign[None, :]).T.astype(np.float32))
    return c


_IN_SHAPES = {
    "xs": ([1024, D], F32), "xp": ([1024, D], F32), "ck": ([256, 256], F32), "cv": ([256, 256], F32),
    "cond": ([2, D], F32),
    "mod_w": ([2, D, 6 * D], F32), "mod_b": ([2, 6 * D], F32),
    "norm_pre_mix": ([2, D], F32), "norm_post_mix": ([2, D], F32),
    "norm_pre_ffn": ([2, D], F32), "norm_post_ffn": ([2, D], F32),
    "ab_w_in": ([1, D, 1536], F32), "sgu_w": ([1, 4, 128, 128], F32), "sgu_b": ([1, 4, 128], F32),
    "sgu_g": ([1, 512], F32), "ab_w_out": ([1, D, D], F32),
    "attn_w_qkv": ([1, D, 1536], F32), "attn_sink": ([1, 8], F32), "attn_w_o": ([1, D, D], F32),
    "ffn_w_gate": ([2, D, DFF], F32), "ffn_w_up": ([2, D, DFF], F32), "ffn_w_down": ([2, DFF, D], F32),
    "ident": ([128, 128], BF16), "identf": ([16, 16], F32), "ones": ([128, 128], BF16), "masks": ([128, 256], BF16),
    "c1024": ([1024, 1024], BF16), "s1024": ([1024, 1024], BF16),
    "c256": ([256, 256], BF16), "s256": ([256, 256], BF16),
    "cc": ([128, 128], BF16), "nsc": ([128, 128], BF16),
    "cosT": ([128, 1024], F32), "ssinT": ([128, 1024], F32),
}
_OUT_SHAPES = {"ys": [1024, D], "yp": [1024, D], "sk": [1024, 256], "sv": [1024, 256]}


def build(phases=None):
    nc = bass.Bass("TRN2", target_bir_lowering=False)
    I = {n: nc.dram_tensor(n, s, dt, kind="ExternalInput").ap() for n, (s, dt) in _IN_SHAPES.items()}
    O = {n: nc.dram_tensor(n, s, F32, kind="ExternalOutput").ap() for n, s in _OUT_SHAPES.items()}
    MSCR = nc.dram_tensor("mscr", [2, 2, 6 * D], F32).ap()

    rem = nc.sbuf_bytes_remaining
    arena = nc.alloc_sbuf_tensor("arena", [128, rem - 2048], U8)
    base = nc.lookup_mloc(arena).addr
    limit = base + rem - 2048
    cur = [base]

    def alloc(name, shape, dt, at=None):
        nbytes = int(np.prod(shape[1:])) * (4 if dt == F32 else 2)
        if at is None:
            at = cur[0]
            cur[0] += (nbytes + 31) // 32 * 32
            assert cur[0] <= limit, (name, cur[0], limit)
        return nc.alloc_sbuf_tensor_at(name, list(shape), dt, offset=at)

    X = alloc("X", [128, 16, D], F32)
    HT = alloc("HT", [128, 8, 1024], BF16)
    OT_off = cur[0]
    OT = alloc("OT", [128, 8, 1024], BF16)
    HID_off = cur[0]
    HIDB = 45056
    HID = alloc("HID", [128, NFF, 1024], BF16)
    cur[0] = HID_off + HIDB
    cur_ring0 = cur[0]
    RING = [alloc("ring%d" % i, [128, 4096], BF16) for i in range(4)]
    MOD = [alloc("mod%d" % i, [128, D], F32) for i in range(3)]
    IDN = alloc("idn", [128, 128], BF16)
    ONES = alloc("ones_sb", [128, 128], BF16)
    MASKS = alloc("masks_sb", [128, 256], BF16)
    JUNK = alloc("junk", [128, D], BF16)
    SS = alloc("ss", [128, 8], F32)
    RS = alloc("rs", [128, 8], F32)
    SS2 = alloc("ss2", [128, 2], F32)
    RS2 = alloc("rs2", [128, 2], F32)
    SCT0 = alloc("sct0", [128, 8, 2], F32)
    C16 = alloc("c16", [16, 128], F32)
    IDF = alloc("idf", [16, 16], F32)
    SCT = alloc("sct", [128, 8, 2], BF16)
    SINKE = alloc("sinke", [128, 8], F32)
    SGG = alloc("sgg", [128, 4], F32)
    BIASB = alloc("biasb", [128, 4, 128], F32)
    SGWT = alloc("sgwt", [128, 4, 128], BF16)
    CCS = alloc("ccs", [128, 128], BF16)
    NSCS = alloc("nscs", [128, 128], BF16)
    LNS = alloc("lns", [128, 8, 4], F32)
    MRW = [alloc("mrw%d" % j, [2, 3, 256], F32) for j in range(2)]
    MW = [alloc("mw%d" % j, [128, 8, 256], BF16) for j in range(2)]
    NSC = {}
    for kind, off0 in (("OT", OT_off), ("HID", HID_off)):
        NSC[kind] = dict(
            tmp=[alloc("tmp%s%d" % (kind, j), [128, D], F32, at=off0 + 4096 * j) for j in range(2)],
            hb=[alloc("hb%s%d" % (kind, j), [128, D], BF16, at=off0 + 8192 + 2048 * j) for j in range(4)],
            tk=[[(kind, 2 * j), (kind, 2 * j + 1)] for j in range(2)],
            hk=[[(kind, 4 + j)] for j in range(4)])
    SG = [alloc("sg%d" % j, [128, 512], F32, at=OT_off + 12288 + 2048 * j) for j in range(2)]
    VLN = alloc("vln", [128, 8, 512], BF16, at=HID_off)
    ZB = alloc("zb", [128, 8, 512], BF16, at=HID_off + 8192)
    GV = alloc("gv", [128, 8, 512], F32, at=HID_off + 16384)
    UG = [alloc("ug%d" % j, [128, 512], F32, at=HID_off + 32768 + 2048 * j) for j in range(2)]
    CZ = [alloc("cz%d" % j, [128, 2, 512], BF16, at=HID_off + 36864 + 2048 * j) for j in range(2)]
    SGT = [alloc("sgt%d" % j, [128, 512], F32, at=HID_off + 40960 + 2048 * j) for j in range(2)]
    QT = alloc("qt", [128, 4, 1024], BF16, at=HID_off)
    KT = alloc("kt", [128, 2, 1280], BF16, at=HID_off + 8192)
    VT = alloc("vt", [128, 10, 256], BF16, at=HID_off + 13312)
    ROPE = alloc("rope", [128, 2, 1024], F32, at=HID_off + 18432)
    EXPB = [alloc("expb%d" % j, [128, 512], BF16, at=HID_off + 26624 + 1024 * j) for j in range(3)]
    RCP = [alloc("rcp%d" % j, [128, 512], F32, at=HID_off + 29696 + 2048 * j) for j in range(2)]
    KVST = [alloc("kvst%d" % j, [128, 512], F32, at=HID_off + 33792 + 2048 * j) for j in range(2)]
    RT = [alloc("rt%d" % j, [128, 512], F32, at=HID_off + 37888 + 2048 * j) for j in range(2)]
    CKB = alloc("ckb", [128, 2, 256], BF16, at=HID_off + 41984)
    print('SBUF free bytes/partition:', limit - cur[0])
    PSUM = nc.alloc_psum_tensor("psum", [128, 4096], F32)

    def bank(b, n=1):
        return PSUM[:, b * 512:(b + n) * 512]

    def bankbf(b):
        return PSUM[:, b * 512:(b + 1) * 512].bitcast(BF16)

    EPSB = alloc("epsb", [128, 1], F32)
    P = Prog(nc)
    st = {"ring": 0, "b1": 0, "b2": 0, "k": 0, "mk": 0}
    FENCE = alloc("fence", [128, 1], F32)
    hidk = [("HID", c) for c in range(NFF)] + ["hidx"]
    REGK = list(hidk)
    REGK += [(n, i) for n in ("vln", "zb", "gv") for i in range(8)]
    REGK += [(n, j) for n in ("ug", "sgt", "rcp", "kvst") for j in range(2)]
    REGK += [("cz", j, c) for j in range(2) for c in range(2)]
    REGK += [("qt", h, t) for h in range(4) for t in range(2)] + [("kt", g, t) for g in range(2) for t in range(2)]
    REGK += [("ktc", g) for g in range(2)] + [("vt", i) for i in range(10)] + ["rope", "ckb"]
    REGK += [("expb", j) for j in range(3)] + [("rt", j) for j in range(2)]

    def fence():
        P.add("dve", lambda e: e.memset(FENCE[:], 0.0), writes=REGK)

    def ps1():
        b = st["b1"] % 8
        st["b1"] += 1
        return b

    def ps2():
        b = (st["b2"] % 4) * 2
        st["b2"] += 1
        return b

    def pk(b, n=1):
        return [("ps", b + i) for i in range(n)]

    def ring_load(src, shape, eng="pool"):
        s = st["ring"] % 4
        st["ring"] += 1
        n = int(np.prod(shape[1:]))
        assert n <= 4096
        if len(shape) == 3:
            dst = RING[s][:, 0:n].rearrange("p (a b) -> p a b", a=shape[1])
        else:
            dst = RING[s][:, 0:n]
        key = RingKey(("ring", s))
        key.gen = P.ring_gen[s] = P.ring_gen.get(s, 0) + 1
        P.add(eng, lambda e, dst=dst, src=src: e.dma_start(out=dst, in_=src), writes=[key], dma=True)
        return dst, key

    def wtile(w2d, c0, c1, r0=0, nk=8):
        return w2d[r0:r0 + nk * 128, :].rearrange("(kc p) n -> p kc n", p=128)[:, :, c0:c1]

    P.add("act", lambda e: e.dma_start(out=C16[:], in_=I["cond"].rearrange("c (kc p) -> (c kc) p", p=128)),
          writes=["c16"], dma=True)
    P.add("act", lambda e: e.dma_start(out=IDF[:], in_=I["identf"]), writes=["idf"], dma=True)
    P.add("sp", lambda e: e.dma_start(out=IDN[:], in_=I["ident"]), writes=["idn"], dma=True)
    P.add("sp", lambda e: e.dma_start(out=ONES[:], in_=I["ones"]), writes=["ones"], dma=True)
    P.add("sp", lambda e: e.dma_start(out=MASKS[:], in_=I["masks"]), writes=["masks"], dma=True)
    P.add("sp", lambda e: e.dma_start(out=CCS[:], in_=I["cc"]), writes=["ccs"], dma=True)
    P.add("sp", lambda e: e.dma_start(out=NSCS[:], in_=I["nsc"]), writes=["nscs"], dma=True)
    P.add("sp", lambda e: e.dma_start(out=SINKE[:], in_=I["attn_sink"][0:1, :].broadcast_to([128, 8])),
          writes=["sinke"], dma=True)

    def load_small_consts():
        P.add("sp", lambda e: e.dma_start(out=SGG[:], in_=I["sgu_g"][0].rearrange("(g p) -> p g", p=128),
                                          allow_slow_non_contiguous=True), writes=["sgg"], dma=True)
        P.add("sp", lambda e: e.dma_start(
            out=BIASB[:], in_=I["sgu_b"][0:1].broadcast_to([128, 4, 128])), writes=["biasb"], dma=True)
    P.add("dve", lambda e: e.memset(EPSB[:], EPS), writes=["epsb"])

    def store_x(half):
        for i in range(half * 8, half * 8 + 8):
            dst = O["ys"] if i < 8 else O["yp"]
            r = (i % 8) * 128
            P.add("sp", lambda e, i=i, dst=dst, r=r: e.dma_start(out=dst[r:r + 128, :], in_=X[:, i, :]),
                  reads=[("X", i)], writes=[("o_y", i)], dma=True)

    def load_x(tiles):
        for i in tiles:
            src = I["xs"] if i < 8 else I["xp"]
            r = (i % 8) * 128
            P.add("sp", lambda e, i=i, src=src, r=r: e.dma_start(out=X[:, i, :], in_=src[r:r + 128, :]),
                  writes=[("X", i)], dma=True)

    P.add("act", lambda e: e.activation(out=SINKE[:], in_=SINKE[:], func=AF.Exp), reads=["sinke"], writes=["sinke"])
    P.add("pe", lambda e: e.transpose(out=PSUM[:, 0:16], in_=C16[:], identity=IDF[:]), reads=["c16", "idf"],
          writes=[("ps", 0)])
    P.add("act", lambda e: e.activation(out=SCT[:].rearrange("p kc c -> p c kc"),
                                        in_=PSUM[:, 0:16].rearrange("p (c kc) -> p c kc", c=2), func=AF.Silu),
          reads=[("ps", 0)], writes=["sct"])

    norm_of_chunk = {1: "norm_pre_mix", 2: "norm_post_mix", 4: "norm_pre_ffn", 5: "norm_post_ffn"}
    allk = [("OT", c) for c in range(8)]
    MBW = 256
    NMB = 6 * D // MBW
    mod_pending = [(l, cb) for l in range(2) for cb in range(NMB)]

    def mod_block(l, cb):
        j = st["mk"] % 2
        st["mk"] += 1
        mk = [("mrow", j)]
        wk = ("mw", j)
        wt = MW[j]
        c0 = cb * MBW
        P.add("pool", lambda e: e.dma_start(out=wt[:], in_=wtile(I["mod_w"][l], c0, c0 + MBW)), writes=[wk], dma=True)
        chunk, hc = c0 // D, (c0 % D)
        P.add("sp", lambda e: e.dma_start(out=MRW[j][:, 0, :], in_=I["mod_b"][l:l + 1, c0:c0 + MBW]
                                          .broadcast_to([2, MBW])), writes=mk, dma=True)
        if chunk in norm_of_chunk:
            P.add("sp", lambda e: e.dma_start(
                out=MRW[j][:, 1, :], in_=I[norm_of_chunk[chunk]][l:l + 1, hc:hc + MBW].broadcast_to([2, MBW])),
                writes=mk, dma=True)
        b = ps1()

        def mm(e):
            ins = None
            for kc in range(8):
                ins = e.matmul(bank(b)[0:2, 0:MBW], lhsT=SCT[:, kc, :], rhs=wt[:, kc, :],
                               start=(kc == 0), stop=(kc == 7))
            return ins
        P.add("pe", mm, reads=[wk, "sct"], writes=pk(b))
        P.add("dve", lambda e: e.tensor_tensor(out=MRW[j][:, 2, :], in0=bank(b)[0:2, 0:MBW], in1=MRW[j][:, 0, :],
                                               op=ALU.add), reads=pk(b) + mk, writes=mk)
        if chunk in (1, 4):
            P.add("dve", lambda e: e.scalar_tensor_tensor(
                out=MRW[j][:, 2, :], in0=MRW[j][:, 2, :], scalar=1.0, in1=MRW[j][:, 1, :],
                op0=ALU.add, op1=ALU.mult), reads=mk, writes=mk)
        elif chunk in (2, 5):
            P.add("dve", lambda e: e.tensor_tensor(out=MRW[j][:, 2, :], in0=MRW[j][:, 2, :], in1=MRW[j][:, 1, :],
                                                   op=ALU.mult), reads=mk, writes=mk)
        P.add("sp", lambda e: e.dma_start(out=MSCR[l, :, c0:c0 + MBW], in_=MRW[j][:, 2, :]),
              reads=mk, writes=[("mscr", l, cb)], dma=True)

    ROWS = alloc("rows", [2, 4096], F32, at=cur_ring0)
    GROW = alloc("grow", [2, 2048], F32, at=cur_ring0 + 16384)
    MWX = [alloc("mwx%d" % j, [128, 8, 256], BF16, at=cur_ring0 + 24576 + 4096 * j) for j in range(2)]

    def mod_group_fast(l, sub, after_first=None):
        rk = [RingKey(("ring", q)) for q in range(4)]
        for q in range(4):
            rk[q].gen = P.ring_gen[q] = P.ring_gen.get(q, 0) + 1
        st["ring"] = 4
        nblk = [0]
        r0 = sub * 3 * D
        P.add("sp", lambda e: e.dma_start(out=ROWS[:, 0:3 * D], in_=I["mod_b"][l:l + 1, r0:r0 + 3 * D]
                                          .broadcast_to([2, 3 * D])), writes=rk[0:2], dma=True)
        for gi, nm in enumerate(("norm_pre_mix", "norm_post_mix") if sub == 0 else ("norm_pre_ffn", "norm_post_ffn")):
            P.add("sp", lambda e, gi=gi, nm=nm: e.dma_start(
                out=GROW[:, gi * D:(gi + 1) * D], in_=I[nm][l:l + 1, :].broadcast_to([2, D])),
                writes=[rk[2]], dma=True)
        blocks = [(pl, pcb) for (pl, pcb) in mod_pending if pl == l and pcb // (NMB // 2) == sub]
        for blk in blocks:
            mod_pending.remove(blk)
            cb = blk[1]
            j = nblk[0] % 4
            nblk[0] += 1
            if j < 2:
                wk, wt = ("mw", j), MW[j]
            else:
                wk, wt = ("mwx", j), MWX[j - 2]
            c0 = cb * MBW
            lc = c0 - r0
            P.add("pool", lambda e, wt=wt, c0=c0: e.dma_start(out=wt[:], in_=wtile(I["mod_w"][l], c0, c0 + MBW)),
                  writes=[wk], dma=True)
            b = ps1()

            def mm(e, wt=wt, b=b):
                ins = None
                for kc in range(8):
                    ins = e.matmul(bank(b)[0:2, 0:MBW], lhsT=SCT[:, kc, :], rhs=wt[:, kc, :],
                                   start=(kc == 0), stop=(kc == 7))
                return ins
            P.add("pe", mm, reads=[wk, "sct"], writes=pk(b))
            P.add("dve", lambda e, b=b, lc=lc: e.tensor_tensor(
                out=ROWS[:, lc:lc + MBW], in0=bank(b)[0:2, 0:MBW], in1=ROWS[:, lc:lc + MBW], op=ALU.add),
                reads=pk(b) + rk[0:2], writes=rk[0:2])
            chunk = lc // D
            hc = lc % D
            if chunk == 1:
                P.add("dve", lambda e, lc=lc, hc=hc: e.scalar_tensor_tensor(
                    out=ROWS[:, lc:lc + MBW], in0=ROWS[:, lc:lc + MBW], scalar=1.0, in1=GROW[:, hc:hc + MBW],
                    op0=ALU.add, op1=ALU.mult), reads=rk, writes=rk[0:2])
            elif chunk == 2:
                P.add("dve", lambda e, lc=lc, hc=hc: e.tensor_tensor(
                    out=ROWS[:, lc:lc + MBW], in0=ROWS[:, lc:lc + MBW], in1=GROW[:, D + hc:D + hc + MBW],
                    op=ALU.mult), reads=rk, writes=rk[0:2])
            if len(blocks) == 12 and blk is blocks[7]:
                P.add("sp", lambda e: e.dma_start(out=MSCR[l, :, r0:r0 + 2 * D], in_=ROWS[:, 0:2 * D]),
                      reads=rk[0:2], writes=[("mscr", l, bb[1]) for bb in blocks[0:8]], dma=True)
                if after_first is not None:
                    after_first()
        if len(blocks) == 12:
            P.add("sp", lambda e: e.dma_start(out=MSCR[l, :, r0 + 2 * D:r0 + 3 * D], in_=ROWS[:, 2 * D:3 * D]),
                  reads=rk[0:2], writes=[("mscr", l, bb[1]) for bb in blocks[8:12]], dma=True)
        else:
            P.add("sp", lambda e: e.dma_start(out=MSCR[l, :, r0:r0 + 3 * D], in_=ROWS[:, 0:3 * D]),
                  reads=rk[0:2], writes=[("mscr", l, bb[1]) for bb in blocks], dma=True)
        P.add("dve", lambda e: e.memset(FENCE[:], 0.0), writes=[("mwx", 2), ("mwx", 3), rk[3]])

    def drain_mod(n):
        for _ in range(n):
            if mod_pending:
                mod_block(*mod_pending.pop(0))

    def ensure_mod(l, sub):
        while any(pl == l and (pcb // (NMB // 2)) == sub for (pl, pcb) in mod_pending):
            mod_block(*mod_pending.pop(0))

    wt, wk = ring_load(I["sgu_w"][0].rearrange("g p q -> p g q"), [128, 4, 128])
    b = ps1()

    def sgw_t(e, wt=wt, b=b):
        ins = None
        for g in range(4):
            ins = e.transpose(out=bankbf(b)[:, g * 128:(g + 1) * 128], in_=wt[:, g, :], identity=IDN[:])
        return ins
    P.add("pe", sgw_t, reads=[wk, "idn"], writes=pk(b))
    P.add("dve", lambda e, b=b: e.tensor_copy(out=SGWT[:].rearrange("p g q -> p (g q)"), in_=bankbf(b)[:, 0:512]),
          reads=pk(b), writes=["sgwt"])

    fence()

    def load_mod_slots01(l, half, sub):
        load_mod(l, half, sub, (0, 1), ensure=False)

    def load_mod(l, half, sub, slots=(0, 1, 2), ensure=True):
        if ensure:
            ensure_mod(l, sub)
        cond = 0 if half == 0 else 1
        base_v = 0 if sub == 0 else 3
        for slot, v in ((0, base_v + 1), (1, base_v + 0), (2, base_v + 2)):
            if slot not in slots:
                continue
            P.add("sp", lambda e, slot=slot, v=v, l=l, cond=cond: e.dma_start(
                out=MOD[slot][:], in_=MSCR[l, cond:cond + 1, v * D:(v + 1) * D].broadcast_to([128, D])),
                reads=[("mscr", l, 4 * v + q) for q in range(4)], writes=[("mod", slot)], dma=True)

    def rstd_cols(ss, rs, n, keys_r, keys_w):
        P.add("dve", lambda e: e.tensor_scalar(out=rs[:, 0:n], in0=ss[:, 0:n], scalar1=1.0 / D, scalar2=EPS,
                                               op0=ALU.mult, op1=ALU.add), reads=keys_r, writes=keys_w)
        P.add("act", lambda e: e.activation(out=rs[:, 0:n], in_=rs[:, 0:n], func=AF.Sqrt), reads=keys_w, writes=keys_w)
        P.add("dve", lambda e: e.reciprocal(out=rs[:, 0:n], in_=rs[:, 0:n]), reads=keys_w, writes=keys_w)

    def pre_norm_tile(xh, hi, kind, bank_t):
        sc = NSC[kind]
        xi = xh * 8 + hi
        q = hi % 4
        j = hi % 2
        sk, rk = ("ss", hi), ("rs", hi)
        P.add("act", lambda e: e.activation(out=JUNK[:], in_=X[:, xi, :], func=AF.Square,
                                            accum_out=SS[:, hi:hi + 1]), reads=[("X", xi)], writes=["junk", sk])
        P.add("act", lambda e: e.activation(out=RS[:, hi:hi + 1], in_=SS[:, hi:hi + 1], func=AF.Sqrt,
                                            bias=EPSB[:, 0:1], scale=1.0 / D), reads=[sk, "epsb"], writes=[rk])
        P.add("dve", lambda e: e.tensor_tensor(out=sc["tmp"][j][:], in0=X[:, xi, :], in1=MOD[0][:], op=ALU.mult),
              reads=[("X", xi), ("mod", 0)], writes=sc["tk"][j])
        P.add("dve", lambda e: e.reciprocal(out=RS[:, hi:hi + 1], in_=RS[:, hi:hi + 1]), reads=[rk], writes=[rk])
        P.add("dve", lambda e: e.scalar_tensor_tensor(
            out=sc["hb"][q][:], in0=sc["tmp"][j][:], scalar=RS[:, hi:hi + 1], in1=MOD[1][:],
            op0=ALU.mult, op1=ALU.add), reads=sc["tk"][j] + [rk, ("mod", 1)], writes=sc["hk"][q])

        def tail():
            b = bank_t

            def tr(e):
                ins = None
                for kc in range(8):
                    ins = e.transpose(out=bankbf(b)[:, kc * 128:(kc + 1) * 128],
                                      in_=sc["hb"][q][:, kc * 128:(kc + 1) * 128], identity=IDN[:])
                return ins
            P.add("pe", tr, reads=sc["hk"][q] + ["idn"], writes=pk(b))
            P.add("act", lambda e: e.activation(
                out=HT[:, :, hi * 128:(hi + 1) * 128], in_=bankbf(b).rearrange("p (kc t) -> p kc t", kc=8),
                func=AF.Copy), reads=pk(b), writes=[("HT", hi)])
        return tail

    def post_norm(half, i, b):
        t0 = half * 8
        j = i % 2
        P.add("act", lambda e: e.activation(out=JUNK[:], in_=bank(b, 2), func=AF.Square,
                                            accum_out=SS2[:, j:j + 1]), reads=pk(b, 2), writes=["junk", ("ss2", j)])
        P.add("act", lambda e: e.activation(out=RS2[:, j:j + 1], in_=SS2[:, j:j + 1], func=AF.Sqrt,
                                            bias=EPSB[:, 0:1], scale=1.0 / D),
              reads=[("ss2", j), "epsb"], writes=[("rs2", j)])
        P.add("dve", lambda e: e.tensor_tensor(out=bank(b, 2), in0=bank(b, 2), in1=MOD[2][:], op=ALU.mult),
              reads=pk(b, 2) + [("mod", 2)], writes=pk(b, 2))
        P.add("dve", lambda e: e.reciprocal(out=RS2[:, j:j + 1], in_=RS2[:, j:j + 1]),
              reads=[("rs2", j)], writes=[("rs2", j)])
        P.add("dve", lambda e: e.scalar_tensor_tensor(
            out=X[:, t0 + i, :], in0=bank(b, 2), scalar=RS2[:, j:j + 1], in1=X[:, t0 + i, :],
            op0=ALU.mult, op1=ALU.add), reads=pk(b, 2) + [("rs2", j), ("X", t0 + i)], writes=[("X", t0 + i)])

    def out_proj(half, w2d, nxt):
        w0, k0 = ring_load(wtile(w2d, 0, 512), [128, 8, 512])
        w1, k1 = ring_load(wtile(w2d, 512, 1024), [128, 8, 512])
        if nxt is not None:
            load_mod(nxt[0], nxt[1], nxt[2], (0, 1))
            fence()
        pend = []
        for i in range(8):
            b = 2 * (i % 3)

            def mm(e, i=i, b=b):
                ins = None
                for hcol, w in ((0, w0), (1, w1)):
                    for c in range(8):
                        ins = e.matmul(bank(b + hcol), lhsT=OT[:, c, i * 128:(i + 1) * 128], rhs=w[:, c, :],
                                       start=(c == 0), stop=(c == 7))
                return ins
            P.add("pe", mm, reads=[k0, k1] + allk, writes=pk(b, 2))
            if len(pend) > 1:
                pend.pop(0)()
            post_norm(half, i, b)
            if nxt is not None and i >= 4:
                pend.append(pre_norm_tile(nxt[1], i - 4, "HID", 6 + (i % 2)))
        if nxt is not None:
            for i in range(4, 8):
                pend.append(pre_norm_tile(nxt[1], i, "HID", 6 + (i % 2)))
                pend.pop(0)()
        while pend:
            pend.pop(0)()

    def ffn(l, half, nxt):
        wg2, wu2, wd2 = I["ffn_w_gate"][l], I["ffn_w_up"][l], I["ffn_w_down"][l]
        groups = [(g * 4, min(4, NFF - g * 4)) for g in range((NFF + 3) // 4)]
        dgroups = [(0, 2)] + [(2 + g * 4, 4) for g in range(5)]
        for (f0, nf) in groups:
            wg, kg = ring_load(wtile(wg2, f0 * 128, (f0 + nf) * 128), [128, 8, nf * 128])
            wu, ku = ring_load(wtile(wu2, f0 * 128, (f0 + nf) * 128), [128, 8, nf * 128])
            for tt in range(2):
                drain_mod(2)
                for f in range(nf):
                    bg = ps1()
                    bu = ps1()

                    def mm(e, w=wg, f=f, tt=tt, b=bg):
                        ins = None
                        for kc in range(8):
                            ins = e.matmul(bank(b), lhsT=w[:, kc, f * 128:(f + 1) * 128],
                                           rhs=HT[:, kc, tt * 512:(tt + 1) * 512], start=(kc == 0), stop=(kc == 7))
                        return ins
                    htk = [("HT", tt * 4 + q) for q in range(4)]
                    P.add("pe", mm, reads=[kg] + htk, writes=pk(bg))
                    P.add("pe", lambda e, mm=mm, w=wu, f=f, tt=tt, b=bu: mm(e, w, f, tt, b), reads=[ku] + htk, writes=pk(bu))
                    j = st["k"] % 2
                    st["k"] += 1
                    sgk = [("OT", 6 + j)]
                    P.add("act", lambda e, j=j, b=bg: e.activation(out=SG[j][:], in_=bank(b), func=AF.Silu),
                          reads=pk(bg), writes=sgk)
                    P.add("dve", lambda e, j=j, b=bu, ff=f0 + f, tt=tt: e.tensor_tensor(
                        out=HID[:, ff, tt * 512:(tt + 1) * 512], in0=SG[j][:], in1=bank(b), op=ALU.mult),
                        reads=pk(bu) + sgk, writes=[("HID", f0 + f)])
        if nxt is not None:
            load_mod(nxt[0], nxt[1], nxt[2], (0, 1))
        tails = []
        NTM = 2
        for ps in range(2):
            def load_g(gi):
                f0, nf = dgroups[gi]
                wd, kd = ring_load(wd2[f0 * 128:(f0 + nf) * 128, :].rearrange("(c p) n -> p c n", p=128),
                                   [128, nf, 1024])
                return (gi, f0, nf, wd, kd)

            def mm_op(gi, f0, nf, wd, kd, ii):
                i = ps * 4 + ii

                def mm(e, first=(gi == 0), last=(gi == len(dgroups) - 1)):
                    ins = None
                    for c in range(nf):
                        for hcol in range(2):
                            ins = e.matmul(bank(2 * ii + hcol), lhsT=HID[:, f0 + c, i * 128:(i + 1) * 128],
                                           rhs=wd[:, c, hcol * 512:(hcol + 1) * 512],
                                           start=(first and c == 0), stop=(last and c == nf - 1))
                    return ins
                P.add("pe", mm, reads=[kd] + [("HID", f0 + c) for c in range(nf)], writes=pk(2 * ii, 2))
            for gi in range(len(dgroups) - NTM):
                g = load_g(gi)
                for ii in range(4):
                    mm_op(*g, ii)
            tm = [load_g(gi) for gi in range(len(dgroups) - NTM, len(dgroups))]
            for ii in range(4):
                for g in tm:
                    mm_op(*g, ii)
                post_norm(half, ps * 4 + ii, 2 * ii)
            while tails:
                tails.pop(0)()
            st["b1"] = 0
            st["b2"] = 0
            if nxt is not None:
                for q in range(4):
                    tails.append(pre_norm_tile(nxt[1], ps * 4 + q, "OT", 2 * q))
        while tails:
            tails.pop(0)()

    def mixer_ab(half, nxt):
        w_in = I["ab_w_in"][0]
        wv, kv = ring_load(wtile(w_in, 512, 1024), [128, 8, 512])
        for i in range(8):
            b = ps1()

            def mm(e, i=i, b=b, w=wv):
                ins = None
                for kc in range(8):
                    ins = e.matmul(bank(b), lhsT=HT[:, kc, i * 128:(i + 1) * 128], rhs=w[:, kc, :],
                                   start=(kc == 0), stop=(kc == 7))
                return ins
            P.add("pe", mm, reads=[kv, ("HT", i)], writes=pk(b))
            if i % 2:
                drain_mod(1)
            P.add("act", lambda e, i=i, b=b: e.activation(out=GV[:, i, :], in_=bank(b), func=AF.Gelu_apprx_tanh,
                                                          accum_out=LNS[:, i, 0:1]),
                  reads=pk(b), writes=[("gv", i), ("lns", i)])
            P.add("dve", lambda e, i=i: e.tensor_scalar(out=LNS[:, i, 2:3], in0=LNS[:, i, 0:1], scalar1=-1.0 / 512,
                                                        scalar2=None, op0=ALU.mult),
                  reads=[("lns", i)], writes=[("lns", i)])
            P.add("act", lambda e, i=i: e.activation(out=JUNK[:, 0:512], in_=GV[:, i, :], func=AF.Square,
                                                     bias=LNS[:, i, 2:3], accum_out=LNS[:, i, 1:2]),
                  reads=[("gv", i), ("lns", i)], writes=["junk", ("lns", i)])
        lk = [("lns", i) for i in range(8)]
        P.add("dve", lambda e: e.tensor_scalar(out=LNS[:, :, 3], in0=LNS[:, :, 1], scalar1=1.0 / 512, scalar2=EPS,
                                               op0=ALU.mult, op1=ALU.add), reads=lk, writes=lk)
        P.add("act", lambda e: e.activation(out=LNS[:, :, 3], in_=LNS[:, :, 3], func=AF.Sqrt), reads=lk, writes=lk)
        P.add("dve", lambda e: e.reciprocal(out=LNS[:, :, 3], in_=LNS[:, :, 3]), reads=lk, writes=lk)
        for i in range(8):
            P.add("dve", lambda e, i=i: e.tensor_scalar(out=VLN[:, i, :], in0=GV[:, i, :], scalar1=LNS[:, i, 2:3],
                                                        scalar2=LNS[:, i, 3:4], op0=ALU.add, op1=ALU.mult),
                  reads=[("gv", i), ("lns", i)], writes=[("vln", i)])
        wb, kb = ring_load(wtile(w_in, 1024, 1536), [128, 8, 512])
        for i in range(8):
            b = ps1()
            P.add("pe", lambda e, mm=mm, i=i, b=b, w=wb: mm(e, i, b, w), reads=[kb, ("HT", i)], writes=pk(b))
            if i % 2:
                drain_mod(1)
            P.add("act", lambda e, i=i, b=b: e.activation(out=ZB[:, i, :], in_=bank(b), func=AF.Copy),
                  reads=pk(b), writes=[("zb", i)])
        wu, ku = ring_load(wtile(w_in, 0, 512), [128, 8, 512])
        for g in range(4):
            for tt in range(2):
                b = ps1()
                j = st["k"] % 2
                st["k"] += 1

                def mmu(e, g=g, tt=tt, b=b):
                    ins = None
                    for kc in range(8):
                        ins = e.matmul(bank(b), lhsT=wu[:, kc, g * 128:(g + 1) * 128],
                                       rhs=HT[:, kc, tt * 512:(tt + 1) * 512], start=(kc == 0), stop=(kc == 7))
                    return ins
                htk = [("HT", tt * 4 + q) for q in range(4)]
                P.add("pe", mmu, reads=[ku] + htk, writes=pk(b))
                if tt == 0:
                    drain_mod(1)
                P.add("act", lambda e, j=j, b=b: e.activation(out=UG[j][:], in_=bank(b), func=AF.Gelu_apprx_tanh),
                      reads=pk(b), writes=[("ug", j)])
                b2 = ps1()

                def mms(e, g=g, tt=tt, b2=b2):
                    ins = None
                    for n in range(4):
                        ins = e.matmul(bank(b2)[:, n * 128:(n + 1) * 128],
                                       lhsT=VLN[:, tt * 4 + n, g * 128:(g + 1) * 128], rhs=SGWT[:, g, :],
                                       start=True, stop=True)
                    return ins
                P.add("pe", mms, reads=["sgwt"] + [("vln", tt * 4 + n) for n in range(4)], writes=pk(b2))
                P.add("dve", lambda e, g=g, j=j, b2=b2: e.scalar_tensor_tensor(
                    out=SGT[j][:].rearrange("p (n q) -> p n q", n=4),
                    in0=bank(b2).rearrange("p (n q) -> p n q", n=4), scalar=SGG[:, g:g + 1],
                    in1=BIASB[:, g:g + 1, :].broadcast_to([128, 4, 128]), op0=ALU.mult, op1=ALU.add),
                    reads=pk(b2) + ["sgg", "biasb"], writes=[("sgt", j)])
                P.add("dve", lambda e, g=g, tt=tt, j=j: e.tensor_tensor(
                    out=OT[:, g, tt * 512:(tt + 1) * 512], in0=SGT[j][:], in1=UG[j][:], op=ALU.mult),
                    reads=[("sgt", j), ("ug", j)], writes=[("OT", g)])
        fpend = None
        if half == 0:
            for pt in range(2):
                ct, kc_ = ring_load(I["c1024"].rearrange("(qc p) n -> p qc n", p=128)[:, :, pt * 512:(pt + 1) * 512],
                                    [128, 8, 512])
                stt, ks_ = ring_load(I["s1024"].rearrange("(qc p) n -> p qc n", p=128)[:, :, pt * 512:(pt + 1) * 512],
                                     [128, 8, 512])
                for g in range(4):
                    s2 = fourier_group(g, pt * 512, [(0, 512, ct, stt, list(range(8)))], [kc_, ks_])
                    if fpend is not None:
                        fpend()
                    fpend = s2
        else:
            ct, kc_ = ring_load(I["c256"].rearrange("(qc p) n -> p qc n", p=128), [128, 2, 256])
            stt, ks_ = ring_load(I["s256"].rearrange("(qc p) n -> p qc n", p=128), [128, 2, 256])
            for pr in range(2):
                for g in range(4):
                    parts = [(jb * 256, 256, ct, stt, [2 * (2 * pr + jb), 2 * (2 * pr + jb) + 1]) for jb in range(2)]
                    s2 = fourier_group(g, pr * 512, parts, [kc_, ks_])
                    if fpend is not None:
                        fpend()
                    fpend = s2
        fpend()
        out_proj(half, I["ab_w_out"][0], nxt)

    def fourier_group(g, col0, parts, wkeys):
        bc = ps1()
        bs = ps1()
        zk = []

        def mmf(e, which, b):
            ins = None
            first = True
            for (co, w, ct, stt, tiles) in parts:
                m = ct if which == 0 else stt
                for qi, ti in enumerate(tiles):
                    ins = e.matmul(bank(b)[:, co:co + w], lhsT=ZB[:, ti, g * 128:(g + 1) * 128], rhs=m[:, qi, 0:w],
                                   start=(qi == 0 and first), stop=(qi == len(tiles) - 1),
                                   skip_group_check=True)
                first = False
            return ins
        for (_, _, _, _, tiles) in parts:
            zk += [("zb", ti) for ti in tiles]
        P.add("pe", lambda e: mmf(e, 0, bc), reads=wkeys + zk, writes=pk(bc))
        P.add("pe", lambda e: mmf(e, 1, bs), reads=wkeys + zk, writes=pk(bs))
        j = st["k"] % 2
        st["k"] += 1
        P.add("act", lambda e: e.activation(out=CZ[j][:, 0, :], in_=bank(bc), func=AF.Copy),
              reads=pk(bc), writes=[("cz", j, 0)])
        P.add("dve", lambda e: e.tensor_copy(out=CZ[j][:, 1, :], in_=bank(bs)), reads=pk(bs), writes=[("cz", j, 1)])
        def stage2():
            bo = ps1()

            def mmc(e):
                e.matmul(bank(bo), lhsT=CCS[:], rhs=CZ[j][:, 0, :], start=True, stop=False)
                return e.matmul(bank(bo), lhsT=NSCS[:], rhs=CZ[j][:, 1, :], start=False, stop=True)
            P.add("pe", mmc, reads=["ccs", "nscs", ("cz", j, 0), ("cz", j, 1)], writes=pk(bo))
            P.add("act", lambda e: e.activation(out=OT[:, 4 + g, col0:col0 + 512], in_=bank(bo), func=AF.Copy),
                  reads=pk(bo), writes=[("OT", 4 + g)])
        return stage2

    def mixer_attn(half, nxt):
        wqkv = I["attn_w_qkv"][0]
        sample = (half == 0)
        scale = 1.0 / np.sqrt(128.0)
        wkv, kkv = ring_load(wtile(wqkv, 1024, 1536), [128, 8, 512])
        if sample:
            P.add("sp", lambda e: e.dma_start(out=ROPE[:, 0, :], in_=I["cosT"]), writes=["rope"], dma=True)
            P.add("sp", lambda e: e.dma_start(out=ROPE[:, 1, :], in_=I["ssinT"]), writes=["rope"], dma=True)
            P.add("pool", lambda e: e.dma_start(out=CKB[:], in_=I["ck"].rearrange("(t p) c -> p t c", p=128)),
                  writes=["ckb"], dma=True)
            P.add("pool", lambda e: e.dma_start(out=VT[:, 8:10, :], in_=I["cv"].rearrange("(t p) c -> p t c", p=128)),
                  writes=[("vt", 8), ("vt", 9)], dma=True)
            b = ps1()

            def trk(e, b=b):
                ins = None
                for gg in range(2):
                    for t in range(2):
                        ins = e.transpose(out=bankbf(b)[:, (gg * 2 + t) * 128:(gg * 2 + t + 1) * 128],
                                          in_=CKB[:, t, gg * 128:(gg + 1) * 128], identity=IDN[:])
                return ins
            P.add("pe", trk, reads=["ckb", "idn"], writes=pk(b))
            P.add("act", lambda e, b=b: e.activation(
                out=KT[:, :, 1024:1280], in_=bankbf(b)[:, 0:512].rearrange("p (g t) -> p g t", g=2), func=AF.Copy),
                reads=pk(b), writes=[("ktc", 0), ("ktc", 1)])
        for i in range(8):
            b = ps1()

            def mm(e, i=i, b=b):
                ins = None
                for kc in range(8):
                    ins = e.matmul(bank(b), lhsT=HT[:, kc, i * 128:(i + 1) * 128], rhs=wkv[:, kc, :],
                                   start=(kc == 0), stop=(kc == 7))
                return ins
            P.add("pe", mm, reads=[kkv, ("HT", i)], writes=pk(b))
            P.add("act", lambda e, i=i, b=b: e.activation(out=VT[:, i, :], in_=bank(b)[:, 256:512], func=AF.Copy),
                  reads=pk(b), writes=[("vt", i)])
            drain_mod(1)
            if not sample:
                j = i % 2
                P.add("dve", lambda e, j=j, b=b: e.tensor_copy(out=KVST[j][:], in_=bank(b)),
                      reads=pk(b), writes=[("kvst", j)])
                P.add("sp", lambda e, j=j, i=i: e.dma_start(out=O["sk"][i * 128:(i + 1) * 128, :], in_=KVST[j][:, 0:256]),
                      reads=[("kvst", j)], writes=[("o_sk", i)], dma=True)
                P.add("sp", lambda e, j=j, i=i: e.dma_start(out=O["sv"][i * 128:(i + 1) * 128, :], in_=KVST[j][:, 256:512]),
                      reads=[("kvst", j)], writes=[("o_sv", i)], dma=True)
        wkvp = kkvp = None
        if sample:
            s = st["ring"] % 4
            st["ring"] += 1
            kkvp = RingKey(("ring", s))
            kkvp.gen = P.ring_gen[s] = P.ring_gen.get(s, 0) + 1
            wkvp = RING[s][:, 0:2048].rearrange("p (a b) -> p a b", a=8)
            for jj in range(2):
                P.add("dve", lambda e, s=s, jj=jj: e.tensor_copy(
                    out=RING[s][:, 0:2048].rearrange("p (k m j c) -> p k m j c", k=8, j=2, c=32)[:, :, :, jj, :],
                    in_=wkv[:, :, 0:256].rearrange("p k (m j c) -> p k m j c", j=2, c=32)[:, :, :, 1 - jj, :]),
                    reads=[kkv], writes=[kkvp])

        def proj_fm(dst, dkeys, w, wkey, wp, wpkey, c0, tt):
            htk = [("HT", tt * 4 + q) for q in range(4)]
            b = ps1()

            def mmq(e, w=w, b=b):
                ins = None
                for kc in range(8):
                    ins = e.matmul(bank(b), lhsT=w[:, kc, c0:c0 + 128], rhs=HT[:, kc, tt * 512:(tt + 1) * 512],
                                   start=(kc == 0), stop=(kc == 7))
                return ins
            P.add("pe", mmq, reads=[wkey] + htk, writes=pk(b))
            if wp is None:
                P.add("act", lambda e: e.activation(out=dst, in_=bank(b), func=AF.Copy), reads=pk(b), writes=dkeys)
                return
            bp = ps1()
            P.add("pe", lambda e: mmq(e, wp, bp), reads=[wpkey] + htk, writes=pk(bp))
            j = 0
            P.add("dve", lambda e: e.tensor_tensor(out=RT[2 * j][:], in0=bank(b), in1=ROPE[:, 0, tt * 512:(tt + 1) * 512],
                                                   op=ALU.mult), reads=pk(b) + ["rope"], writes=[("rt", 2 * j)])
            P.add("dve", lambda e: e.tensor_tensor(out=RT[2 * j + 1][:], in0=bank(bp),
                                                   in1=ROPE[:, 1, tt * 512:(tt + 1) * 512], op=ALU.mult),
                  reads=pk(bp) + ["rope"], writes=[("rt", 2 * j + 1)])
            P.add("dve", lambda e: e.tensor_tensor(out=dst, in0=RT[2 * j][:], in1=RT[2 * j + 1][:], op=ALU.add),
                  reads=[("rt", 2 * j), ("rt", 2 * j + 1)], writes=dkeys)

        for gg in range(2):
            for tt in range(2):
                proj_fm(KT[:, gg, tt * 512:(tt + 1) * 512], [("kt", gg, tt)], wkv, kkv, wkvp, kkvp, gg * 128, tt)
                drain_mod(1)
        for gg in range(2):
            wq, kq = ring_load(wtile(wqkv, gg * 512, (gg + 1) * 512), [128, 8, 512])
            wqp = kqp = None
            if sample:
                s = st["ring"] % 4
                st["ring"] += 1
                kqp = RingKey(("ring", s))
                kqp.gen = P.ring_gen[s] = P.ring_gen.get(s, 0) + 1
                wqp = RING[s][:, 0:4096].rearrange("p (a b) -> p a b", a=8)
                for jj in range(2):
                    P.add("dve", lambda e, s=s, jj=jj, wq=wq: e.tensor_copy(
                        out=RING[s][:, 0:4096].rearrange("p (k m j c) -> p k m j c", k=8, j=2, c=32)[:, :, :, jj, :],
                        in_=wq.rearrange("p k (m j c) -> p k m j c", j=2, c=32)[:, :, :, 1 - jj, :]),
                        reads=[kq], writes=[kqp])
            for hh in range(4):
                for tt in range(2):
                    proj_fm(QT[:, hh, tt * 512:(tt + 1) * 512], [("qt", hh, tt)], wq, kq, wqp, kqp, hh * 128, tt)
            fin = None
            for hh in range(4):
                h = gg * 4 + hh
                for qt in range(2):
                    fin = attend(sample, gg, hh, h, qt, scale, fin)
            fin()
        out_proj(half, I["attn_w_o"][0], nxt)

    def attend(sample, gg, hh, h, qt, scale, prev_fin):
        a = st.get("att", 0)
        st["att"] = a + 1
        bo = 2 * (a % 2)
        bd = bo + 1

        def ps1():
            b_ = 4 + st.get("bs", 0) % 4
            st["bs"] = st.get("bs", 0) + 1
            return b_
        items = []
        if sample:
            for jc in range(2):
                items.append((KT[:, gg, 1024 + jc * 128:1024 + (jc + 1) * 128], VT[:, 8 + jc, gg * 128:(gg + 1) * 128],
                              0, 512, [("ktc", gg), ("vt", 8 + jc)], []))
            for j in range(max(0, 4 * qt - 1), min(8, 4 * qt + 5)):
                ilo = max(j - 1, 4 * qt)
                ihi = min(j + 1, 4 * qt + 3)
                lo = (ilo - 4 * qt) * 128
                hi = (ihi - 4 * qt + 1) * 128
                ms = []
                if ilo == j - 1:
                    ms.append((0, 0))
                if ihi == j + 1:
                    ms.append((hi - lo - 128, 1))
                items.append((KT[:, gg, j * 128:(j + 1) * 128], VT[:, j, gg * 128:(gg + 1) * 128], lo, hi,
                              [("kt", gg, j // 4), ("vt", j)], ms))
        else:
            for jb in range(2):
                bt = 2 * qt + jb
                for kt_ in range(2):
                    ti = 2 * bt + kt_
                    items.append((KT[:, gg, ti * 128:(ti + 1) * 128], VT[:, ti, gg * 128:(gg + 1) * 128],
                                  jb * 256, jb * 256 + 256, [("kt", gg, ti // 4), ("vt", ti)], []))
        first = True
        pend = []
        for (kap, vap, lo, hi, keys, ms) in items:
            n = hi - lo
            bs = ps1()
            P.add("pe", lambda e, kap=kap, lo=lo, hi=hi, bs=bs: e.matmul(
                bank(bs)[:, 0:hi - lo], lhsT=kap, rhs=QT[:, hh, qt * 512 + lo:qt * 512 + hi], start=True, stop=True),
                reads=[keys[0], ("qt", hh, qt)], writes=pk(bs))
            j = st["k"] % 3
            st["k"] += 1
            ek = [("expb", j)]
            P.add("act", lambda e, j=j, bs=bs, n=n: e.activation(out=EXPB[j][:, 0:n], in_=bank(bs)[:, 0:n],
                                                                 func=AF.Exp, scale=float(scale)),
                  reads=pk(bs), writes=ek)
            for (co, which) in ms:
                P.add("dve", lambda e, j=j, co=co, which=which: e.tensor_tensor(
                    out=EXPB[j][:, co:co + 128], in0=EXPB[j][:, co:co + 128],
                    in1=MASKS[:, which * 128:(which + 1) * 128], op=ALU.mult), reads=ek + ["masks"], writes=ek)

            def pv(e, vap=vap, j=j, lo=lo, hi=hi, first=first):
                e.matmul(bank(bo)[:, lo:hi], lhsT=vap, rhs=EXPB[j][:, 0:hi - lo], start=first, stop=False,
                         skip_group_check=True)
                return e.matmul(bank(bd)[:, lo:hi], lhsT=ONES[:], rhs=EXPB[j][:, 0:hi - lo], start=first, stop=False,
                                skip_group_check=True)
            pend.append((pv, ek + [keys[1], "ones"]))
            if len(pend) > 2:
                pv_, rd_ = pend.pop(0)
                P.add("pe", pv_, reads=rd_, writes=pk(bo) + pk(bd))
                if prev_fin is not None:
                    prev_fin()
                    prev_fin = None
            first = False
        while pend:
            pv_, rd_ = pend.pop(0)
            P.add("pe", pv_, reads=rd_, writes=pk(bo) + pk(bd))
        if prev_fin is not None:
            prev_fin()
        j = st["k"] % 2
        st["k"] += 1
        rk = [("rcp", j)]

        def fin():
            P.add("act", lambda e: e.activation(out=RCP[j][:], in_=bank(bd), func=AF.Ln, bias=SINKE[:, h:h + 1]),
                  reads=pk(bd) + ["sinke"], writes=rk)
            P.add("act", lambda e: e.activation(out=RCP[j][:], in_=RCP[j][:], func=AF.Exp, scale=-1.0),
                  reads=rk, writes=rk)
            P.add("dve", lambda e: e.tensor_tensor(out=OT[:, h, qt * 512:(qt + 1) * 512], in0=bank(bo), in1=RCP[j][:],
                                                   op=ALU.mult), reads=pk(bo) + rk, writes=[("OT", h)])
        return fin

    if phases is None:
        phases = [(l, half, sub) for half in range(2) for l in range(2) for sub in range(2)]
    for idx, (l, half, sub) in enumerate(phases):
        nxt = phases[idx + 1] if idx + 1 < len(phases) else None
        if idx == 0:
            load_x(range(half * 8, half * 8 + 8))
            mod_group_fast(l, sub, after_first=lambda: load_mod_slots01(l, half, sub))
            load_small_consts()
            load_x(range((1 - half) * 8, (1 - half) * 8 + 8))
            pend = []
            for i in range(8):
                pend.append(pre_norm_tile(half, i, "OT", i % 8))
                if len(pend) > 2:
                    pend.pop(0)()
            while pend:
                pend.pop(0)()
        load_mod(l, half, sub, (2,))
        fence()
        if sub == 0:
            if l == 0:
                mixer_ab(half, nxt)
            else:
                mixer_attn(half, nxt)
        else:
            ffn(l, half, nxt)
        st["b1"] = 0
        st["b2"] = 0
        if not any(ph[1] == half for ph in phases[idx + 1:]):
            store_x(half)
    for hf in range(2):
        if not any(ph[1] == hf for ph in phases):
            store_x(hf)
    outk = [("o_y", i) for i in range(16)]
    outk += [("o_sk", i) for i in range(8)] + [("o_sv", i) for i in range(8)]
    P.add("sp", lambda e: e.nop(), reads=outk)

    P.analyze()
    with contextlib.ExitStack() as stack:
        esems = {k: stack.enter_context(nc.semaphore("es_" + k)) for k in Prog.ENGS}
        dsems = [stack.enter_context(nc.semaphore("ds_%d" % i)) for i in range(P.ndsems)]
        block = stack.enter_context(nc.Block())
        P.emit(block, esems, dsems)
    return nc


_CONSTS = None


def make_in_maps(inputs):
    global _CONSTS
    if _CONSTS is None:
        _CONSTS = _consts()
    f = lambda a: np.ascontiguousarray(np.asarray(a, dtype=np.float32))
    shared = {k: f(inputs[k]) for k in ("mod_w", "mod_b", "norm_pre_mix", "norm_post_mix", "norm_pre_ffn",
                                        "norm_post_ffn", "ab_w_in", "sgu_w", "sgu_b", "sgu_g", "ab_w_out",
                                        "attn_w_qkv", "attn_sink", "attn_w_o", "ffn_w_gate", "ffn_w_up",
                                        "ffn_w_down")}
    shared.update(_CONSTS)
    xs, xp = f(inputs["x_sample"]), f(inputs["x_prompt"])
    ck, cv = f(inputs["cache_k"]), f(inputs["cache_v"])
    c, cctx = f(inputs["c"]), f(inputs["c_ctx"])
    maps = []
    for b in range(NCORES):
        m = dict(shared)
        m["xs"] = xs[b]
        m["xp"] = np.ascontiguousarray(xp[4 * b:4 * b + 4].reshape(1024, D))
        m["ck"] = np.ascontiguousarray(ck[b, 0].reshape(256, 256))
        m["cv"] = np.ascontiguousarray(cv[b, 0].reshape(256, 256))
        m["cond"] = np.ascontiguousarray(np.stack([c[b], cctx], axis=0))
        maps.append(m)
    return maps


def gather(results):
    ys = np.stack([r["ys"] for r in results], axis=0).astype(np.float32)
    yp = np.concatenate([r["yp"].reshape(4, 256, D) for r in results], axis=0).astype(np.float32)
    sk = np.concatenate([r["sk"].reshape(4, 1, 256, 2, 128) for r in results], axis=0).astype(np.float32)
    sv = np.concatenate([r["sv"].reshape(4, 1, 256, 2, 128) for r in results], axis=0).astype(np.float32)
    return (yp, ys, sk, sv)


def kernel(**inputs):
    nc = build()
    res = run_bass_kernel_spmd(nc, make_in_maps(inputs), core_ids=list(range(NCORES)))
    return gather(res.results)
```

```python
import contextlib
import numpy as np
import ml_dtypes
import concourse.bass as bass
import concourse.mybir as mybir
from concourse.bass_utils import run_bass_kernel_spmd

F32 = mybir.dt.float32
BF16 = mybir.dt.bfloat16
U8 = mybir.dt.uint8
AF = mybir.ActivationFunctionType
ALU = mybir.AluOpType

D = 1024
DFF = 2816
NFF = DFF // 128
EPS = 1e-6
NCORES = 8


class Op:
    __slots__ = ("eng", "fn", "reads", "writes", "dma", "idx", "waits", "clock",
                 "signal", "sem", "semval", "gidx")

    def __init__(self, eng, fn, reads, writes, dma):
        self.eng = eng
        self.fn = fn
        self.reads = reads
        self.writes = writes
        self.dma = dma
        self.waits = None
        self.clock = None
        self.signal = False
        self.sem = None
        self.semval = None


class RingKey(tuple):
    gen = None


class Prog:
    ENGS = ("pe", "act", "dve", "pool", "sp")
    MAX_DMA_SEMS = 72

    def __init__(self, nc):
        self.nc = nc
        self.ops = []
        self.byeng = {e: [] for e in self.ENGS}
        self.ring_gen = {}

    def add(self, eng, fn, reads=(), writes=(), dma=False):
        for r in reads:
            assert getattr(r, "gen", None) is None or self.ring_gen[r[1]] == r.gen, ("stale ring tile", r)
        op = Op(eng, fn, tuple(reads), tuple(writes), dma)
        op.gidx = len(self.ops)
        op.idx = len(self.byeng[eng])
        self.ops.append(op)
        self.byeng[eng].append(op)
        return op

    def analyze(self):
        last_w = {}
        readers = {}
        clock = {e: {} for e in self.ENGS}
        dsems = []
        for op in self.ops:
            E = op.eng
            deps = {}
            for r in op.reads:
                d = last_w.get(r)
                if d is not None:
                    deps[d.gidx] = (d, True)
                if isinstance(r, tuple) and r[0] == "ps":
                    for d in readers.get(r, ()):
                        if d.eng != E and d.gidx not in deps:
                            deps[d.gidx] = (d, False)
            for w in op.writes:
                d = last_w.get(w)
                if d is not None and d.gidx not in deps:
                    deps[d.gidx] = (d, False)
                for d in readers.get(w, ()):
                    if d.gidx not in deps:
                        deps[d.gidx] = (d, False)
            ck = clock[E]
            need = {}
            for d, raw in deps.values():
                if d is op:
                    continue
                if d.dma:
                    key, val = ("dma", d.sem), d.semval
                else:
                    if d.eng == E and not op.dma and E == "pe":
                        continue
                    key, val = d.eng, d.idx + 1
                if ck.get(key, 0) >= val:
                    continue
                if key not in need or need[key][0] < val:
                    need[key] = (val, d)
            if op.dma:
                chosen = None
                cls = "sw" if E == "pool" else "hw"
                for si, (tot, lop, scls) in enumerate(dsems):
                    if scls != cls:
                        continue
                    k = ("dma", si)
                    have = max(ck.get(k, 0), need[k][0] if k in need else 0)
                    if have >= tot:
                        chosen = si
                        break
                if chosen is None:
                    if len(dsems) < self.MAX_DMA_SEMS:
                        dsems.append([0, None, cls])
                        chosen = len(dsems) - 1
                    else:
                        chosen = min((i for i in range(len(dsems)) if dsems[i][2] == cls),
                                     key=lambda i: dsems[i][1].gidx)
                        tot, lop, _ = dsems[chosen]
                        need[("dma", chosen)] = (tot, lop)
                op.sem = chosen
                dsems[chosen][0] += 16
                dsems[chosen][1] = op
                op.semval = dsems[chosen][0]
                op.signal = True
            if need:
                ck = dict(ck)
                for key, (val, d) in need.items():
                    d.signal = True
                    for k2, v2 in d.clock.items():
                        if ck.get(k2, 0) < v2:
                            ck[k2] = v2
                    if ck.get(key, 0) < val:
                        ck[key] = val
                clock[E] = ck
            op.waits = [(key, d) for key, (val, d) in need.items()]
            op.clock = ck
            for r in op.reads:
                readers.setdefault(r, []).append(op)
            for w in op.writes:
                last_w[w] = op
                readers[w] = []
        self.ndsems = len(dsems)
        for e in self.ENGS:
            n = 0
            for op in self.byeng[e]:
                if op.dma:
                    continue
                if op.signal:
                    n += 1
                    op.semval = n

    def emit(self, block, esems, dsems):
        engobj = {"pe": "tensor", "act": "scalar", "dve": "vector", "pool": "gpsimd", "sp": "sync"}

        def make(ename):
            ops = self.byeng[ename]

            def body(e):
                for op in ops:
                    for key, d in op.waits:
                        if isinstance(key, tuple):
                            e.wait_ge(dsems[key[1]], d.semval)
                        else:
                            e.wait_ge(esems[key], d.semval)
                    ins = op.fn(e)
                    assert ins is not None, (ename, op.reads, op.writes)
                    if op.signal:
                        if op.dma:
                            ins.then_inc(dsems[op.sem], 16)
                        else:
                            ins.then_inc(esems[ename], 1)
            return body

        for ename in self.ENGS:
            getattr(block, engobj[ename])(make(ename))


def _dft(n, scale, kind):
    idx = (np.arange(n)[:, None].astype(np.int64) * np.arange(n)[None, :].astype(np.int64)) % n
    ang = 2.0 * np.pi * idx.astype(np.float64) / n
    m = np.cos(ang) if kind == "c" else np.sin(ang)
    return (m * scale).astype(np.float32)


def _consts():
    bf = ml_dtypes.bfloat16
    c = {}
    c["ident"] = np.eye(128, dtype=np.float32).astype(bf)
    c["ones"] = np.ones((128, 128), dtype=np.float32).astype(bf)
    c["identf"] = np.eye(16, dtype=np.float32)
    dd = np.arange(128)
    perm = np.where((dd % 64) < 32, dd + 32, dd - 32)
    rpm = np.zeros((128, 128), dtype=np.float32)
    rpm[perm, dd] = 1.0
    c["rpm"] = rpm.astype(bf)
    sl = np.arange(128)[:, None]
    ql = np.arange(128)[None, :]
    c["masks"] = np.concatenate([(sl <= ql), (ql <= sl)], axis=1).astype(np.float32).astype(bf)
    c["c1024"] = _dft(1024, 1.0 / 32.0, "c").astype(bf)
    c["s1024"] = _dft(1024, 1.0 / 32.0, "s").astype(bf)
    c["c256"] = _dft(256, 1.0 / 16.0, "c").astype(bf)
    c["s256"] = _dft(256, 1.0 / 16.0, "s").astype(bf)
    c["cc"] = _dft(128, 1.0 / np.sqrt(128.0), "c").astype(bf)
    c["nsc"] = (-_dft(128, 1.0 / np.sqrt(128.0), "s")).astype(bf)
    t = 1024
    rows = np.repeat(np.arange(t // 64), 64).astype(np.float32)
    cols = np.tile(np.arange(64), t // 64).astype(np.float32)
    inv = (np.float32(10000.0) ** (-np.arange(0, 64, 2, dtype=np.float32) / np.float32(64))).astype(np.float32)
    ar = rows[:, None] * inv
    ac = cols[:, None] * inv
    ang = np.concatenate([ar, ar, ac, ac], axis=-1).astype(np.float32)
    sign = np.where((np.arange(128) % 64) < 32, -1.0, 1.0).astype(np.float32)
    c["cosT"] = np.ascontiguousarray(np.cos(ang).T.astype(np.float32))
    c["ssinT"] = np.ascontiguousarray((np.sin(ang) * sign[None, :]).T.astype(np.float32))
    return c


_IN_SHAPES = {
    "xs": ([1024, D], F32), "xp": ([1024, D], F32), "ck": ([256, 256], F32), "cv": ([256, 256], F32),
    "cond": ([2, D], F32),
    "mod_w": ([2, D, 6 * D], F32), "mod_b": ([2, 6 * D], F32),
    "norm_pre_mix": ([2, D], F32), "norm_post_mix": ([2, D], F32),
    "norm_pre_ffn": ([2, D], F32), "norm_post_ffn": ([2, D], F32),
    "ab_w_in": ([1, D, 1536], F32), "sgu_w": ([1, 4, 128, 128], F32), "sgu_b": ([1, 4, 128], F32),
    "sgu_g": ([1, 512], F32), "ab_w_out": ([1, D, D], F32),
    "attn_w_qkv": ([1, D, 1536], F32), "attn_sink": ([1, 8], F32), "attn_w_o": ([1, D, D], F32),
    "ffn_w_gate": ([2, D, DFF], F32), "ffn_w_up": ([2, D, DFF], F32), "ffn_w_down": ([2, DFF, D], F32),
    "ident": ([128, 128], BF16), "identf": ([16, 16], F32), "rpm": ([128, 128], BF16), "ones": ([128, 128], BF16), "masks": ([128, 256], BF16),
    "c1024": ([1024, 1024], BF16), "s1024": ([1024, 1024], BF16),
    "c256": ([256, 256], BF16), "s256": ([256, 256], BF16),
    "cc": ([128, 128], BF16), "nsc": ([128, 128], BF16),
    "cosT": ([128, 1024], F32), "ssinT": ([128, 1024], F32),
}
_OUT_SHAPES = {"ys": [1024, D], "yp": [1024, D], "sk": [1024, 256], "sv": [1024, 256]}


def build(phases=None):
    nc = bass.Bass("TRN2", target_bir_lowering=False)
    I = {n: nc.dram_tensor(n, s, dt, kind="ExternalInput").ap() for n, (s, dt) in _IN_SHAPES.items()}
    O = {n: nc.dram_tensor(n, s, F32, kind="ExternalOutput").ap() for n, s in _OUT_SHAPES.items()}
    MSCR = nc.dram_tensor("mscr", [2, 2, 6 * D], F32).ap()

    rem = nc.sbuf_bytes_remaining
    arena = nc.alloc_sbuf_tensor("arena", [128, rem - 2048], U8)
    base = nc.lookup_mloc(arena).addr
    limit = base + rem - 2048
    cur = [base]

    def alloc(name, shape, dt, at=None):
        nbytes = int(np.prod(shape[1:])) * (4 if dt == F32 else 2)
        if at is None:
            at = cur[0]
            cur[0] += (nbytes + 31) // 32 * 32
            assert cur[0] <= limit, (name, cur[0], limit)
        return nc.alloc_sbuf_tensor_at(name, list(shape), dt, offset=at)

    X = alloc("X", [128, 16, D], F32)
    HT = alloc("HT", [128, 8, 1024], BF16)
    OT_off = cur[0]
    OT = alloc("OT", [128, 8, 1024], BF16)
    HID_off = cur[0]
    HIDB = 45056
    HID = alloc("HID", [128, NFF, 1024], BF16)
    cur[0] = HID_off + HIDB
    cur_ring0 = cur[0]
    RING = [alloc("ring%d" % i, [128, 4096], BF16) for i in range(4)]
    MOD = [alloc("mod%d" % i, [128, D], F32) for i in range(3)]
    IDN = alloc("idn", [128, 128], BF16)
    ONES = alloc("ones_sb", [128, 128], BF16)
    MASKS = alloc("masks_sb", [128, 256], BF16)
    JUNK = alloc("junk", [128, D], BF16)
    SS = alloc("ss", [128, 8], F32)
    RS = alloc("rs", [128, 8], F32)
    SS2 = alloc("ss2", [128, 2], F32)
    RS2 = alloc("rs2", [128, 2], F32)
    SCT0 = alloc("sct0", [128, 8, 2], F32)
    C16 = alloc("c16", [16, 128], F32)
    IDF = alloc("idf", [16, 16], F32)
    RPM = alloc("rpm_sb", [128, 128], BF16)
    SCT = alloc("sct", [128, 8, 2], BF16)
    SINKE = alloc("sinke", [128, 8], F32)
    SGG = alloc("sgg", [128, 4], F32)
    BIASB = alloc("biasb", [128, 4, 128], F32)
    SGWT = alloc("sgwt", [128, 4, 128], BF16)
    CCS = alloc("ccs", [128, 128], BF16)
    NSCS = alloc("nscs", [128, 128], BF16)
    LNS = alloc("lns", [128, 8, 4], F32)
    MRW = [alloc("mrw%d" % j, [2, 3, 256], F32) for j in range(2)]
    MW = [alloc("mw%d" % j, [128, 8, 256], BF16) for j in range(2)]
    NSC = {}
    for kind, off0 in (("OT", OT_off), ("HID", HID_off)):
        NSC[kind] = dict(
            tmp=[alloc("tmp%s%d" % (kind, j), [128, D], F32, at=off0 + 4096 * j) for j in range(2)],
            hb=[alloc("hb%s%d" % (kind, j), [128, D], BF16, at=off0 + 8192 + 2048 * j) for j in range(4)],
            tk=[[(kind, 2 * j), (kind, 2 * j + 1)] for j in range(2)],
            hk=[[(kind, 4 + j)] for j in range(4)])
    SG = [alloc("sg%d" % j, [128, 512], F32, at=OT_off + 12288 + 2048 * j) for j in range(2)]
    VLN = alloc("vln", [128, 8, 512], BF16, at=HID_off)
    ZB = alloc("zb", [128, 8, 512], BF16, at=HID_off + 8192)
    GV = alloc("gv", [128, 8, 512], F32, at=HID_off + 16384)
    UG = [alloc("ug%d" % j, [128, 512], F32, at=HID_off + 32768 + 2048 * j) for j in range(2)]
    CZ = [alloc("cz%d" % j, [128, 2, 512], BF16, at=HID_off + 36864 + 2048 * j) for j in range(2)]
    SGT = [alloc("sgt%d" % j, [128, 512], F32, at=HID_off + 40960 + 2048 * j) for j in range(2)]
    QT = alloc("qt", [128, 4, 1024], BF16, at=HID_off)
    KT = alloc("kt", [128, 2, 1280], BF16, at=HID_off + 8192)
    VT = alloc("vt", [128, 10, 256], BF16, at=HID_off + 13312)
    ROPE = alloc("rope", [128, 2, 1024], F32, at=HID_off + 18432)
    EXPB = [alloc("expb%d" % j, [128, 512], BF16, at=HID_off + 26624 + 1024 * j) for j in range(3)]
    RCP = [alloc("rcp%d" % j, [128, 512], F32, at=HID_off + 29696 + 2048 * j) for j in range(2)]
    KVST = [alloc("kvst%d" % j, [128, 512], F32, at=HID_off + 33792 + 2048 * j) for j in range(2)]
    RT = [alloc("rt%d" % j, [128, 512], F32, at=HID_off + 37888 + 2048 * j) for j in range(2)]
    CKB = alloc("ckb", [128, 2, 256], BF16, at=HID_off + 41984)
    print('SBUF free bytes/partition:', limit - cur[0])
    PSUM = nc.alloc_psum_tensor("psum", [128, 4096], F32)

    def bank(b, n=1):
        return PSUM[:, b * 512:(b + n) * 512]

    def bankbf(b):
        return PSUM[:, b * 512:(b + 1) * 512].bitcast(BF16)

    EPSB = alloc("epsb", [128, 1], F32)
    P = Prog(nc)
    st = {"ring": 0, "b1": 0, "b2": 0, "k": 0, "mk": 0}
    FENCE = alloc("fence", [128, 1], F32)
    hidk = [("HID", c) for c in range(NFF)] + ["hidx"]
    REGK = list(hidk)
    REGK += [(n, i) for n in ("vln", "zb", "gv") for i in range(8)]
    REGK += [(n, j) for n in ("ug", "sgt", "rcp", "kvst") for j in range(2)]
    REGK += [("cz", j, c) for j in range(2) for c in range(2)]
    REGK += [("qt", h, t) for h in range(4) for t in range(2)] + [("kt", g, t) for g in range(2) for t in range(2)]
    REGK += [("ktc", g) for g in range(2)] + [("vt", i) for i in range(10)] + ["rope", "ckb"]
    REGK += [("expb", j) for j in range(3)] + [("rt", j) for j in range(2)]

    def fence():
        P.add("dve", lambda e: e.memset(FENCE[:], 0.0), writes=REGK)

    def ps1():
        b = st["b1"] % 8
        st["b1"] += 1
        return b

    def ps2():
        b = (st["b2"] % 4) * 2
        st["b2"] += 1
        return b

    def pk(b, n=1):
        return [("ps", b + i) for i in range(n)]

    def ring_load(src, shape, eng="pool"):
        s = st["ring"] % 4
        st["ring"] += 1
        n = int(np.prod(shape[1:]))
        assert n <= 4096
        if len(shape) == 3:
            dst = RING[s][:, 0:n].rearrange("p (a b) -> p a b", a=shape[1])
        else:
            dst = RING[s][:, 0:n]
        key = RingKey(("ring", s))
        key.gen = P.ring_gen[s] = P.ring_gen.get(s, 0) + 1
        P.add(eng, lambda e, dst=dst, src=src: e.dma_start(out=dst, in_=src), writes=[key], dma=True)
        return dst, key

    def wtile(w2d, c0, c1, r0=0, nk=8):
        return w2d[r0:r0 + nk * 128, :].rearrange("(kc p) n -> p kc n", p=128)[:, :, c0:c1]

    P.add("act", lambda e: e.dma_start(out=C16[:], in_=I["cond"].rearrange("c (kc p) -> (c kc) p", p=128)),
          writes=["c16"], dma=True)
    P.add("act", lambda e: e.dma_start(out=IDF[:], in_=I["identf"]), writes=["idf"], dma=True)
    P.add("sp", lambda e: e.dma_start(out=IDN[:], in_=I["ident"]), writes=["idn"], dma=True)
    P.add("sp", lambda e: e.dma_start(out=ONES[:], in_=I["ones"]), writes=["ones"], dma=True)
    P.add("sp", lambda e: e.dma_start(out=RPM[:], in_=I["rpm"]), writes=["rpm"], dma=True)
    P.add("sp", lambda e: e.dma_start(out=MASKS[:], in_=I["masks"]), writes=["masks"], dma=True)
    P.add("sp", lambda e: e.dma_start(out=CCS[:], in_=I["cc"]), writes=["ccs"], dma=True)
    P.add("sp", lambda e: e.dma_start(out=NSCS[:], in_=I["nsc"]), writes=["nscs"], dma=True)
    P.add("sp", lambda e: e.dma_start(out=SINKE[:], in_=I["attn_sink"][0:1, :].broadcast_to([128, 8])),
          writes=["sinke"], dma=True)

    def load_small_consts():
        P.add("sp", lambda e: e.dma_start(out=SGG[:], in_=I["sgu_g"][0].rearrange("(g p) -> p g", p=128),
                                          allow_slow_non_contiguous=True), writes=["sgg"], dma=True)
        P.add("sp", lambda e: e.dma_start(
            out=BIASB[:], in_=I["sgu_b"][0:1].broadcast_to([128, 4, 128])), writes=["biasb"], dma=True)
    P.add("dve", lambda e: e.memset(EPSB[:], EPS), writes=["epsb"])

    def store_x(half):
        for i in range(half * 8, half * 8 + 8):
            dst = O["ys"] if i < 8 else O["yp"]
            r = (i % 8) * 128
            P.add("sp", lambda e, i=i, dst=dst, r=r: e.dma_start(out=dst[r:r + 128, :], in_=X[:, i, :]),
                  reads=[("X", i)], writes=[("o_y", i)], dma=True)

    def load_x(tiles):
        for i in tiles:
            src = I["xs"] if i < 8 else I["xp"]
            r = (i % 8) * 128
            P.add("sp", lambda e, i=i, src=src, r=r: e.dma_start(out=X[:, i, :], in_=src[r:r + 128, :]),
                  writes=[("X", i)], dma=True)

    P.add("act", lambda e: e.activation(out=SINKE[:], in_=SINKE[:], func=AF.Exp), reads=["sinke"], writes=["sinke"])
    P.add("pe", lambda e: e.transpose(out=PSUM[:, 0:16], in_=C16[:], identity=IDF[:]), reads=["c16", "idf"],
          writes=[("ps", 0)])
    P.add("act", lambda e: e.activation(out=SCT[:].rearrange("p kc c -> p c kc"),
                                        in_=PSUM[:, 0:16].rearrange("p (c kc) -> p c kc", c=2), func=AF.Silu),
          reads=[("ps", 0)], writes=["sct"])

    norm_of_chunk = {1: "norm_pre_mix", 2: "norm_post_mix", 4: "norm_pre_ffn", 5: "norm_post_ffn"}
    allk = [("OT", c) for c in range(8)]
    MBW = 256
    NMB = 6 * D // MBW
    mod_pending = [(l, cb) for l in range(2) for cb in range(NMB)]

    def mod_block(l, cb):
        j = st["mk"] % 2
        st["mk"] += 1
        mk = [("mrow", j)]
        wk = ("mw", j)
        wt = MW[j]
        c0 = cb * MBW
        P.add("pool", lambda e: e.dma_start(out=wt[:], in_=wtile(I["mod_w"][l], c0, c0 + MBW)), writes=[wk], dma=True)
        chunk, hc = c0 // D, (c0 % D)
        P.add("sp", lambda e: e.dma_start(out=MRW[j][:, 0, :], in_=I["mod_b"][l:l + 1, c0:c0 + MBW]
                                          .broadcast_to([2, MBW])), writes=mk, dma=True)
        if chunk in norm_of_chunk:
            P.add("sp", lambda e: e.dma_start(
                out=MRW[j][:, 1, :], in_=I[norm_of_chunk[chunk]][l:l + 1, hc:hc + MBW].broadcast_to([2, MBW])),
                writes=mk, dma=True)
        b = ps1()

        def mm(e):
            ins = None
            for kc in range(8):
                ins = e.matmul(bank(b)[0:2, 0:MBW], lhsT=SCT[:, kc, :], rhs=wt[:, kc, :],
                               start=(kc == 0), stop=(kc == 7))
            return ins
        P.add("pe", mm, reads=[wk, "sct"], writes=pk(b))
        P.add("dve", lambda e: e.tensor_tensor(out=MRW[j][:, 2, :], in0=bank(b)[0:2, 0:MBW], in1=MRW[j][:, 0, :],
                                               op=ALU.add), reads=pk(b) + mk, writes=mk)
        if chunk in (1, 4):
            P.add("dve", lambda e: e.scalar_tensor_tensor(
                out=MRW[j][:, 2, :], in0=MRW[j][:, 2, :], scalar=1.0, in1=MRW[j][:, 1, :],
                op0=ALU.add, op1=ALU.mult), reads=mk, writes=mk)
        elif chunk in (2, 5):
            P.add("dve", lambda e: e.tensor_tensor(out=MRW[j][:, 2, :], in0=MRW[j][:, 2, :], in1=MRW[j][:, 1, :],
                                                   op=ALU.mult), reads=mk, writes=mk)
        P.add("sp", lambda e: e.dma_start(out=MSCR[l, :, c0:c0 + MBW], in_=MRW[j][:, 2, :]),
              reads=mk, writes=[("mscr", l, cb)], dma=True)

    ROWS = alloc("rows", [2, 4096], F32, at=cur_ring0)
    GROW = alloc("grow", [2, 2048], F32, at=cur_ring0 + 16384)
    MWX = [alloc("mwx%d" % j, [128, 8, 256], BF16, at=cur_ring0 + 24576 + 4096 * j) for j in range(2)]

    def mod_group_fast(l, sub, after_first=None):
        rk = [RingKey(("ring", q)) for q in range(4)]
        for q in range(4):
            rk[q].gen = P.ring_gen[q] = P.ring_gen.get(q, 0) + 1
        st["ring"] = 4
        nblk = [0]
        r0 = sub * 3 * D
        P.add("sp", lambda e: e.dma_start(out=ROWS[:, 0:3 * D], in_=I["mod_b"][l:l + 1, r0:r0 + 3 * D]
                                          .broadcast_to([2, 3 * D])), writes=rk[0:2], dma=True)
        for gi, nm in enumerate(("norm_pre_mix", "norm_post_mix") if sub == 0 else ("norm_pre_ffn", "norm_post_ffn")):
            P.add("sp", lambda e, gi=gi, nm=nm: e.dma_start(
                out=GROW[:, gi * D:(gi + 1) * D], in_=I[nm][l:l + 1, :].broadcast_to([2, D])),
                writes=[rk[2]], dma=True)
        blocks = [(pl, pcb) for (pl, pcb) in mod_pending if pl == l and pcb // (NMB // 2) == sub]
        for blk in blocks:
            mod_pending.remove(blk)
            cb = blk[1]
            j = nblk[0] % 4
            nblk[0] += 1
            if j < 2:
                wk, wt = ("mw", j), MW[j]
            else:
                wk, wt = ("mwx", j), MWX[j - 2]
            c0 = cb * MBW
            lc = c0 - r0
            P.add("pool", lambda e, wt=wt, c0=c0: e.dma_start(out=wt[:], in_=wtile(I["mod_w"][l], c0, c0 + MBW)),
                  writes=[wk], dma=True)
            b = ps1()

            def mm(e, wt=wt, b=b):
                ins = None
                for kc in range(8):
                    ins = e.matmul(bank(b)[0:2, 0:MBW], lhsT=SCT[:, kc, :], rhs=wt[:, kc, :],
                                   start=(kc == 0), stop=(kc == 7))
                return ins
            P.add("pe", mm, reads=[wk, "sct"], writes=pk(b))
            P.add("dve", lambda e, b=b, lc=lc: e.tensor_tensor(
                out=ROWS[:, lc:lc + MBW], in0=bank(b)[0:2, 0:MBW], in1=ROWS[:, lc:lc + MBW], op=ALU.add),
                reads=pk(b) + rk[0:2], writes=rk[0:2])
            chunk = lc // D
            hc = lc % D
            if chunk == 1:
                P.add("dve", lambda e, lc=lc, hc=hc: e.scalar_tensor_tensor(
                    out=ROWS[:, lc:lc + MBW], in0=ROWS[:, lc:lc + MBW], scalar=1.0, in1=GROW[:, hc:hc + MBW],
                    op0=ALU.add, op1=ALU.mult), reads=rk, writes=rk[0:2])
            elif chunk == 2:
                P.add("dve", lambda e, lc=lc, hc=hc: e.tensor_tensor(
                    out=ROWS[:, lc:lc + MBW], in0=ROWS[:, lc:lc + MBW], in1=GROW[:, D + hc:D + hc + MBW],
                    op=ALU.mult), reads=rk, writes=rk[0:2])
            if len(blocks) == 12 and blk is blocks[7]:
                P.add("sp", lambda e: e.dma_start(out=MSCR[l, :, r0:r0 + 2 * D], in_=ROWS[:, 0:2 * D]),
                      reads=rk[0:2], writes=[("mscr", l, bb[1]) for bb in blocks[0:8]], dma=True)
                if after_first is not None:
                    after_first()
        if len(blocks) == 12:
            P.add("sp", lambda e: e.dma_start(out=MSCR[l, :, r0 + 2 * D:r0 + 3 * D], in_=ROWS[:, 2 * D:3 * D]),
                  reads=rk[0:2], writes=[("mscr", l, bb[1]) for bb in blocks[8:12]], dma=True)
        else:
            P.add("sp", lambda e: e.dma_start(out=MSCR[l, :, r0:r0 + 3 * D], in_=ROWS[:, 0:3 * D]),
                  reads=rk[0:2], writes=[("mscr", l, bb[1]) for bb in blocks], dma=True)
        P.add("dve", lambda e: e.memset(FENCE[:], 0.0), writes=[("mwx", 2), ("mwx", 3), rk[3]])

    def drain_mod(n):
        for _ in range(n):
            if mod_pending:
                mod_block(*mod_pending.pop(0))

    def ensure_mod(l, sub):
        while any(pl == l and (pcb // (NMB // 2)) == sub for (pl, pcb) in mod_pending):
            mod_block(*mod_pending.pop(0))

    wt, wk = ring_load(I["sgu_w"][0].rearrange("g p q -> p g q"), [128, 4, 128])
    b = ps1()

    def sgw_t(e, wt=wt, b=b):
        ins = None
        for g in range(4):
            ins = e.transpose(out=bankbf(b)[:, g * 128:(g + 1) * 128], in_=wt[:, g, :], identity=IDN[:])
        return ins
    P.add("pe", sgw_t, reads=[wk, "idn"], writes=pk(b))
    P.add("dve", lambda e, b=b: e.tensor_copy(out=SGWT[:].rearrange("p g q -> p (g q)"), in_=bankbf(b)[:, 0:512]),
          reads=pk(b), writes=["sgwt"])

    fence()

    def load_mod_slots01(l, half, sub):
        load_mod(l, half, sub, (0, 1), ensure=False)

    def load_mod(l, half, sub, slots=(0, 1, 2), ensure=True):
        if ensure:
            ensure_mod(l, sub)
        cond = 0 if half == 0 else 1
        base_v = 0 if sub == 0 else 3
        for slot, v in ((0, base_v + 1), (1, base_v + 0), (2, base_v + 2)):
            if slot not in slots:
                continue
            P.add("sp", lambda e, slot=slot, v=v, l=l, cond=cond: e.dma_start(
                out=MOD[slot][:], in_=MSCR[l, cond:cond + 1, v * D:(v + 1) * D].broadcast_to([128, D])),
                reads=[("mscr", l, 4 * v + q) for q in range(4)], writes=[("mod", slot)], dma=True)

    def rstd_cols(ss, rs, n, keys_r, keys_w):
        P.add("dve", lambda e: e.tensor_scalar(out=rs[:, 0:n], in0=ss[:, 0:n], scalar1=1.0 / D, scalar2=EPS,
                                               op0=ALU.mult, op1=ALU.add), reads=keys_r, writes=keys_w)
        P.add("act", lambda e: e.activation(out=rs[:, 0:n], in_=rs[:, 0:n], func=AF.Sqrt), reads=keys_w, writes=keys_w)
        P.add("dve", lambda e: e.reciprocal(out=rs[:, 0:n], in_=rs[:, 0:n]), reads=keys_w, writes=keys_w)

    def pre_norm_tile(xh, hi, kind, bank_t):
        sc = NSC[kind]
        xi = xh * 8 + hi
        q = hi % 4
        j = hi % 2
        sk, rk = ("ss", hi), ("rs", hi)
        P.add("act", lambda e: e.activation(out=JUNK[:], in_=X[:, xi, :], func=AF.Square,
                                            accum_out=SS[:, hi:hi + 1]), reads=[("X", xi)], writes=["junk", sk])
        P.add("act", lambda e: e.activation(out=RS[:, hi:hi + 1], in_=SS[:, hi:hi + 1], func=AF.Sqrt,
                                            bias=EPSB[:, 0:1], scale=1.0 / D), reads=[sk, "epsb"], writes=[rk])
        P.add("dve", lambda e: e.tensor_tensor(out=sc["tmp"][j][:], in0=X[:, xi, :], in1=MOD[0][:], op=ALU.mult),
              reads=[("X", xi), ("mod", 0)], writes=sc["tk"][j])
        P.add("dve", lambda e: e.reciprocal(out=RS[:, hi:hi + 1], in_=RS[:, hi:hi + 1]), reads=[rk], writes=[rk])
        P.add("dve", lambda e: e.scalar_tensor_tensor(
            out=sc["hb"][q][:], in0=sc["tmp"][j][:], scalar=RS[:, hi:hi + 1], in1=MOD[1][:],
            op0=ALU.mult, op1=ALU.add), reads=sc["tk"][j] + [rk, ("mod", 1)], writes=sc["hk"][q])

        def tail():
            b = bank_t

            def tr(e):
                ins = None
                for kc in range(8):
                    ins = e.transpose(out=bankbf(b)[:, kc * 128:(kc + 1) * 128],
                                      in_=sc["hb"][q][:, kc * 128:(kc + 1) * 128], identity=IDN[:])
                return ins
            P.add("pe", tr, reads=sc["hk"][q] + ["idn"], writes=pk(b))
            P.add("act", lambda e: e.activation(
                out=HT[:, :, hi * 128:(hi + 1) * 128], in_=bankbf(b).rearrange("p (kc t) -> p kc t", kc=8),
                func=AF.Copy), reads=pk(b), writes=[("HT", hi)])
        return tail

    def post_norm(half, i, b):
        t0 = half * 8
        j = i % 2
        P.add("act", lambda e: e.activation(out=JUNK[:], in_=bank(b, 2), func=AF.Square,
                                            accum_out=SS2[:, j:j + 1]), reads=pk(b, 2), writes=["junk", ("ss2", j)])
        P.add("act", lambda e: e.activation(out=RS2[:, j:j + 1], in_=SS2[:, j:j + 1], func=AF.Sqrt,
                                            bias=EPSB[:, 0:1], scale=1.0 / D),
              reads=[("ss2", j), "epsb"], writes=[("rs2", j)])
        P.add("dve", lambda e: e.tensor_tensor(out=bank(b, 2), in0=bank(b, 2), in1=MOD[2][:], op=ALU.mult),
              reads=pk(b, 2) + [("mod", 2)], writes=pk(b, 2))
        P.add("dve", lambda e: e.reciprocal(out=RS2[:, j:j + 1], in_=RS2[:, j:j + 1]),
              reads=[("rs2", j)], writes=[("rs2", j)])
        P.add("dve", lambda e: e.scalar_tensor_tensor(
            out=X[:, t0 + i, :], in0=bank(b, 2), scalar=RS2[:, j:j + 1], in1=X[:, t0 + i, :],
            op0=ALU.mult, op1=ALU.add), reads=pk(b, 2) + [("rs2", j), ("X", t0 + i)], writes=[("X", t0 + i)])

    def out_proj(half, w2d, nxt):
        w0, k0 = ring_load(wtile(w2d, 0, 512), [128, 8, 512])
        w1, k1 = ring_load(wtile(w2d, 512, 1024), [128, 8, 512])
        if nxt is not None:
            load_mod(nxt[0], nxt[1], nxt[2], (0, 1))
            fence()
        pend = []
        for i in range(8):
            b = 2 * (i % 3)

            def mm(e, i=i, b=b):
                ins = None
                for hcol, w in ((0, w0), (1, w1)):
                    for c in range(8):
                        ins = e.matmul(bank(b + hcol), lhsT=OT[:, c, i * 128:(i + 1) * 128], rhs=w[:, c, :],
                                       start=(c == 0), stop=(c == 7))
                return ins
            P.add("pe", mm, reads=[k0, k1] + allk, writes=pk(b, 2))
            if len(pend) > 1:
                pend.pop(0)()
            post_norm(half, i, b)
            if nxt is not None and i >= 4:
                pend.append(pre_norm_tile(nxt[1], i - 4, "HID", 6 + (i % 2)))
        if nxt is not None:
            for i in range(4, 8):
                pend.append(pre_norm_tile(nxt[1], i, "HID", 6 + (i % 2)))
                pend.pop(0)()
        while pend:
            pend.pop(0)()

    def ffn(l, half, nxt):
        wg2, wu2, wd2 = I["ffn_w_gate"][l], I["ffn_w_up"][l], I["ffn_w_down"][l]
        groups = [(g * 4, min(4, NFF - g * 4)) for g in range((NFF + 3) // 4)]
        dgroups = [(0, 2)] + [(2 + g * 4, 4) for g in range(5)]
        for (f0, nf) in groups:
            wg, kg = ring_load(wtile(wg2, f0 * 128, (f0 + nf) * 128), [128, 8, nf * 128])
            wu, ku = ring_load(wtile(wu2, f0 * 128, (f0 + nf) * 128), [128, 8, nf * 128])
            for tt in range(2):
                drain_mod(2)
                for f in range(nf):
                    bg = ps1()
                    bu = ps1()

                    def mm(e, w=wg, f=f, tt=tt, b=bg):
                        ins = None
                        for kc in range(8):
                            ins = e.matmul(bank(b), lhsT=w[:, kc, f * 128:(f + 1) * 128],
                                           rhs=HT[:, kc, tt * 512:(tt + 1) * 512], start=(kc == 0), stop=(kc == 7))
                        return ins
                    htk = [("HT", tt * 4 + q) for q in range(4)]
                    P.add("pe", mm, reads=[kg] + htk, writes=pk(bg))
                    P.add("pe", lambda e, mm=mm, w=wu, f=f, tt=tt, b=bu: mm(e, w, f, tt, b), reads=[ku] + htk, writes=pk(bu))
                    j = st["k"] % 2
                    st["k"] += 1
                    sgk = [("OT", 6 + j)]
                    P.add("act", lambda e, j=j, b=bg: e.activation(out=SG[j][:], in_=bank(b), func=AF.Silu),
                          reads=pk(bg), writes=sgk)
                    P.add("dve", lambda e, j=j, b=bu, ff=f0 + f, tt=tt: e.tensor_tensor(
                        out=HID[:, ff, tt * 512:(tt + 1) * 512], in0=SG[j][:], in1=bank(b), op=ALU.mult),
                        reads=pk(bu) + sgk, writes=[("HID", f0 + f)])
        if nxt is not None:
            load_mod(nxt[0], nxt[1], nxt[2], (0, 1))
        tails = []
        for ps in range(2):
            for gi, (f0, nf) in enumerate(dgroups):
                wd, kd = ring_load(wd2[f0 * 128:(f0 + nf) * 128, :].rearrange("(c p) n -> p c n", p=128),
                                   [128, nf, 1024])
                for ii in range(4):
                    i = ps * 4 + ii

                    def mm(e, wd=wd, f0=f0, nf=nf, i=i, ii=ii, first=(gi == 0), last=(gi == len(dgroups) - 1)):
                        ins = None
                        for c in range(nf):
                            for hcol in range(2):
                                ins = e.matmul(bank(2 * ii + hcol), lhsT=HID[:, f0 + c, i * 128:(i + 1) * 128],
                                               rhs=wd[:, c, hcol * 512:(hcol + 1) * 512],
                                               start=(first and c == 0), stop=(last and c == nf - 1))
                        return ins
                    P.add("pe", mm, reads=[kd] + [("HID", f0 + c) for c in range(nf)], writes=pk(2 * ii, 2))
            for ii in range(4):
                post_norm(half, ps * 4 + ii, 2 * ii)
            while tails:
                tails.pop(0)()
            st["b1"] = 0
            st["b2"] = 0
            if nxt is not None:
                for q in range(4):
                    tails.append(pre_norm_tile(nxt[1], ps * 4 + q, "OT", 2 * q))
        while tails:
            tails.pop(0)()

    def mixer_ab(half, nxt):
        w_in = I["ab_w_in"][0]
        wv, kv = ring_load(wtile(w_in, 512, 1024), [128, 8, 512])
        for i in range(8):
            b = ps1()

            def mm(e, i=i, b=b, w=wv):
                ins = None
                for kc in range(8):
                    ins = e.matmul(bank(b), lhsT=HT[:, kc, i * 128:(i + 1) * 128], rhs=w[:, kc, :],
                                   start=(kc == 0), stop=(kc == 7))
                return ins
            P.add("pe", mm, reads=[kv, ("HT", i)], writes=pk(b))
            if i % 2:
                drain_mod(1)
            P.add("act", lambda e, i=i, b=b: e.activation(out=GV[:, i, :], in_=bank(b), func=AF.Gelu_apprx_tanh,
                                                          accum_out=LNS[:, i, 0:1]),
                  reads=pk(b), writes=[("gv", i), ("lns", i)])
            P.add("dve", lambda e, i=i: e.tensor_scalar(out=LNS[:, i, 2:3], in0=LNS[:, i, 0:1], scalar1=-1.0 / 512,
                                                        scalar2=None, op0=ALU.mult),
                  reads=[("lns", i)], writes=[("lns", i)])
            P.add("act", lambda e, i=i: e.activation(out=JUNK[:, 0:512], in_=GV[:, i, :], func=AF.Square,
                                                     bias=LNS[:, i, 2:3], accum_out=LNS[:, i, 1:2]),
                  reads=[("gv", i), ("lns", i)], writes=["junk", ("lns", i)])
        lk = [("lns", i) for i in range(8)]
        P.add("dve", lambda e: e.tensor_scalar(out=LNS[:, :, 3], in0=LNS[:, :, 1], scalar1=1.0 / 512, scalar2=EPS,
                                               op0=ALU.mult, op1=ALU.add), reads=lk, writes=lk)
        P.add("act", lambda e: e.activation(out=LNS[:, :, 3], in_=LNS[:, :, 3], func=AF.Sqrt), reads=lk, writes=lk)
        P.add("dve", lambda e: e.reciprocal(out=LNS[:, :, 3], in_=LNS[:, :, 3]), reads=lk, writes=lk)
        for i in range(8):
            P.add("dve", lambda e, i=i: e.tensor_scalar(out=VLN[:, i, :], in0=GV[:, i, :], scalar1=LNS[:, i, 2:3],
                                                        scalar2=LNS[:, i, 3:4], op0=ALU.add, op1=ALU.mult),
                  reads=[("gv", i), ("lns", i)], writes=[("vln", i)])
        wb, kb = ring_load(wtile(w_in, 1024, 1536), [128, 8, 512])
        for i in range(8):
            b = ps1()
            P.add("pe", lambda e, mm=mm, i=i, b=b, w=wb: mm(e, i, b, w), reads=[kb, ("HT", i)], writes=pk(b))
            if i % 2:
                drain_mod(1)
            P.add("act", lambda e, i=i, b=b: e.activation(out=ZB[:, i, :], in_=bank(b), func=AF.Copy),
                  reads=pk(b), writes=[("zb", i)])
        wu, ku = ring_load(wtile(w_in, 0, 512), [128, 8, 512])
        for g in range(4):
            for tt in range(2):
                b = ps1()
                j = st["k"] % 2
                st["k"] += 1

                def mmu(e, g=g, tt=tt, b=b):
                    ins = None
                    for kc in range(8):
                        ins = e.matmul(bank(b), lhsT=wu[:, kc, g * 128:(g + 1) * 128],
                                       rhs=HT[:, kc, tt * 512:(tt + 1) * 512], start=(kc == 0), stop=(kc == 7))
                    return ins
                htk = [("HT", tt * 4 + q) for q in range(4)]
                P.add("pe", mmu, reads=[ku] + htk, writes=pk(b))
                if tt == 0:
                    drain_mod(1)
                P.add("act", lambda e, j=j, b=b: e.activation(out=UG[j][:], in_=bank(b), func=AF.Gelu_apprx_tanh),
                      reads=pk(b), writes=[("ug", j)])
                b2 = ps1()

                def mms(e, g=g, tt=tt, b2=b2):
                    ins = None
                    for n in range(4):
                        ins = e.matmul(bank(b2)[:, n * 128:(n + 1) * 128],
                                       lhsT=VLN[:, tt * 4 + n, g * 128:(g + 1) * 128], rhs=SGWT[:, g, :],
                                       start=True, stop=True)
                    return ins
                P.add("pe", mms, reads=["sgwt"] + [("vln", tt * 4 + n) for n in range(4)], writes=pk(b2))
                P.add("dve", lambda e, g=g, j=j, b2=b2: e.scalar_tensor_tensor(
                    out=SGT[j][:].rearrange("p (n q) -> p n q", n=4),
                    in0=bank(b2).rearrange("p (n q) -> p n q", n=4), scalar=SGG[:, g:g + 1],
                    in1=BIASB[:, g:g + 1, :].broadcast_to([128, 4, 128]), op0=ALU.mult, op1=ALU.add),
                    reads=pk(b2) + ["sgg", "biasb"], writes=[("sgt", j)])
                P.add("dve", lambda e, g=g, tt=tt, j=j: e.tensor_tensor(
                    out=OT[:, g, tt * 512:(tt + 1) * 512], in0=SGT[j][:], in1=UG[j][:], op=ALU.mult),
                    reads=[("sgt", j), ("ug", j)], writes=[("OT", g)])
        fpend = None
        if half == 0:
            for pt in range(2):
                ct, kc_ = ring_load(I["c1024"].rearrange("(qc p) n -> p qc n", p=128)[:, :, pt * 512:(pt + 1) * 512],
                                    [128, 8, 512])
                stt, ks_ = ring_load(I["s1024"].rearrange("(qc p) n -> p qc n", p=128)[:, :, pt * 512:(pt + 1) * 512],
                                     [128, 8, 512])
                for g in range(4):
                    s2 = fourier_group(g, pt * 512, [(0, 512, ct, stt, list(range(8)))], [kc_, ks_])
                    if fpend is not None:
                        fpend()
                    fpend = s2
        else:
            ct, kc_ = ring_load(I["c256"].rearrange("(qc p) n -> p qc n", p=128), [128, 2, 256])
            stt, ks_ = ring_load(I["s256"].rearrange("(qc p) n -> p qc n", p=128), [128, 2, 256])
            for pr in range(2):
                for g in range(4):
                    parts = [(jb * 256, 256, ct, stt, [2 * (2 * pr + jb), 2 * (2 * pr + jb) + 1]) for jb in range(2)]
                    s2 = fourier_group(g, pr * 512, parts, [kc_, ks_])
                    if fpend is not None:
                        fpend()
                    fpend = s2
        fpend()
        out_proj(half, I["ab_w_out"][0], nxt)

    def fourier_group(g, col0, parts, wkeys):
        bc = ps1()
        bs = ps1()
        zk = []

        def mmf(e, which, b):
            ins = None
            first = True
            for (co, w, ct, stt, tiles) in parts:
                m = ct if which == 0 else stt
                for qi, ti in enumerate(tiles):
                    ins = e.matmul(bank(b)[:, co:co + w], lhsT=ZB[:, ti, g * 128:(g + 1) * 128], rhs=m[:, qi, 0:w],
                                   start=(qi == 0 and first), stop=(qi == len(tiles) - 1),
                                   skip_group_check=True)
                first = False
            return ins
        for (_, _, _, _, tiles) in parts:
            zk += [("zb", ti) for ti in tiles]
        P.add("pe", lambda e: mmf(e, 0, bc), reads=wkeys + zk, writes=pk(bc))
        P.add("pe", lambda e: mmf(e, 1, bs), reads=wkeys + zk, writes=pk(bs))
        j = st["k"] % 2
        st["k"] += 1
        P.add("act", lambda e: e.activation(out=CZ[j][:, 0, :], in_=bank(bc), func=AF.Copy),
              reads=pk(bc), writes=[("cz", j, 0)])
        P.add("dve", lambda e: e.tensor_copy(out=CZ[j][:, 1, :], in_=bank(bs)), reads=pk(bs), writes=[("cz", j, 1)])
        def stage2():
            bo = ps1()

            def mmc(e):
                e.matmul(bank(bo), lhsT=CCS[:], rhs=CZ[j][:, 0, :], start=True, stop=False)
                return e.matmul(bank(bo), lhsT=NSCS[:], rhs=CZ[j][:, 1, :], start=False, stop=True)
            P.add("pe", mmc, reads=["ccs", "nscs", ("cz", j, 0), ("cz", j, 1)], writes=pk(bo))
            P.add("act", lambda e: e.activation(out=OT[:, 4 + g, col0:col0 + 512], in_=bank(bo), func=AF.Copy),
                  reads=pk(bo), writes=[("OT", 4 + g)])
        return stage2

    def mixer_attn(half, nxt):
        wqkv = I["attn_w_qkv"][0]
        sample = (half == 0)
        scale = 1.0 / np.sqrt(128.0)
        wkv, kkv = ring_load(wtile(wqkv, 1024, 1536), [128, 8, 512])
        if sample:
            P.add("sp", lambda e: e.dma_start(out=ROPE[:, 0, :], in_=I["cosT"]), writes=["rope"], dma=True)
            P.add("sp", lambda e: e.dma_start(out=ROPE[:, 1, :], in_=I["ssinT"]), writes=["rope"], dma=True)
            P.add("pool", lambda e: e.dma_start(out=CKB[:], in_=I["ck"].rearrange("(t p) c -> p t c", p=128)),
                  writes=["ckb"], dma=True)
            P.add("pool", lambda e: e.dma_start(out=VT[:, 8:10, :], in_=I["cv"].rearrange("(t p) c -> p t c", p=128)),
                  writes=[("vt", 8), ("vt", 9)], dma=True)
            b = ps1()

            def trk(e, b=b):
                ins = None
                for gg in range(2):
                    for t in range(2):
                        ins = e.transpose(out=bankbf(b)[:, (gg * 2 + t) * 128:(gg * 2 + t + 1) * 128],
                                          in_=CKB[:, t, gg * 128:(gg + 1) * 128], identity=IDN[:])
                return ins
            P.add("pe", trk, reads=["ckb", "idn"], writes=pk(b))
            P.add("act", lambda e, b=b: e.activation(
                out=KT[:, :, 1024:1280], in_=bankbf(b)[:, 0:512].rearrange("p (g t) -> p g t", g=2), func=AF.Copy),
                reads=pk(b), writes=[("ktc", 0), ("ktc", 1)])
        for i in range(8):
            b = ps1()

            def mm(e, i=i, b=b):
                ins = None
                for kc in range(8):
                    ins = e.matmul(bank(b), lhsT=HT[:, kc, i * 128:(i + 1) * 128], rhs=wkv[:, kc, :],
                                   start=(kc == 0), stop=(kc == 7))
                return ins
            P.add("pe", mm, reads=[kkv, ("HT", i)], writes=pk(b))
            P.add("act", lambda e, i=i, b=b: e.activation(out=VT[:, i, :], in_=bank(b)[:, 256:512], func=AF.Copy),
                  reads=pk(b), writes=[("vt", i)])
            drain_mod(1)
            if not sample:
                j = i % 2
                P.add("dve", lambda e, j=j, b=b: e.tensor_copy(out=KVST[j][:], in_=bank(b)),
                      reads=pk(b), writes=[("kvst", j)])
                P.add("sp", lambda e, j=j, i=i: e.dma_start(out=O["sk"][i * 128:(i + 1) * 128, :], in_=KVST[j][:, 0:256]),
                      reads=[("kvst", j)], writes=[("o_sk", i)], dma=True)
                P.add("sp", lambda e, j=j, i=i: e.dma_start(out=O["sv"][i * 128:(i + 1) * 128, :], in_=KVST[j][:, 256:512]),
                      reads=[("kvst", j)], writes=[("o_sv", i)], dma=True)
        rope_pend = []

        def proj_fm(dst, dkeys, w, wkey, rope, c0, tt):
            htk = [("HT", tt * 4 + q) for q in range(4)]
            b = ps1()

            def mmq(e):
                ins = None
                for kc in range(8):
                    ins = e.matmul(bank(b), lhsT=w[:, kc, c0:c0 + 128], rhs=HT[:, kc, tt * 512:(tt + 1) * 512],
                                   start=(kc == 0), stop=(kc == 7))
                return ins
            P.add("pe", mmq, reads=[wkey] + htk, writes=pk(b))
            if not rope:
                P.add("act", lambda e: e.activation(out=dst, in_=bank(b), func=AF.Copy), reads=pk(b), writes=dkeys)
                return
            jq = st["k"] % 2
            st["k"] += 1
            qk_ = [("expb", jq)]
            P.add("act", lambda e: e.activation(out=EXPB[jq][:], in_=bank(b), func=AF.Copy), reads=pk(b), writes=qk_)
            if rope_pend:
                rope_pend.pop(0)()

            def fin_rope():
                bp = ps1()
                P.add("pe", lambda e: e.matmul(bank(bp), lhsT=RPM[:], rhs=EXPB[jq][:], start=True, stop=True),
                      reads=qk_ + ["rpm"], writes=pk(bp))
                P.add("dve", lambda e: e.tensor_tensor(out=RT[0][:], in0=bank(b), in1=ROPE[:, 0, tt * 512:(tt + 1) * 512],
                                                       op=ALU.mult), reads=pk(b) + ["rope"], writes=[("rt", 0)])
                P.add("dve", lambda e: e.tensor_tensor(out=RT[1][:], in0=bank(bp),
                                                       in1=ROPE[:, 1, tt * 512:(tt + 1) * 512], op=ALU.mult),
                      reads=pk(bp) + ["rope"], writes=[("rt", 1)])
                P.add("dve", lambda e: e.tensor_tensor(out=dst, in0=RT[0][:], in1=RT[1][:], op=ALU.add),
                      reads=[("rt", 0), ("rt", 1)], writes=dkeys)
            rope_pend.append(fin_rope)

        def rope_flush():
            while rope_pend:
                rope_pend.pop(0)()

        for gg in range(2):
            for tt in range(2):
                proj_fm(KT[:, gg, tt * 512:(tt + 1) * 512], [("kt", gg, tt)], wkv, kkv, sample, gg * 128, tt)
                drain_mod(1)
        for gg in range(2):
            wq, kq = ring_load(wtile(wqkv, gg * 512, (gg + 1) * 512), [128, 8, 512])
            for hh in range(4):
                for tt in range(2):
                    proj_fm(QT[:, hh, tt * 512:(tt + 1) * 512], [("qt", hh, tt)], wq, kq, sample, hh * 128, tt)
            rope_flush()
            fin = None
            for hh in range(4):
                h = gg * 4 + hh
                for qt in range(2):
                    fin = attend(sample, gg, hh, h, qt, scale, fin)
            fin()
        out_proj(half, I["attn_w_o"][0], nxt)

    def attend(sample, gg, hh, h, qt, scale, prev_fin):
        a = st.get("att", 0)
        st["att"] = a + 1
        bo = 2 * (a % 2)
        bd = bo + 1

        def ps1():
            b_ = 4 + st.get("bs", 0) % 4
            st["bs"] = st.get("bs", 0) + 1
            return b_
        items = []
        if sample:
            for jc in range(2):
                items.append((KT[:, gg, 1024 + jc * 128:1024 + (jc + 1) * 128], VT[:, 8 + jc, gg * 128:(gg + 1) * 128],
                              0, 512, [("ktc", gg), ("vt", 8 + jc)], []))
            for j in range(max(0, 4 * qt - 1), min(8, 4 * qt + 5)):
                ilo = max(j - 1, 4 * qt)
                ihi = min(j + 1, 4 * qt + 3)
                lo = (ilo - 4 * qt) * 128
                hi = (ihi - 4 * qt + 1) * 128
                ms = []
                if ilo == j - 1:
                    ms.append((0, 0))
                if ihi == j + 1:
                    ms.append((hi - lo - 128, 1))
                items.append((KT[:, gg, j * 128:(j + 1) * 128], VT[:, j, gg * 128:(gg + 1) * 128], lo, hi,
                              [("kt", gg, j // 4), ("vt", j)], ms))
        else:
            for jb in range(2):
                bt = 2 * qt + jb
                for kt_ in range(2):
                    ti = 2 * bt + kt_
                    items.append((KT[:, gg, ti * 128:(ti + 1) * 128], VT[:, ti, gg * 128:(gg + 1) * 128],
                                  jb * 256, jb * 256 + 256, [("kt", gg, ti // 4), ("vt", ti)], []))
        first = True
        pend = []
        for (kap, vap, lo, hi, keys, ms) in items:
            n = hi - lo
            bs = ps1()
            P.add("pe", lambda e, kap=kap, lo=lo, hi=hi, bs=bs: e.matmul(
                bank(bs)[:, 0:hi - lo], lhsT=kap, rhs=QT[:, hh, qt * 512 + lo:qt * 512 + hi], start=True, stop=True),
                reads=[keys[0], ("qt", hh, qt)], writes=pk(bs))
            j = st["k"] % 3
            st["k"] += 1
            ek = [("expb", j)]
            P.add("act", lambda e, j=j, bs=bs, n=n: e.activation(out=EXPB[j][:, 0:n], in_=bank(bs)[:, 0:n],
                                                                 func=AF.Exp, scale=float(scale)),
                  reads=pk(bs), writes=ek)
            for (co, which) in ms:
                P.add("dve", lambda e, j=j, co=co, which=which: e.tensor_tensor(
                    out=EXPB[j][:, co:co + 128], in0=EXPB[j][:, co:co + 128],
                    in1=MASKS[:, which * 128:(which + 1) * 128], op=ALU.mult), reads=ek + ["masks"], writes=ek)

            def pv(e, vap=vap, j=j, lo=lo, hi=hi, first=first):
                e.matmul(bank(bo)[:, lo:hi], lhsT=vap, rhs=EXPB[j][:, 0:hi - lo], start=first, stop=False,
                         skip_group_check=True)
                return e.matmul(bank(bd)[:, lo:hi], lhsT=ONES[:], rhs=EXPB[j][:, 0:hi - lo], start=first, stop=False,
                                skip_group_check=True)
            pend.append((pv, ek + [keys[1], "ones"]))
            if len(pend) > 2:
                pv_, rd_ = pend.pop(0)
                P.add("pe", pv_, reads=rd_, writes=pk(bo) + pk(bd))
                if prev_fin is not None:
                    prev_fin()
                    prev_fin = None
            first = False
        while pend:
            pv_, rd_ = pend.pop(0)
            P.add("pe", pv_, reads=rd_, writes=pk(bo) + pk(bd))
        if prev_fin is not None:
            prev_fin()
        j = st["k"] % 2
        st["k"] += 1
        rk = [("rcp", j)]

        def fin():
            P.add("act", lambda e: e.activation(out=RCP[j][:], in_=bank(bd), func=AF.Ln, bias=SINKE[:, h:h + 1]),
                  reads=pk(bd) + ["sinke"], writes=rk)
            P.add("act", lambda e: e.activation(out=RCP[j][:], in_=RCP[j][:], func=AF.Exp, scale=-1.0),
                  reads=rk, writes=rk)
            P.add("dve", lambda e: e.tensor_tensor(out=OT[:, h, qt * 512:(qt + 1) * 512], in0=bank(bo), in1=RCP[j][:],
                                                   op=ALU.mult), reads=pk(bo) + rk, writes=[("OT", h)])
        return fin

    if phases is None:
        phases = [(l, half, sub) for half in range(2) for l in range(2) for sub in range(2)]
    for idx, (l, half, sub) in enumerate(phases):
        nxt = phases[idx + 1] if idx + 1 < len(phases) else None
        if idx == 0:
            load_x(range(half * 8, half * 8 + 8))
            mod_group_fast(l, sub, after_first=lambda: load_mod_slots01(l, half, sub))
            load_small_consts()
            load_x(range((1 - half) * 8, (1 - half) * 8 + 8))
            pend = []
            for i in range(8):
                pend.append(pre_norm_tile(half, i, "OT", i % 8))
                if len(pend) > 2:
                    pend.pop(0)()
            while pend:
                pend.pop(0)()
        load_mod(l, half, sub, (2,))
        fence()
        if sub == 0:
            if l == 0:
                mixer_ab(half, nxt)
            else:
                mixer_attn(half, nxt)
        else:
            ffn(l, half, nxt)
        st["b1"] = 0
        st["b2"] = 0
        if not any(ph[1] == half for ph in phases[idx + 1:]):
            store_x(half)
    for hf in range(2):
        if not any(ph[1] == hf for ph in phases):
            store_x(hf)
    outk = [("o_y", i) for i in range(16)]
    outk += [("o_sk", i) for i in range(8)] + [("o_sv", i) for i in range(8)]
    P.add("sp", lambda e: e.nop(), reads=outk)

    P.analyze()
    with contextlib.ExitStack() as stack:
        esems = {k: stack.enter_context(nc.semaphore("es_" + k)) for k in Prog.ENGS}
        dsems = [stack.enter_context(nc.semaphore("ds_%d" % i)) for i in range(P.ndsems)]
        block = stack.enter_context(nc.Block())
        P.emit(block, esems, dsems)
    return nc


_CONSTS = None


def make_in_maps(inputs):
    global _CONSTS
    if _CONSTS is None:
        _CONSTS = _consts()
    f = lambda a: np.ascontiguousarray(np.asarray(a, dtype=np.float32))
    shared = {k: f(inputs[k]) for k in ("mod_w", "mod_b", "norm_pre_mix", "norm_post_mix", "norm_pre_ffn",
                                        "norm_post_ffn", "ab_w_in", "sgu_w", "sgu_b", "sgu_g", "ab_w_out",
                                        "attn_w_qkv", "attn_sink", "attn_w_o", "ffn_w_gate", "ffn_w_up",
                                        "ffn_w_down")}
    shared.update(_CONSTS)
    xs, xp = f(inputs["x_sample"]), f(inputs["x_prompt"])
    ck, cv = f(inputs["cache_k"]), f(inputs["cache_v"])
    c, cctx = f(inputs["c"]), f(inputs["c_ctx"])
    maps = []
    for b in range(NCORES):
        m = dict(shared)
        m["xs"] = xs[b]
        m["xp"] = np.ascontiguousarray(xp[4 * b:4 * b + 4].reshape(1024, D))
        m["ck"] = np.ascontiguousarray(ck[b, 0].reshape(256, 256))
        m["cv"] = np.ascontiguousarray(cv[b, 0].reshape(256, 256))
        m["cond"] = np.ascontiguousarray(np.stack([c[b], cctx], axis=0))
        maps.append(m)
    return maps


def gather(results):
    ys = np.stack([r["ys"] for r in results], axis=0).astype(np.float32)
    yp = np.concatenate([r["yp"].reshape(4, 256, D) for r in results], axis=0).astype(np.float32)
    sk = np.concatenate([r["sk"].reshape(4, 1, 256, 2, 128) for r in results], axis=0).astype(np.float32)
    sv = np.concatenate([r["sv"].reshape(4, 1, 256, 2, 128) for r in results], axis=0).astype(np.float32)
    return (yp, ys, sk, sv)


def kernel(**inputs):
    nc = build()
    res = run_bass_kernel_spmd(nc, make_in_maps(inputs), core_ids=list(range(NCORES)))
    return gather(res.results)
```

```python
import contextlib
import numpy as np
import ml_dtypes
import concourse.bass as bass
import concourse.mybir as mybir
from concourse.bass_utils import run_bass_kernel_spmd

F32 = mybir.dt.float32
BF16 = mybir.dt.bfloat16
U8 = mybir.dt.uint8
AF = mybir.ActivationFunctionType
ALU = mybir.AluOpType

D = 1024
DFF = 2816
NFF = DFF // 128
EPS = 1e-6
NCORES = 8


class Op:
    __slots__ = ("eng", "fn", "reads", "writes", "dma", "idx", "waits", "clock",
                 "signal", "sem", "semval", "gidx")

    def __init__(self, eng, fn, reads, writes, dma):
        self.eng = eng
        self.fn = fn
        self.reads = reads
        self.writes = writes
        self.dma = dma
        self.waits = None
        self.clock = None
        self.signal = False
        self.sem = None
        self.semval = None


class RingKey(tuple):
    gen = None


class Prog:
    ENGS = ("pe", "act", "dve", "pool", "sp")
    MAX_DMA_SEMS = 72

    def __init__(self, nc):
        self.nc = nc
        self.ops = []
        self.byeng = {e: [] for e in self.ENGS}
        self.ring_gen = {}

    def add(self, eng, fn, reads=(), writes=(), dma=False):
        for r in reads:
            assert getattr(r, "gen", None) is None or self.ring_gen[r[1]] == r.gen, ("stale ring tile", r)
        op = Op(eng, fn, tuple(reads), tuple(writes), dma)
        op.gidx = len(self.ops)
        op.idx = len(self.byeng[eng])
        self.ops.append(op)
        self.byeng[eng].append(op)
        return op

    def analyze(self):
        last_w = {}
        readers = {}
        clock = {e: {} for e in self.ENGS}
        dsems = []
        for op in self.ops:
            E = op.eng
            deps = {}
            for r in op.reads:
                d = last_w.get(r)
                if d is not None:
                    deps[d.gidx] = (d, True)
                if isinstance(r, tuple) and r[0] == "ps":
                    for d in readers.get(r, ()):
                        if d.eng != E and d.gidx not in deps:
                            deps[d.gidx] = (d, False)
            for w in op.writes:
                d = last_w.get(w)
                if d is not None and d.gidx not in deps:
                    deps[d.gidx] = (d, False)
                for d in readers.get(w, ()):
                    if d.gidx not in deps:
                        deps[d.gidx] = (d, False)
            ck = clock[E]
            need = {}
            for d, raw in deps.values():
                if d is op:
                    continue
                if d.dma:
                    key, val = ("dma", d.sem), d.semval
                else:
                    if d.eng == E and not op.dma and E == "pe":
                        continue
                    key, val = d.eng, d.idx + 1
                if ck.get(key, 0) >= val:
                    continue
                if key not in need or need[key][0] < val:
                    need[key] = (val, d)
            if op.dma:
                chosen = None
                cls = "sw" if E == "pool" else "hw"
                for si, (tot, lop, scls) in enumerate(dsems):
                    if scls != cls:
                        continue
                    k = ("dma", si)
                    have = max(ck.get(k, 0), need[k][0] if k in need else 0)
                    if have >= tot:
                        chosen = si
                        break
                if chosen is None:
                    if len(dsems) < self.MAX_DMA_SEMS:
                        dsems.append([0, None, cls])
                        chosen = len(dsems) - 1
                    else:
                        chosen = min((i for i in range(len(dsems)) if dsems[i][2] == cls),
                                     key=lambda i: dsems[i][1].gidx)
                        tot, lop, _ = dsems[chosen]
                        need[("dma", chosen)] = (tot, lop)
                op.sem = chosen
                dsems[chosen][0] += 16
                dsems[chosen][1] = op
                op.semval = dsems[chosen][0]
                op.signal = True
            if need:
                ck = dict(ck)
                for key, (val, d) in need.items():
                    d.signal = True
                    for k2, v2 in d.clock.items():
                        if ck.get(k2, 0) < v2:
                            ck[k2] = v2
                    if ck.get(key, 0) < val:
                        ck[key] = val
                clock[E] = ck
            op.waits = [(key, d) for key, (val, d) in need.items()]
            op.clock = ck
            for r in op.reads:
                readers.setdefault(r, []).append(op)
            for w in op.writes:
                last_w[w] = op
                readers[w] = []
        self.ndsems = len(dsems)
        for e in self.ENGS:
            n = 0
            for op in self.byeng[e]:
                if op.dma:
                    continue
                if op.signal:
                    n += 1
                    op.semval = n

    def emit(self, block, esems, dsems):
        engobj = {"pe": "tensor", "act": "scalar", "dve": "vector", "pool": "gpsimd", "sp": "sync"}

        def make(ename):
            ops = self.byeng[ename]

            def body(e):
                for op in ops:
                    for key, d in op.waits:
                        if isinstance(key, tuple):
                            e.wait_ge(dsems[key[1]], d.semval)
                        else:
                            e.wait_ge(esems[key], d.semval)
                    ins = op.fn(e)
                    assert ins is not None, (ename, op.reads, op.writes)
                    if op.signal:
                        if op.dma:
                            ins.then_inc(dsems[op.sem], 16)
                        else:
                            ins.then_inc(esems[ename], 1)
            return body

        for ename in self.ENGS:
            getattr(block, engobj[ename])(make(ename))


def _dft(n, scale, kind):
    idx = (np.arange(n)[:, None].astype(np.int64) * np.arange(n)[None, :].astype(np.int64)) % n
    ang = 2.0 * np.pi * idx.astype(np.float64) / n
    m = np.cos(ang) if kind == "c" else np.sin(ang)
    return (m * scale).astype(np.float32)


def _consts():
    bf = ml_dtypes.bfloat16
    c = {}
    c["ident"] = np.eye(128, dtype=np.float32).astype(bf)
    c["ones"] = np.ones((128, 128), dtype=np.float32).astype(bf)
    c["identf"] = np.eye(16, dtype=np.float32)
    dd = np.arange(128)
    perm = np.where((dd % 64) < 32, dd + 32, dd - 32)
    rpm = np.zeros((128, 128), dtype=np.float32)
    rpm[perm, dd] = 1.0
    c["rpm"] = rpm.astype(bf)
    sl = np.arange(128)[:, None]
    ql = np.arange(128)[None, :]
    c["masks"] = np.concatenate([(sl <= ql), (ql <= sl)], axis=1).astype(np.float32).astype(bf)
    c["c1024"] = _dft(1024, 1.0 / 32.0, "c").astype(bf)
    c["s1024"] = _dft(1024, 1.0 / 32.0, "s").astype(bf)
    c["c256"] = _dft(256, 1.0 / 16.0, "c").astype(bf)
    c["s256"] = _dft(256, 1.0 / 16.0, "s").astype(bf)
    c["cc"] = _dft(128, 1.0 / np.sqrt(128.0), "c").astype(bf)
    c["nsc"] = (-_dft(128, 1.0 / np.sqrt(128.0), "s")).astype(bf)
    t = 1024
    rows = np.repeat(np.arange(t // 64), 64).astype(np.float32)
    cols = np.tile(np.arange(64), t // 64).astype(np.float32)
    inv = (np.float32(10000.0) ** (-np.arange(0, 64, 2, dtype=np.float32) / np.float32(64))).astype(np.float32)
    ar = rows[:, None] * inv
    ac = cols[:, None] * inv
    ang = np.concatenate([ar, ar, ac, ac], axis=-1).astype(np.float32)
    sign = np.where((np.arange(128) % 64) < 32, -1.0, 1.0).astype(np.float32)
    c["cosT"] = np.ascontiguousarray(np.cos(ang).T.astype(np.float32))
    c["ssinT"] = np.ascontiguousarray((np.sin(ang) * sign[None, :]).T.astype(np.float32))
    return c


_IN_SHAPES = {
    "xs": ([1024, D], F32), "xp": ([1024, D], F32), "ck": ([256, 256], F32), "cv": ([256, 256], F32),
    "cond": ([2, D], F32),
    "mod_w": ([2, D, 6 * D], F32), "mod_b": ([2, 6 * D], F32),
    "norm_pre_mix": ([2, D], F32), "norm_post_mix": ([2, D], F32),
    "norm_pre_ffn": ([2, D], F32), "norm_post_ffn": ([2, D], F32),
    "ab_w_in": ([1, D, 1536], F32), "sgu_w": ([1, 4, 128, 128], F32), "sgu_b": ([1, 4, 128], F32),
    "sgu_g": ([1, 512], F32), "ab_w_out": ([1, D, D], F32),
    "attn_w_qkv": ([1, D, 1536], F32), "attn_sink": ([1, 8], F32), "attn_w_o": ([1, D, D], F32),
    "ffn_w_gate": ([2, D, DFF], F32), "ffn_w_up": ([2, D, DFF], F32), "ffn_w_down": ([2, DFF, D], F32),
    "ident": ([128, 128], BF16), "identf": ([16, 16], F32), "rpm": ([128, 128], BF16), "ones": ([128, 128], BF16), "masks": ([128, 256], BF16),
    "c1024": ([1024, 1024], BF16), "s1024": ([1024, 1024], BF16),
    "c256": ([256, 256], BF16), "s256": ([256, 256], BF16),
    "cc": ([128, 128], BF16), "nsc": ([128, 128], BF16),
    "cosT": ([128, 1024], F32), "ssinT": ([128, 1024], F32),
}
_OUT_SHAPES = {"ys": [1024, D], "yp": [1024, D], "sk": [1024, 256], "sv": [1024, 256]}


def build(phases=None):
    nc = bass.Bass("TRN2", target_bir_lowering=False)
    I = {n: nc.dram_tensor(n, s, dt, kind="ExternalInput").ap() for n, (s, dt) in _IN_SHAPES.items()}
    O = {n: nc.dram_tensor(n, s, F32, kind="ExternalOutput").ap() for n, s in _OUT_SHAPES.items()}
    MSCR = nc.dram_tensor("mscr", [2, 2, 6 * D], F32).ap()

    rem = nc.sbuf_bytes_remaining
    arena = nc.alloc_sbuf_tensor("arena", [128, rem - 2048], U8)
    base = nc.lookup_mloc(arena).addr
    limit = base + rem - 2048
    cur = [base]

    def alloc(name, shape, dt, at=None):
        nbytes = int(np.prod(shape[1:])) * (4 if dt == F32 else 2)
        if at is None:
            at = cur[0]
            cur[0] += (nbytes + 31) // 32 * 32
            assert cur[0] <= limit, (name, cur[0], limit)
        return nc.alloc_sbuf_tensor_at(name, list(shape), dt, offset=at)

    X = alloc("X", [128, 16, D], F32)
    HT = alloc("HT", [128, 8, 1024], BF16)
    OT_off = cur[0]
    OT = alloc("OT", [128, 8, 1024], BF16)
    HID_off = cur[0]
    HIDB = 45056
    HID = alloc("HID", [128, NFF, 1024], BF16)
    cur[0] = HID_off + HIDB
    cur_ring0 = cur[0]
    RING = [alloc("ring%d" % i, [128, 4096], BF16) for i in range(4)]
    MOD = [alloc("mod%d" % i, [128, D], F32) for i in range(3)]
    IDN = alloc("idn", [128, 128], BF16)
    ONES = alloc("ones_sb", [128, 128], BF16)
    MASKS = alloc("masks_sb", [128, 256], BF16)
    JUNK = alloc("junk", [128, D], BF16)
    SS = alloc("ss", [128, 8], F32)
    RS = alloc("rs", [128, 8], F32)
    SS2 = alloc("ss2", [128, 2], F32)
    RS2 = alloc("rs2", [128, 2], F32)
    SCT0 = alloc("sct0", [128, 8, 2], F32)
    C16 = alloc("c16", [16, 128], F32)
    IDF = alloc("idf", [16, 16], F32)
    RPM = alloc("rpm_sb", [128, 128], BF16)
    SCT = alloc("sct", [128, 8, 2], BF16)
    SINKE = alloc("sinke", [128, 8], F32)
    SGG = alloc("sgg", [128, 4], F32)
    BIASB = alloc("biasb", [128, 4, 128], F32)
    SGWT = alloc("sgwt", [128, 4, 128], BF16)
    CCS = alloc("ccs", [128, 128], BF16)
    NSCS = alloc("nscs", [128, 128], BF16)
    LNS = alloc("lns", [128, 8, 4], F32)
    MRW = [alloc("mrw%d" % j, [2, 3, 256], F32) for j in range(2)]
    MW = [alloc("mw%d" % j, [128, 8, 256], BF16) for j in range(2)]
    NSC = {}
    for kind, off0 in (("OT", OT_off), ("HID", HID_off)):
        NSC[kind] = dict(
            tmp=[alloc("tmp%s%d" % (kind, j), [128, D], F32, at=off0 + 4096 * j) for j in range(2)],
            hb=[alloc("hb%s%d" % (kind, j), [128, D], BF16, at=off0 + 8192 + 2048 * j) for j in range(4)],
            tk=[[(kind, 2 * j), (kind, 2 * j + 1)] for j in range(2)],
            hk=[[(kind, 4 + j)] for j in range(4)])
    SG = [alloc("sg%d" % j, [128, 512], F32, at=OT_off + 12288 + 2048 * j) for j in range(2)]
    VLN = alloc("vln", [128, 8, 512], BF16, at=HID_off)
    ZB = alloc("zb", [128, 8, 512], BF16, at=HID_off + 8192)
    GV = alloc("gv", [128, 8, 512], F32, at=HID_off + 16384)
    UG = [alloc("ug%d" % j, [128, 512], F32, at=HID_off + 32768 + 2048 * j) for j in range(2)]
    CZ = [alloc("cz%d" % j, [128, 2, 512], BF16, at=HID_off + 36864 + 2048 * j) for j in range(2)]
    SGT = [alloc("sgt%d" % j, [128, 512], F32, at=HID_off + 40960 + 2048 * j) for j in range(2)]
    QT = alloc("qt", [128, 4, 1024], BF16, at=HID_off)
    KT = alloc("kt", [128, 2, 1280], BF16, at=HID_off + 8192)
    VT = alloc("vt", [128, 10, 256], BF16, at=HID_off + 13312)
    ROPE = alloc("rope", [128, 2, 1024], F32, at=HID_off + 18432)
    EXPB = [alloc("expb%d" % j, [128, 512], BF16, at=HID_off + 26624 + 1024 * j) for j in range(3)]
    RCP = [alloc("rcp%d" % j, [128, 512], F32, at=HID_off + 29696 + 2048 * j) for j in range(2)]
    KVST = [alloc("kvst%d" % j, [128, 512], F32, at=HID_off + 33792 + 2048 * j) for j in range(2)]
    RT = [alloc("rt%d" % j, [128, 512], F32, at=HID_off + 37888 + 2048 * j) for j in range(2)]
    CKB = alloc("ckb", [128, 2, 256], BF16, at=HID_off + 41984)
    PSUM = nc.alloc_psum_tensor("psum", [128, 4096], F32)

    def bank(b, n=1):
        return PSUM[:, b * 512:(b + n) * 512]

    def bankbf(b):
        return PSUM[:, b * 512:(b + 1) * 512].bitcast(BF16)

    EPSB = alloc("epsb", [128, 1], F32)
    P = Prog(nc)
    st = {"ring": 0, "b1": 0, "b2": 0, "k": 0, "mk": 0}
    FENCE = alloc("fence", [128, 1], F32)
    hidk = [("HID", c) for c in range(NFF)] + ["hidx"]
    REGK = list(hidk)
    REGK += [(n, i) for n in ("vln", "zb", "gv") for i in range(8)]
    REGK += [(n, j) for n in ("ug", "sgt", "rcp", "kvst") for j in range(2)]
    REGK += [("cz", j, c) for j in range(2) for c in range(2)]
    REGK += [("qt", h, t) for h in range(4) for t in range(2)] + [("kt", g, t) for g in range(2) for t in range(2)]
    REGK += [("ktc", g) for g in range(2)] + [("vt", i) for i in range(10)] + ["rope", "ckb"]
    REGK += [("expb", j) for j in range(3)] + [("rt", j) for j in range(2)]

    def fence():
        P.add("dve", lambda e: e.memset(FENCE[:], 0.0), writes=REGK)

    def ps1():
        b = st["b1"] % 8
        st["b1"] += 1
        return b

    def ps2():
        b = (st["b2"] % 4) * 2
        st["b2"] += 1
        return b

    def pk(b, n=1):
        return [("ps", b + i) for i in range(n)]

    def ring_load(src, shape, eng="pool"):
        s = st["ring"] % 4
        st["ring"] += 1
        n = int(np.prod(shape[1:]))
        assert n <= 4096
        if len(shape) == 3:
            dst = RING[s][:, 0:n].rearrange("p (a b) -> p a b", a=shape[1])
        else:
            dst = RING[s][:, 0:n]
        key = RingKey(("ring", s))
        key.gen = P.ring_gen[s] = P.ring_gen.get(s, 0) + 1
        P.add(eng, lambda e, dst=dst, src=src: e.dma_start(out=dst, in_=src), writes=[key], dma=True)
        return dst, key

    def wtile(w2d, c0, c1, r0=0, nk=8):
        return w2d[r0:r0 + nk * 128, :].rearrange("(kc p) n -> p kc n", p=128)[:, :, c0:c1]

    P.add("act", lambda e: e.dma_start(out=C16[:], in_=I["cond"].rearrange("c (kc p) -> (c kc) p", p=128)),
          writes=["c16"], dma=True)
    P.add("act", lambda e: e.dma_start(out=IDF[:], in_=I["identf"]), writes=["idf"], dma=True)
    P.add("sp", lambda e: e.dma_start(out=IDN[:], in_=I["ident"]), writes=["idn"], dma=True)
    P.add("sp", lambda e: e.dma_start(out=ONES[:], in_=I["ones"]), writes=["ones"], dma=True)
    P.add("sp", lambda e: e.dma_start(out=RPM[:], in_=I["rpm"]), writes=["rpm"], dma=True)
    P.add("sp", lambda e: e.dma_start(out=MASKS[:], in_=I["masks"]), writes=["masks"], dma=True)
    P.add("sp", lambda e: e.dma_start(out=CCS[:], in_=I["cc"]), writes=["ccs"], dma=True)
    P.add("sp", lambda e: e.dma_start(out=NSCS[:], in_=I["nsc"]), writes=["nscs"], dma=True)
    P.add("sp", lambda e: e.dma_start(out=SINKE[:], in_=I["attn_sink"][0:1, :].broadcast_to([128, 8])),
          writes=["sinke"], dma=True)

    def load_small_consts():
        P.add("sp", lambda e: e.dma_start(out=SGG[:], in_=I["sgu_g"][0].rearrange("(g p) -> p g", p=128),
                                          allow_slow_non_contiguous=True), writes=["sgg"], dma=True)
        P.add("sp", lambda e: e.dma_start(
            out=BIASB[:], in_=I["sgu_b"][0:1].broadcast_to([128, 4, 128])), writes=["biasb"], dma=True)
    P.add("dve", lambda e: e.memset(EPSB[:], EPS), writes=["epsb"])

    def store_x(half):
        for i in range(half * 8, half * 8 + 8):
            dst = O["ys"] if i < 8 else O["yp"]
            r = (i % 8) * 128
            P.add("sp", lambda e, i=i, dst=dst, r=r: e.dma_start(out=dst[r:r + 128, :], in_=X[:, i, :]),
                  reads=[("X", i)], writes=[("o_y", i)], dma=True)

    def load_x(tiles):
        for i in tiles:
            src = I["xs"] if i < 8 else I["xp"]
            r = (i % 8) * 128
            P.add("sp", lambda e, i=i, src=src, r=r: e.dma_start(out=X[:, i, :], in_=src[r:r + 128, :]),
                  writes=[("X", i)], dma=True)

    P.add("act", lambda e: e.activation(out=SINKE[:], in_=SINKE[:], func=AF.Exp), reads=["sinke"], writes=["sinke"])
    P.add("pe", lambda e: e.transpose(out=PSUM[:, 0:16], in_=C16[:], identity=IDF[:]), reads=["c16", "idf"],
          writes=[("ps", 0)])
    P.add("act", lambda e: e.activation(out=SCT[:].rearrange("p kc c -> p c kc"),
                                        in_=PSUM[:, 0:16].rearrange("p (c kc) -> p c kc", c=2), func=AF.Silu),
          reads=[("ps", 0)], writes=["sct"])

    norm_of_chunk = {1: "norm_pre_mix", 2: "norm_post_mix", 4: "norm_pre_ffn", 5: "norm_post_ffn"}
    allk = [("OT", c) for c in range(8)]
    MBW = 256
    NMB = 6 * D // MBW
    mod_pending = [(l, cb) for l in range(2) for cb in range(NMB)]

    def mod_block(l, cb):
        j = st["mk"] % 2
        st["mk"] += 1
        mk = [("mrow", j)]
        wk = ("mw", j)
        wt = MW[j]
        c0 = cb * MBW
        P.add("pool", lambda e: e.dma_start(out=wt[:], in_=wtile(I["mod_w"][l], c0, c0 + MBW)), writes=[wk], dma=True)
        chunk, hc = c0 // D, (c0 % D)
        P.add("sp", lambda e: e.dma_start(out=MRW[j][:, 0, :], in_=I["mod_b"][l:l + 1, c0:c0 + MBW]
                                          .broadcast_to([2, MBW])), writes=mk, dma=True)
        if chunk in norm_of_chunk:
            P.add("sp", lambda e: e.dma_start(
                out=MRW[j][:, 1, :], in_=I[norm_of_chunk[chunk]][l:l + 1, hc:hc + MBW].broadcast_to([2, MBW])),
                writes=mk, dma=True)
        b = ps1()

        def mm(e):
            ins = None
            for kc in range(8):
                ins = e.matmul(bank(b)[0:2, 0:MBW], lhsT=SCT[:, kc, :], rhs=wt[:, kc, :],
                               start=(kc == 0), stop=(kc == 7))
            return ins
        P.add("pe", mm, reads=[wk, "sct"], writes=pk(b))
        P.add("dve", lambda e: e.tensor_tensor(out=MRW[j][:, 2, :], in0=bank(b)[0:2, 0:MBW], in1=MRW[j][:, 0, :],
                                               op=ALU.add), reads=pk(b) + mk, writes=mk)
        if chunk in (1, 4):
            P.add("dve", lambda e: e.scalar_tensor_tensor(
                out=MRW[j][:, 2, :], in0=MRW[j][:, 2, :], scalar=1.0, in1=MRW[j][:, 1, :],
                op0=ALU.add, op1=ALU.mult), reads=mk, writes=mk)
        elif chunk in (2, 5):
            P.add("dve", lambda e: e.tensor_tensor(out=MRW[j][:, 2, :], in0=MRW[j][:, 2, :], in1=MRW[j][:, 1, :],
                                                   op=ALU.mult), reads=mk, writes=mk)
        P.add("sp", lambda e: e.dma_start(out=MSCR[l, :, c0:c0 + MBW], in_=MRW[j][:, 2, :]),
              reads=mk, writes=[("mscr", l, cb)], dma=True)

    ROWS = alloc("rows", [2, 4096], F32, at=cur_ring0)
    GROW = alloc("grow", [2, 2048], F32, at=cur_ring0 + 16384)
    MWX = [alloc("mwx%d" % j, [128, 8, 256], BF16, at=cur_ring0 + 24576 + 4096 * j) for j in range(2)]

    def mod_group_fast(l, sub, after_first=None):
        rk = [RingKey(("ring", q)) for q in range(4)]
        for q in range(4):
            rk[q].gen = P.ring_gen[q] = P.ring_gen.get(q, 0) + 1
        st["ring"] = 4
        nblk = [0]
        r0 = sub * 3 * D
        P.add("sp", lambda e: e.dma_start(out=ROWS[:, 0:3 * D], in_=I["mod_b"][l:l + 1, r0:r0 + 3 * D]
                                          .broadcast_to([2, 3 * D])), writes=rk[0:2], dma=True)
        for gi, nm in enumerate(("norm_pre_mix", "norm_post_mix") if sub == 0 else ("norm_pre_ffn", "norm_post_ffn")):
            P.add("sp", lambda e, gi=gi, nm=nm: e.dma_start(
                out=GROW[:, gi * D:(gi + 1) * D], in_=I[nm][l:l + 1, :].broadcast_to([2, D])),
                writes=[rk[2]], dma=True)
        blocks = [(pl, pcb) for (pl, pcb) in mod_pending if pl == l and pcb // (NMB // 2) == sub]
        for blk in blocks:
            mod_pending.remove(blk)
            cb = blk[1]
            j = nblk[0] % 4
            nblk[0] += 1
            if j < 2:
                wk, wt = ("mw", j), MW[j]
            else:
                wk, wt = ("mwx", j), MWX[j - 2]
            c0 = cb * MBW
            lc = c0 - r0
            P.add("pool", lambda e, wt=wt, c0=c0: e.dma_start(out=wt[:], in_=wtile(I["mod_w"][l], c0, c0 + MBW)),
                  writes=[wk], dma=True)
            b = ps1()

            def mm(e, wt=wt, b=b):
                ins = None
                for kc in range(8):
                    ins = e.matmul(bank(b)[0:2, 0:MBW], lhsT=SCT[:, kc, :], rhs=wt[:, kc, :],
                                   start=(kc == 0), stop=(kc == 7))
                return ins
            P.add("pe", mm, reads=[wk, "sct"], writes=pk(b))
            P.add("dve", lambda e, b=b, lc=lc: e.tensor_tensor(
                out=ROWS[:, lc:lc + MBW], in0=bank(b)[0:2, 0:MBW], in1=ROWS[:, lc:lc + MBW], op=ALU.add),
                reads=pk(b) + rk[0:2], writes=rk[0:2])
            chunk = lc // D
            hc = lc % D
            if chunk == 1:
                P.add("dve", lambda e, lc=lc, hc=hc: e.scalar_tensor_tensor(
                    out=ROWS[:, lc:lc + MBW], in0=ROWS[:, lc:lc + MBW], scalar=1.0, in1=GROW[:, hc:hc + MBW],
                    op0=ALU.add, op1=ALU.mult), reads=rk, writes=rk[0:2])
            elif chunk == 2:
                P.add("dve", lambda e, lc=lc, hc=hc: e.tensor_tensor(
                    out=ROWS[:, lc:lc + MBW], in0=ROWS[:, lc:lc + MBW], in1=GROW[:, D + hc:D + hc + MBW],
                    op=ALU.mult), reads=rk, writes=rk[0:2])
            if len(blocks) == 12 and blk is blocks[7]:
                P.add("sp", lambda e: e.dma_start(out=MSCR[l, :, r0:r0 + 2 * D], in_=ROWS[:, 0:2 * D]),
                      reads=rk[0:2], writes=[("mscr", l, bb[1]) for bb in blocks[0:8]], dma=True)
                if after_first is not None:
                    after_first()
        if len(blocks) == 12:
            P.add("sp", lambda e: e.dma_start(out=MSCR[l, :, r0 + 2 * D:r0 + 3 * D], in_=ROWS[:, 2 * D:3 * D]),
                  reads=rk[0:2], writes=[("mscr", l, bb[1]) for bb in blocks[8:12]], dma=True)
        else:
            P.add("sp", lambda e: e.dma_start(out=MSCR[l, :, r0:r0 + 3 * D], in_=ROWS[:, 0:3 * D]),
                  reads=rk[0:2], writes=[("mscr", l, bb[1]) for bb in blocks], dma=True)
        P.add("dve", lambda e: e.memset(FENCE[:], 0.0), writes=[("mwx", 2), ("mwx", 3), rk[3]])

    def drain_mod(n):
        for _ in range(n):
            if mod_pending:
                mod_block(*mod_pending.pop(0))

    def ensure_mod(l, sub):
        while any(pl == l and (pcb // (NMB // 2)) == sub for (pl, pcb) in mod_pending):
            mod_block(*mod_pending.pop(0))

    wt, wk = ring_load(I["sgu_w"][0].rearrange("g p q -> p g q"), [128, 4, 128])
    b = ps1()

    def sgw_t(e, wt=wt, b=b):
        ins = None
        for g in range(4):
            ins = e.transpose(out=bankbf(b)[:, g * 128:(g + 1) * 128], in_=wt[:, g, :], identity=IDN[:])
        return ins
    P.add("pe", sgw_t, reads=[wk, "idn"], writes=pk(b))
    P.add("dve", lambda e, b=b: e.tensor_copy(out=SGWT[:].rearrange("p g q -> p (g q)"), in_=bankbf(b)[:, 0:512]),
          reads=pk(b), writes=["sgwt"])

    fence()

    def load_mod_slots01(l, half, sub):
        load_mod(l, half, sub, (0, 1), ensure=False)

    def load_mod(l, half, sub, slots=(0, 1, 2), ensure=True):
        if ensure:
            ensure_mod(l, sub)
        cond = 0 if half == 0 else 1
        base_v = 0 if sub == 0 else 3
        for slot, v in ((0, base_v + 1), (1, base_v + 0), (2, base_v + 2)):
            if slot not in slots:
                continue
            P.add("sp", lambda e, slot=slot, v=v, l=l, cond=cond: e.dma_start(
                out=MOD[slot][:], in_=MSCR[l, cond:cond + 1, v * D:(v + 1) * D].broadcast_to([128, D])),
                reads=[("mscr", l, 4 * v + q) for q in range(4)], writes=[("mod", slot)], dma=True)

    def rstd_cols(ss, rs, n, keys_r, keys_w):
        P.add("dve", lambda e: e.tensor_scalar(out=rs[:, 0:n], in0=ss[:, 0:n], scalar1=1.0 / D, scalar2=EPS,
                                               op0=ALU.mult, op1=ALU.add), reads=keys_r, writes=keys_w)
        P.add("act", lambda e: e.activation(out=rs[:, 0:n], in_=rs[:, 0:n], func=AF.Sqrt), reads=keys_w, writes=keys_w)
        P.add("dve", lambda e: e.reciprocal(out=rs[:, 0:n], in_=rs[:, 0:n]), reads=keys_w, writes=keys_w)

    def pre_norm_tile(xh, hi, kind, bank_t):
        sc = NSC[kind]
        xi = xh * 8 + hi
        q = hi % 4
        j = hi % 2
        sk, rk = ("ss", hi), ("rs", hi)
        P.add("act", lambda e: e.activation(out=JUNK[:], in_=X[:, xi, :], func=AF.Square,
                                            accum_out=SS[:, hi:hi + 1]), reads=[("X", xi)], writes=["junk", sk])
        P.add("act", lambda e: e.activation(out=RS[:, hi:hi + 1], in_=SS[:, hi:hi + 1], func=AF.Sqrt,
                                            bias=EPSB[:, 0:1], scale=1.0 / D), reads=[sk, "epsb"], writes=[rk])
        P.add("dve", lambda e: e.tensor_tensor(out=sc["tmp"][j][:], in0=X[:, xi, :], in1=MOD[0][:], op=ALU.mult),
              reads=[("X", xi), ("mod", 0)], writes=sc["tk"][j])
        P.add("dve", lambda e: e.reciprocal(out=RS[:, hi:hi + 1], in_=RS[:, hi:hi + 1]), reads=[rk], writes=[rk])
        P.add("dve", lambda e: e.scalar_tensor_tensor(
            out=sc["hb"][q][:], in0=sc["tmp"][j][:], scalar=RS[:, hi:hi + 1], in1=MOD[1][:],
            op0=ALU.mult, op1=ALU.add), reads=sc["tk"][j] + [rk, ("mod", 1)], writes=sc["hk"][q])

        def tail():
            b = bank_t

            def tr(e):
                ins = None
                for kc in range(8):
                    ins = e.transpose(out=bankbf(b)[:, kc * 128:(kc + 1) * 128],
                                      in_=sc["hb"][q][:, kc * 128:(kc + 1) * 128], identity=IDN[:])
                return ins
            P.add("pe", tr, reads=sc["hk"][q] + ["idn"], writes=pk(b))
            P.add("act", lambda e: e.activation(
                out=HT[:, :, hi * 128:(hi + 1) * 128], in_=bankbf(b).rearrange("p (kc t) -> p kc t", kc=8),
                func=AF.Copy), reads=pk(b), writes=[("HT", hi)])
        return tail

    def post_norm(half, i, b):
        t0 = half * 8
        j = i % 2
        P.add("act", lambda e: e.activation(out=JUNK[:], in_=bank(b, 2), func=AF.Square,
                                            accum_out=SS2[:, j:j + 1]), reads=pk(b, 2), writes=["junk", ("ss2", j)])
        P.add("act", lambda e: e.activation(out=RS2[:, j:j + 1], in_=SS2[:, j:j + 1], func=AF.Sqrt,
                                            bias=EPSB[:, 0:1], scale=1.0 / D),
              reads=[("ss2", j), "epsb"], writes=[("rs2", j)])
        P.add("dve", lambda e: e.tensor_tensor(out=bank(b, 2), in0=bank(b, 2), in1=MOD[2][:], op=ALU.mult),
              reads=pk(b, 2) + [("mod", 2)], writes=pk(b, 2))
        P.add("dve", lambda e: e.reciprocal(out=RS2[:, j:j + 1], in_=RS2[:, j:j + 1]),
              reads=[("rs2", j)], writes=[("rs2", j)])
        P.add("dve", lambda e: e.scalar_tensor_tensor(
            out=X[:, t0 + i, :], in0=bank(b, 2), scalar=RS2[:, j:j + 1], in1=X[:, t0 + i, :],
            op0=ALU.mult, op1=ALU.add), reads=pk(b, 2) + [("rs2", j), ("X", t0 + i)], writes=[("X", t0 + i)])

    def out_proj(half, w2d, nxt):
        w0, k0 = ring_load(wtile(w2d, 0, 512), [128, 8, 512])
        w1, k1 = ring_load(wtile(w2d, 512, 1024), [128, 8, 512])
        if nxt is not None:
            load_mod(nxt[0], nxt[1], nxt[2], (0, 1))
            fence()
        pend = []
        for i in range(8):
            b = 2 * (i % 3)

            def mm(e, i=i, b=b):
                ins = None
                for hcol, w in ((0, w0), (1, w1)):
                    for c in range(8):
                        ins = e.matmul(bank(b + hcol), lhsT=OT[:, c, i * 128:(i + 1) * 128], rhs=w[:, c, :],
                                       start=(c == 0), stop=(c == 7))
                return ins
            P.add("pe", mm, reads=[k0, k1] + allk, writes=pk(b, 2))
            if len(pend) > 1:
                pend.pop(0)()
            post_norm(half, i, b)
            if nxt is not None and i >= 4:
                pend.append(pre_norm_tile(nxt[1], i - 4, "HID", 6 + (i % 2)))
        if nxt is not None:
            for i in range(4, 8):
                pend.append(pre_norm_tile(nxt[1], i, "HID", 6 + (i % 2)))
                pend.pop(0)()
        while pend:
            pend.pop(0)()

    def ffn(l, half, nxt):
        wg2, wu2, wd2 = I["ffn_w_gate"][l], I["ffn_w_up"][l], I["ffn_w_down"][l]
        groups = [(g * 4, min(4, NFF - g * 4)) for g in range((NFF + 3) // 4)]
        dgroups = [(0, 2)] + [(2 + g * 4, 4) for g in range(5)]
        for (f0, nf) in groups:
            wg, kg = ring_load(wtile(wg2, f0 * 128, (f0 + nf) * 128), [128, 8, nf * 128])
            wu, ku = ring_load(wtile(wu2, f0 * 128, (f0 + nf) * 128), [128, 8, nf * 128])
            for tt in range(2):
                drain_mod(2)
                for f in range(nf):
                    bg = ps1()
                    bu = ps1()

                    def mm(e, w=wg, f=f, tt=tt, b=bg):
                        ins = None
                        for kc in range(8):
                            ins = e.matmul(bank(b), lhsT=w[:, kc, f * 128:(f + 1) * 128],
                                           rhs=HT[:, kc, tt * 512:(tt + 1) * 512], start=(kc == 0), stop=(kc == 7))
                        return ins
                    htk = [("HT", tt * 4 + q) for q in range(4)]
                    P.add("pe", mm, reads=[kg] + htk, writes=pk(bg))
                    P.add("pe", lambda e, mm=mm, w=wu, f=f, tt=tt, b=bu: mm(e, w, f, tt, b), reads=[ku] + htk, writes=pk(bu))
                    j = st["k"] % 2
                    st["k"] += 1
                    sgk = [("OT", 6 + j)]
                    P.add("act", lambda e, j=j, b=bg: e.activation(out=SG[j][:], in_=bank(b), func=AF.Silu),
                          reads=pk(bg), writes=sgk)
                    P.add("dve", lambda e, j=j, b=bu, ff=f0 + f, tt=tt: e.tensor_tensor(
                        out=HID[:, ff, tt * 512:(tt + 1) * 512], in0=SG[j][:], in1=bank(b), op=ALU.mult),
                        reads=pk(bu) + sgk, writes=[("HID", f0 + f)])
        if nxt is not None:
            load_mod(nxt[0], nxt[1], nxt[2], (0, 1))
        tails = []
        for ps in range(2):
            for gi, (f0, nf) in enumerate(dgroups):
                wd, kd = ring_load(wd2[f0 * 128:(f0 + nf) * 128, :].rearrange("(c p) n -> p c n", p=128),
                                   [128, nf, 1024])
                for ii in range(4):
                    i = ps * 4 + ii

                    def mm(e, wd=wd, f0=f0, nf=nf, i=i, ii=ii, first=(gi == 0), last=(gi == len(dgroups) - 1)):
                        ins = None
                        for c in range(nf):
                            for hcol in range(2):
                                ins = e.matmul(bank(2 * ii + hcol), lhsT=HID[:, f0 + c, i * 128:(i + 1) * 128],
                                               rhs=wd[:, c, hcol * 512:(hcol + 1) * 512],
                                               start=(first and c == 0), stop=(last and c == nf - 1))
                        return ins
                    P.add("pe", mm, reads=[kd] + [("HID", f0 + c) for c in range(nf)], writes=pk(2 * ii, 2))
            for ii in range(4):
                post_norm(half, ps * 4 + ii, 2 * ii)
            while tails:
                tails.pop(0)()
            st["b1"] = 0
            st["b2"] = 0
            if nxt is not None:
                for q in range(4):
                    tails.append(pre_norm_tile(nxt[1], ps * 4 + q, "OT", 2 * q))
        while tails:
            tails.pop(0)()

    def mixer_ab(half, nxt):
        w_in = I["ab_w_in"][0]
        wv, kv = ring_load(wtile(w_in, 512, 1024), [128, 8, 512])
        for i in range(8):
            b = ps1()

            def mm(e, i=i, b=b, w=wv):
                ins = None
                for kc in range(8):
                    ins = e.matmul(bank(b), lhsT=HT[:, kc, i * 128:(i + 1) * 128], rhs=w[:, kc, :],
                                   start=(kc == 0), stop=(kc == 7))
                return ins
            P.add("pe", mm, reads=[kv, ("HT", i)], writes=pk(b))
            if i % 2:
                drain_mod(1)
            P.add("act", lambda e, i=i, b=b: e.activation(out=GV[:, i, :], in_=bank(b), func=AF.Gelu_apprx_tanh,
                                                          accum_out=LNS[:, i, 0:1]),
                  reads=pk(b), writes=[("gv", i), ("lns", i)])
            P.add("dve", lambda e, i=i: e.tensor_scalar(out=LNS[:, i, 2:3], in0=LNS[:, i, 0:1], scalar1=-1.0 / 512,
                                                        scalar2=None, op0=ALU.mult),
                  reads=[("lns", i)], writes=[("lns", i)])
            P.add("act", lambda e, i=i: e.activation(out=JUNK[:, 0:512], in_=GV[:, i, :], func=AF.Square,
                                                     bias=LNS[:, i, 2:3], accum_out=LNS[:, i, 1:2]),
                  reads=[("gv", i), ("lns", i)], writes=["junk", ("lns", i)])
        lk = [("lns", i) for i in range(8)]
        P.add("dve", lambda e: e.tensor_scalar(out=LNS[:, :, 3], in0=LNS[:, :, 1], scalar1=1.0 / 512, scalar2=EPS,
                                               op0=ALU.mult, op1=ALU.add), reads=lk, writes=lk)
        P.add("act", lambda e: e.activation(out=LNS[:, :, 3], in_=LNS[:, :, 3], func=AF.Sqrt), reads=lk, writes=lk)
        P.add("dve", lambda e: e.reciprocal(out=LNS[:, :, 3], in_=LNS[:, :, 3]), reads=lk, writes=lk)
        for i in range(8):
            P.add("dve", lambda e, i=i: e.tensor_scalar(out=VLN[:, i, :], in0=GV[:, i, :], scalar1=LNS[:, i, 2:3],
                                                        scalar2=LNS[:, i, 3:4], op0=ALU.add, op1=ALU.mult),
                  reads=[("gv", i), ("lns", i)], writes=[("vln", i)])
        wb, kb = ring_load(wtile(w_in, 1024, 1536), [128, 8, 512])
        for i in range(8):
            b = ps1()
            P.add("pe", lambda e, mm=mm, i=i, b=b, w=wb: mm(e, i, b, w), reads=[kb, ("HT", i)], writes=pk(b))
            if i % 2:
                drain_mod(1)
            P.add("act", lambda e, i=i, b=b: e.activation(out=ZB[:, i, :], in_=bank(b), func=AF.Copy),
                  reads=pk(b), writes=[("zb", i)])
        wu, ku = ring_load(wtile(w_in, 0, 512), [128, 8, 512])
        for g in range(4):
            for tt in range(2):
                b = ps1()
                j = st["k"] % 2
                st["k"] += 1

                def mmu(e, g=g, tt=tt, b=b):
                    ins = None
                    for kc in range(8):
                        ins = e.matmul(bank(b), lhsT=wu[:, kc, g * 128:(g + 1) * 128],
                                       rhs=HT[:, kc, tt * 512:(tt + 1) * 512], start=(kc == 0), stop=(kc == 7))
                    return ins
                htk = [("HT", tt * 4 + q) for q in range(4)]
                P.add("pe", mmu, reads=[ku] + htk, writes=pk(b))
                if tt == 0:
                    drain_mod(1)
                P.add("act", lambda e, j=j, b=b: e.activation(out=UG[j][:], in_=bank(b), func=AF.Gelu_apprx_tanh),
                      reads=pk(b), writes=[("ug", j)])
                b2 = ps1()

                def mms(e, g=g, tt=tt, b2=b2):
                    ins = None
                    for n in range(4):
                        ins = e.matmul(bank(b2)[:, n * 128:(n + 1) * 128],
                                       lhsT=VLN[:, tt * 4 + n, g * 128:(g + 1) * 128], rhs=SGWT[:, g, :],
                                       start=True, stop=True)
                    return ins
                P.add("pe", mms, reads=["sgwt"] + [("vln", tt * 4 + n) for n in range(4)], writes=pk(b2))
                P.add("dve", lambda e, g=g, j=j, b2=b2: e.scalar_tensor_tensor(
                    out=SGT[j][:].rearrange("p (n q) -> p n q", n=4),
                    in0=bank(b2).rearrange("p (n q) -> p n q", n=4), scalar=SGG[:, g:g + 1],
                    in1=BIASB[:, g:g + 1, :].broadcast_to([128, 4, 128]), op0=ALU.mult, op1=ALU.add),
                    reads=pk(b2) + ["sgg", "biasb"], writes=[("sgt", j)])
                P.add("dve", lambda e, g=g, tt=tt, j=j: e.tensor_tensor(
                    out=OT[:, g, tt * 512:(tt + 1) * 512], in0=SGT[j][:], in1=UG[j][:], op=ALU.mult),
                    reads=[("sgt", j), ("ug", j)], writes=[("OT", g)])
        fpend = None
        if half == 0:
            for pt in range(2):
                ct, kc_ = ring_load(I["c1024"].rearrange("(qc p) n -> p qc n", p=128)[:, :, pt * 512:(pt + 1) * 512],
                                    [128, 8, 512])
                stt, ks_ = ring_load(I["s1024"].rearrange("(qc p) n -> p qc n", p=128)[:, :, pt * 512:(pt + 1) * 512],
                                     [128, 8, 512])
                for g in range(4):
                    s2 = fourier_group(g, pt * 512, [(0, 512, ct, stt, list(range(8)))], [kc_, ks_])
                    if fpend is not None:
                        fpend()
                    fpend = s2
        else:
            ct, kc_ = ring_load(I["c256"].rearrange("(qc p) n -> p qc n", p=128), [128, 2, 256])
            stt, ks_ = ring_load(I["s256"].rearrange("(qc p) n -> p qc n", p=128), [128, 2, 256])
            for pr in range(2):
                for g in range(4):
                    parts = [(jb * 256, 256, ct, stt, [2 * (2 * pr + jb), 2 * (2 * pr + jb) + 1]) for jb in range(2)]
                    s2 = fourier_group(g, pr * 512, parts, [kc_, ks_])
                    if fpend is not None:
                        fpend()
                    fpend = s2
        fpend()
        out_proj(half, I["ab_w_out"][0], nxt)

    def fourier_group(g, col0, parts, wkeys):
        bc = ps1()
        bs = ps1()
        zk = []

        def mmf(e, which, b):
            ins = None
            first = True
            for (co, w, ct, stt, tiles) in parts:
                m = ct if which == 0 else stt
                for qi, ti in enumerate(tiles):
                    ins = e.matmul(bank(b)[:, co:co + w], lhsT=ZB[:, ti, g * 128:(g + 1) * 128], rhs=m[:, qi, 0:w],
                                   start=(qi == 0 and first), stop=(qi == len(tiles) - 1),
                                   skip_group_check=True)
                first = False
            return ins
        for (_, _, _, _, tiles) in parts:
            zk += [("zb", ti) for ti in tiles]
        P.add("pe", lambda e: mmf(e, 0, bc), reads=wkeys + zk, writes=pk(bc))
        P.add("pe", lambda e: mmf(e, 1, bs), reads=wkeys + zk, writes=pk(bs))
        j = st["k"] % 2
        st["k"] += 1
        P.add("act", lambda e: e.activation(out=CZ[j][:, 0, :], in_=bank(bc), func=AF.Copy),
              reads=pk(bc), writes=[("cz", j, 0)])
        P.add("dve", lambda e: e.tensor_copy(out=CZ[j][:, 1, :], in_=bank(bs)), reads=pk(bs), writes=[("cz", j, 1)])
        def stage2():
            bo = ps1()

            def mmc(e):
                e.matmul(bank(bo), lhsT=CCS[:], rhs=CZ[j][:, 0, :], start=True, stop=False)
                return e.matmul(bank(bo), lhsT=NSCS[:], rhs=CZ[j][:, 1, :], start=False, stop=True)
            P.add("pe", mmc, reads=["ccs", "nscs", ("cz", j, 0), ("cz", j, 1)], writes=pk(bo))
            P.add("act", lambda e: e.activation(out=OT[:, 4 + g, col0:col0 + 512], in_=bank(bo), func=AF.Copy),
                  reads=pk(bo), writes=[("OT", 4 + g)])
        return stage2

    def mixer_attn(half, nxt):
        wqkv = I["attn_w_qkv"][0]
        sample = (half == 0)
        scale = 1.0 / np.sqrt(128.0)
        wkv, kkv = ring_load(wtile(wqkv, 1024, 1536), [128, 8, 512])
        if sample:
            P.add("sp", lambda e: e.dma_start(out=ROPE[:, 0, :], in_=I["cosT"]), writes=["rope"], dma=True)
            P.add("sp", lambda e: e.dma_start(out=ROPE[:, 1, :], in_=I["ssinT"]), writes=["rope"], dma=True)
            P.add("pool", lambda e: e.dma_start(out=CKB[:], in_=I["ck"].rearrange("(t p) c -> p t c", p=128)),
                  writes=["ckb"], dma=True)
            P.add("pool", lambda e: e.dma_start(out=VT[:, 8:10, :], in_=I["cv"].rearrange("(t p) c -> p t c", p=128)),
                  writes=[("vt", 8), ("vt", 9)], dma=True)
            b = ps1()

            def trk(e, b=b):
                ins = None
                for gg in range(2):
                    for t in range(2):
                        ins = e.transpose(out=bankbf(b)[:, (gg * 2 + t) * 128:(gg * 2 + t + 1) * 128],
                                          in_=CKB[:, t, gg * 128:(gg + 1) * 128], identity=IDN[:])
                return ins
            P.add("pe", trk, reads=["ckb", "idn"], writes=pk(b))
            P.add("act", lambda e, b=b: e.activation(
                out=KT[:, :, 1024:1280], in_=bankbf(b)[:, 0:512].rearrange("p (g t) -> p g t", g=2), func=AF.Copy),
                reads=pk(b), writes=[("ktc", 0), ("ktc", 1)])
        for i in range(8):
            b = ps1()

            def mm(e, i=i, b=b):
                ins = None
                for kc in range(8):
                    ins = e.matmul(bank(b), lhsT=HT[:, kc, i * 128:(i + 1) * 128], rhs=wkv[:, kc, :],
                                   start=(kc == 0), stop=(kc == 7))
                return ins
            P.add("pe", mm, reads=[kkv, ("HT", i)], writes=pk(b))
            P.add("act", lambda e, i=i, b=b: e.activation(out=VT[:, i, :], in_=bank(b)[:, 256:512], func=AF.Copy),
                  reads=pk(b), writes=[("vt", i)])
            drain_mod(1)
            if not sample:
                j = i % 2
                P.add("dve", lambda e, j=j, b=b: e.tensor_copy(out=KVST[j][:], in_=bank(b)),
                      reads=pk(b), writes=[("kvst", j)])
                P.add("sp", lambda e, j=j, i=i: e.dma_start(out=O["sk"][i * 128:(i + 1) * 128, :], in_=KVST[j][:, 0:256]),
                      reads=[("kvst", j)], writes=[("o_sk", i)], dma=True)
                P.add("sp", lambda e, j=j, i=i: e.dma_start(out=O["sv"][i * 128:(i + 1) * 128, :], in_=KVST[j][:, 256:512]),
                      reads=[("kvst", j)], writes=[("o_sv", i)], dma=True)
        rope_pend = []

        def proj_fm(dst, dkeys, w, wkey, rope, c0, tt):
            htk = [("HT", tt * 4 + q) for q in range(4)]
            b = ps1()

            def mmq(e):
                ins = None
                for kc in range(8):
                    ins = e.matmul(bank(b), lhsT=w[:, kc, c0:c0 + 128], rhs=HT[:, kc, tt * 512:(tt + 1) * 512],
                                   start=(kc == 0), stop=(kc == 7))
                return ins
            P.add("pe", mmq, reads=[wkey] + htk, writes=pk(b))
            if not rope:
                P.add("act", lambda e: e.activation(out=dst, in_=bank(b), func=AF.Copy), reads=pk(b), writes=dkeys)
                return
            jq = st["k"] % 2
            st["k"] += 1
            qk_ = [("expb", jq)]
            P.add("act", lambda e: e.activation(out=EXPB[jq][:], in_=bank(b), func=AF.Copy), reads=pk(b), writes=qk_)
            if rope_pend:
                rope_pend.pop(0)()

            def fin_rope():
                bp = ps1()
                P.add("pe", lambda e: e.matmul(bank(bp), lhsT=RPM[:], rhs=EXPB[jq][:], start=True, stop=True),
                      reads=qk_ + ["rpm"], writes=pk(bp))
                P.add("dve", lambda e: e.tensor_tensor(out=RT[0][:], in0=bank(b), in1=ROPE[:, 0, tt * 512:(tt + 1) * 512],
                                                       op=ALU.mult), reads=pk(b) + ["rope"], writes=[("rt", 0)])
                P.add("dve", lambda e: e.tensor_tensor(out=RT[1][:], in0=bank(bp),
                                                       in1=ROPE[:, 1, tt * 512:(tt + 1) * 512], op=ALU.mult),
                      reads=pk(bp) + ["rope"], writes=[("rt", 1)])
                P.add("dve", lambda e: e.tensor_tensor(out=dst, in0=RT[0][:], in1=RT[1][:], op=ALU.add),
                      reads=[("rt", 0), ("rt", 1)], writes=dkeys)
            rope_pend.append(fin_rope)

        def rope_flush():
            while rope_pend:
                rope_pend.pop(0)()

        for gg in range(2):
            for tt in range(2):
                proj_fm(KT[:, gg, tt * 512:(tt + 1) * 512], [("kt", gg, tt)], wkv, kkv, sample, gg * 128, tt)
                drain_mod(1)
        for gg in range(2):
            wq, kq = ring_load(wtile(wqkv, gg * 512, (gg + 1) * 512), [128, 8, 512])
            for hh in range(4):
                for tt in range(2):
                    proj_fm(QT[:, hh, tt * 512:(tt + 1) * 512], [("qt", hh, tt)], wq, kq, sample, hh * 128, tt)
            rope_flush()
            fin = None
            for hh in range(4):
                h = gg * 4 + hh
                for qt in range(2):
                    fin = attend(sample, gg, hh, h, qt, scale, fin)
            fin()
        out_proj(half, I["attn_w_o"][0], nxt)

    def attend(sample, gg, hh, h, qt, scale, prev_fin):
        a = st.get("att", 0)
        st["att"] = a + 1
        bo = 2 * (a % 2)
        bd = bo + 1

        def ps1():
            b_ = 4 + st.get("bs", 0) % 4
            st["bs"] = st.get("bs", 0) + 1
            return b_
        items = []
        if sample:
            for jc in range(2):
                items.append((KT[:, gg, 1024 + jc * 128:1024 + (jc + 1) * 128], VT[:, 8 + jc, gg * 128:(gg + 1) * 128],
                              0, 512, [("ktc", gg), ("vt", 8 + jc)], []))
            for j in range(max(0, 4 * qt - 1), min(8, 4 * qt + 5)):
                ilo = max(j - 1, 4 * qt)
                ihi = min(j + 1, 4 * qt + 3)
                lo = (ilo - 4 * qt) * 128
                hi = (ihi - 4 * qt + 1) * 128
                ms = []
                if ilo == j - 1:
                    ms.append((0, 0))
                if ihi == j + 1:
                    ms.append((hi - lo - 128, 1))
                items.append((KT[:, gg, j * 128:(j + 1) * 128], VT[:, j, gg * 128:(gg + 1) * 128], lo, hi,
                              [("kt", gg, j // 4), ("vt", j)], ms))
        else:
            for jb in range(2):
                bt = 2 * qt + jb
                for kt_ in range(2):
                    ti = 2 * bt + kt_
                    items.append((KT[:, gg, ti * 128:(ti + 1) * 128], VT[:, ti, gg * 128:(gg + 1) * 128],
                                  jb * 256, jb * 256 + 256, [("kt", gg, ti // 4), ("vt", ti)], []))
        first = True
        pend = []
        for (kap, vap, lo, hi, keys, ms) in items:
            n = hi - lo
            bs = ps1()
            P.add("pe", lambda e, kap=kap, lo=lo, hi=hi, bs=bs: e.matmul(
                bank(bs)[:, 0:hi - lo], lhsT=kap, rhs=QT[:, hh, qt * 512 + lo:qt * 512 + hi], start=True, stop=True),
                reads=[keys[0], ("qt", hh, qt)], writes=pk(bs))
            j = st["k"] % 3
            st["k"] += 1
            ek = [("expb", j)]
            P.add("act", lambda e, j=j, bs=bs, n=n: e.activation(out=EXPB[j][:, 0:n], in_=bank(bs)[:, 0:n],
                                                                 func=AF.Exp, scale=float(scale)),
                  reads=pk(bs), writes=ek)
            for (co, which) in ms:
                P.add("dve", lambda e, j=j, co=co, which=which: e.tensor_tensor(
                    out=EXPB[j][:, co:co + 128], in0=EXPB[j][:, co:co + 128],
                    in1=MASKS[:, which * 128:(which + 1) * 128], op=ALU.mult), reads=ek + ["masks"], writes=ek)

            def pv(e, vap=vap, j=j, lo=lo, hi=hi, first=first):
                e.matmul(bank(bo)[:, lo:hi], lhsT=vap, rhs=EXPB[j][:, 0:hi - lo], start=first, stop=False,
                         skip_group_check=True)
                return e.matmul(bank(bd)[:, lo:hi], lhsT=ONES[:], rhs=EXPB[j][:, 0:hi - lo], start=first, stop=False,
                                skip_group_check=True)
            pend.append((pv, ek + [keys[1], "ones"]))
            if len(pend) > 2:
                pv_, rd_ = pend.pop(0)
                P.add("pe", pv_, reads=rd_, writes=pk(bo) + pk(bd))
                if prev_fin is not None:
                    prev_fin()
                    prev_fin = None
            first = False
        while pend:
            pv_, rd_ = pend.pop(0)
            P.add("pe", pv_, reads=rd_, writes=pk(bo) + pk(bd))
        if prev_fin is not None:
            prev_fin()
        j = st["k"] % 2
        st["k"] += 1
        rk = [("rcp", j)]

        def fin():
            P.add("act", lambda e: e.activation(out=RCP[j][:], in_=bank(bd), func=AF.Ln, bias=SINKE[:, h:h + 1]),
                  reads=pk(bd) + ["sinke"], writes=rk)
            P.add("act", lambda e: e.activation(out=RCP[j][:], in_=RCP[j][:], func=AF.Exp, scale=-1.0),
                  reads=rk, writes=rk)
            P.add("dve", lambda e: e.tensor_tensor(out=OT[:, h, qt * 512:(qt + 1) * 512], in0=bank(bo), in1=RCP[j][:],
                                                   op=ALU.mult), reads=pk(bo) + rk, writes=[("OT", h)])
        return fin

    if phases is None:
        phases = [(l, half, sub) for half in range(2) for l in range(2) for sub in range(2)]
    for idx, (l, half, sub) in enumerate(phases):
        nxt = phases[idx + 1] if idx + 1 < len(phases) else None
        if idx == 0:
            load_x(range(half * 8, half * 8 + 8))
            mod_group_fast(l, sub, after_first=lambda: load_mod_slots01(l, half, sub))
            load_small_consts()
            load_x(range((1 - half) * 8, (1 - half) * 8 + 8))
            pend = []
            for i in range(8):
                pend.append(pre_norm_tile(half, i, "OT", i % 8))
                if len(pend) > 2:
                    pend.pop(0)()
            while pend:
                pend.pop(0)()
        load_mod(l, half, sub, (2,))
        fence()
        if sub == 0:
            if l == 0:
                mixer_ab(half, nxt)
            else:
                mixer_attn(half, nxt)
        else:
            ffn(l, half, nxt)
        st["b1"] = 0
        st["b2"] = 0
        if not any(ph[1] == half for ph in phases[idx + 1:]):
            store_x(half)
    for hf in range(2):
        if not any(ph[1] == hf for ph in phases):
            store_x(hf)
    outk = [("o_y", i) for i in range(16)]
    outk += [("o_sk", i) for i in range(8)] + [("o_sv", i) for i in range(8)]
    P.add("sp", lambda e: e.nop(), reads=outk)

    P.analyze()
    with contextlib.ExitStack() as stack:
        esems = {k: stack.enter_context(nc.semaphore("es_" + k)) for k in Prog.ENGS}
        dsems = [stack.enter_context(nc.semaphore("ds_%d" % i)) for i in range(P.ndsems)]
        block = stack.enter_context(nc.Block())
        P.emit(block, esems, dsems)
    return nc


_CONSTS = None


def make_in_maps(inputs):
    global _CONSTS
    if _CONSTS is None:
        _CONSTS = _consts()
    f = lambda a: np.ascontiguousarray(np.asarray(a, dtype=np.float32))
    shared = {k: f(inputs[k]) for k in ("mod_w", "mod_b", "norm_pre_mix", "norm_post_mix", "norm_pre_ffn",
                                        "norm_post_ffn", "ab_w_in", "sgu_w", "sgu_b", "sgu_g", "ab_w_out",
                                        "attn_w_qkv", "attn_sink", "attn_w_o", "ffn_w_gate", "ffn_w_up",
                                        "ffn_w_down")}
    shared.update(_CONSTS)
    xs, xp = f(inputs["x_sample"]), f(inputs["x_prompt"])
    ck, cv = f(inputs["cache_k"]), f(inputs["cache_v"])
    c, cctx = f(inputs["c"]), f(inputs["c_ctx"])
    maps = []
    for b in range(NCORES):
        m = dict(shared)
        m["xs"] = xs[b]
        m["xp"] = np.ascontiguousarray(xp[4 * b:4 * b + 4].reshape(1024, D))
        m["ck"] = np.ascontiguousarray(ck[b, 0].reshape(256, 256))
        m["cv"] = np.ascontiguousarray(cv[b, 0].reshape(256, 256))
        m["cond"] = np.ascontiguousarray(np.stack([c[b], cctx], axis=0))
        maps.append(m)
    return maps


def gather(results):
    ys = np.stack([r["ys"] for r in results], axis=0).astype(np.float32)
    yp = np.concatenate([r["yp"].reshape(4, 256, D) for r in results], axis=0).astype(np.float32)
    sk = np.concatenate([r["sk"].reshape(4, 1, 256, 2, 128) for r in results], axis=0).astype(np.float32)
    sv = np.concatenate([r["sv"].reshape(4, 1, 256, 2, 128) for r in results], axis=0).astype(np.float32)
    return (yp, ys, sk, sv)


def kernel(**inputs):
    nc = build()
    res = run_bass_kernel_spmd(nc, make_in_maps(inputs), core_ids=list(range(NCORES)))
    return gather(res.results)
```
